# Optimizing a Trainium2 kernel written in Bass

```python
import math
import jax, jax.numpy as jnp
from jax import lax
import numpy as np

D_MODEL = 1024
BATCH = 8
SEQ = 4096
DEPTH = 4

HEAD_DIM = 64
ATTN_WIDTH = 3 * D_MODEL // 4
ATTN_HEADS = ATTN_WIDTH // HEAD_DIM
DILATED_GROUPS = ((128, 1), (512, 4), (2048, 16))
HEADS_PER_GROUP = ATTN_HEADS // len(DILATED_GROUPS)
ATTN_BLOCK = 128
ROT_DIM = HEAD_DIM // 4
ROPE_THETA = 500000.0
CONV_WIDTH = D_MODEL // 2
CONV_K = 3
SG_WIDTH = D_MODEL // 2
SG_CHUNK = 128
SG_GROUPS = 4
SG_GROUP_CH = SG_WIDTH // SG_GROUPS
N_BRANCH = 3
D_FF = 4 * D_MODEL
PLE_DIM = 256
RMS_EPS = 1e-6
LN_EPS = 1e-5
IN_WIDTH = 3 * ATTN_WIDTH + 3 * CONV_WIDTH + 2 * SG_WIDTH + N_BRANCH * D_MODEL

kernel_name = 'hybrid_gated_dilated_conv_sgu_trunk'


def rms_norm(x, g):
    xf = x.astype(jnp.float32)
    y = xf * lax.rsqrt(jnp.mean(xf * xf, axis=-1, keepdims=True) + RMS_EPS)
    return (y * g.astype(jnp.float32)).astype(x.dtype)


def layer_norm(x, g, b):
    xf = x.astype(jnp.float32)
    mu = jnp.mean(xf, axis=-1, keepdims=True)
    xc = xf - mu
    y = xc * lax.rsqrt(jnp.mean(xc * xc, axis=-1, keepdims=True) + LN_EPS)
    return (y * g.astype(jnp.float32) + b.astype(jnp.float32)).astype(x.dtype)


def rotary_partial(t, cos, sin):
    half = ROT_DIM // 2
    x1 = t[..., :half]
    x2 = t[..., half:ROT_DIM]
    rot = jnp.concatenate([x1 * cos - x2 * sin, x2 * cos + x1 * sin], axis=-1)
    return jnp.concatenate([rot, t[..., ROT_DIM:]], axis=-1)


def dilated_causal_attention(q, k, v, window, dilation):
    bsz, S, H, E = q.shape
    span = window // dilation
    assert span <= ATTN_BLOCK
    unit = dilation * ATTN_BLOCK
    L = -(-S // unit) * unit
    M = L // dilation
    NB = M // ATTN_BLOCK

    def to_blocks(t):
        t = jnp.pad(t, ((0, 0), (0, L - S), (0, 0), (0, 0)))
        t = t.reshape(bsz, M, dilation, H, E).transpose(0, 2, 3, 1, 4)
        return t.reshape(bsz, dilation, H, NB, ATTN_BLOCK, E)

    def with_prev(t):
        prev = jnp.pad(t, ((0, 0), (0, 0), (0, 0), (1, 0), (0, 0), (0, 0)))[:, :, :, :-1]
        return jnp.concatenate([prev, t], axis=-2)

    qb = to_blocks(q)
    kw = with_prev(to_blocks(k))
    vw = with_prev(to_blocks(v))
    s = jnp.einsum('brhnqe,brhnke->brhnqk', qb, kw).astype(jnp.float32) * (HEAD_DIM ** -0.5)
    qi = jnp.arange(ATTN_BLOCK)[:, None]
    kj = jnp.arange(2 * ATTN_BLOCK)[None, :]
    dist = qi + ATTN_BLOCK - kj
    band = (dist >= 0) & (dist <= span)
    blk = jnp.arange(NB)[:, None, None]
    mask = band[None] & (blk * ATTN_BLOCK + kj[None] - ATTN_BLOCK >= 0)
    s = jnp.where(mask, s, -jnp.inf)
    lse = jax.nn.logsumexp(s, axis=-1)
    probs = jnp.exp(s - lse[..., None])
    o = jnp.einsum('brhnqk,brhnke->brhnqe', probs.astype(v.dtype), vw)
    o = o.reshape(bsz, dilation, H, M, E).transpose(0, 3, 1, 2, 4).reshape(bsz, L, H, E)[:, :S]
    lse = lse.reshape(bsz, dilation, H, M).transpose(0, 3, 1, 2).reshape(bsz, L, H)[:, :S]
    return o, lse


def dilated_mixture(q, k, v):
    outs, lses = [], []
    for g, (window, dilation) in enumerate(DILATED_GROUPS):
        sl = slice(g * HEADS_PER_GROUP, (g + 1) * HEADS_PER_GROUP)
        o, l = dilated_causal_attention(q[:, :, sl], k[:, :, sl], v[:, :, sl], window, dilation)
        outs.append(o)
        lses.append(l)
    alpha = jax.nn.softmax(jnp.stack(lses, axis=0), axis=0)
    ya = jnp.concatenate([o * alpha[g][..., None].astype(o.dtype) for g, o in enumerate(outs)], axis=2)
    bsz, S = q.shape[0], q.shape[1]
    return ya.reshape(bsz, S, ATTN_WIDTH)


def short_conv(z, w):
    C = z.shape[-1]
    return lax.conv_general_dilated(z, w[:, None, :].astype(z.dtype), window_strides=(1,),
                                    padding=[(CONV_K - 1, 0)],
                                    dimension_numbers=('NWC', 'WIO', 'NWC'),
                                    feature_group_count=C)


def spatial_gating(zs, ln_g, ln_b, w_s, b_s):
    bsz, S, _ = zs.shape
    zs = jax.nn.gelu(zs, approximate=False)
    u, v = zs[..., :SG_WIDTH], zs[..., SG_WIDTH:]
    v = layer_norm(v, ln_g, ln_b)
    v = v.reshape(bsz, S // SG_CHUNK, SG_CHUNK, SG_GROUPS, SG_GROUP_CH)
    causal = jnp.tril(jnp.ones((SG_CHUNK, SG_CHUNK), dtype=bool))
    w = jnp.where(causal[None], w_s, jnp.zeros_like(w_s))
    sv = jnp.einsum('gts,bnsgc->bntgc', w, v) + b_s.T[None, None, :, :, None]
    return u * sv.reshape(bsz, S, SG_WIDTH)


def setup_inputs(seed: int = 0) -> dict:
    key = jax.random.key(seed)
    ks = jax.random.split(key, 24)

    def nrm(k, shape, scale):
        return jax.random.normal(k, shape, dtype=jnp.float32) * scale

    def gain(k, shape):
        return 1.0 + 0.05 * jax.random.normal(k, shape, dtype=jnp.float32)

    x = nrm(ks[0], (BATCH, SEQ, D_MODEL), 1.0)
    p = nrm(ks[1], (DEPTH, BATCH, SEQ, PLE_DIM), 1.0)
    offsets = jax.random.randint(ks[2], (BATCH, 1), 0, 1024, dtype=jnp.int32)
    positions = offsets + jnp.arange(SEQ, dtype=jnp.int32)[None, :]
    return {
        'x': x,
        'p': p,
        'positions': positions,
        'norm_mix_g': gain(ks[3], (DEPTH, D_MODEL)),
        'w_in': nrm(ks[4], (DEPTH, D_MODEL, IN_WIDTH), D_MODEL ** -0.5),
        'conv_w': nrm(ks[5], (DEPTH, CONV_K, CONV_WIDTH), CONV_K ** -0.5),
        'sg_ln_g': gain(ks[6], (DEPTH, SG_WIDTH)),
        'sg_ln_b': nrm(ks[7], (DEPTH, SG_WIDTH), 0.02),
        'sg_w': nrm(ks[8], (DEPTH, SG_GROUPS, SG_CHUNK, SG_CHUNK), SG_CHUNK ** -0.5),
        'sg_b': gain(ks[9], (DEPTH, SG_GROUPS, SG_CHUNK)),
        'w_branch_a': nrm(ks[10], (DEPTH, ATTN_WIDTH, D_MODEL), ATTN_WIDTH ** -0.5),
        'w_branch_b': nrm(ks[11], (DEPTH, CONV_WIDTH, D_MODEL), CONV_WIDTH ** -0.5),
        'w_branch_c': nrm(ks[12], (DEPTH, SG_WIDTH, D_MODEL), SG_WIDTH ** -0.5),
        'w_out': nrm(ks[13], (DEPTH, D_MODEL, D_MODEL), 0.5 * D_MODEL ** -0.5),
        'norm_mlp_g': gain(ks[14], (DEPTH, D_MODEL)),
        'w_up': nrm(ks[15], (DEPTH, D_MODEL, D_FF), D_MODEL ** -0.5),
        'w_down': nrm(ks[16], (DEPTH, D_FF, D_MODEL), 0.5 * D_FF ** -0.5),
        'norm_ple_g': gain(ks[17], (DEPTH, D_MODEL)),
        'w_ple_gate': nrm(ks[18], (DEPTH, D_MODEL, D_MODEL), D_MODEL ** -0.5),
        'w_ple_proj': nrm(ks[19], (DEPTH, PLE_DIM, D_MODEL), 0.5 * PLE_DIM ** -0.5),
        'norm_final_g': gain(ks[20], (D_MODEL,)),
    }


def reference(x, p, positions, norm_mix_g, w_in, conv_w, sg_ln_g, sg_ln_b, sg_w, sg_b,
              w_branch_a, w_branch_b, w_branch_c, w_out, norm_mlp_g, w_up, w_down,
              norm_ple_g, w_ple_gate, w_ple_proj, norm_final_g):
    bsz, S, _ = x.shape
    inv_freq = ROPE_THETA ** (-(jnp.arange(0, ROT_DIM, 2, dtype=jnp.float32) / ROT_DIM))
    ang = positions.astype(jnp.float32)[..., None] * inv_freq
    cos = jnp.cos(ang)[:, :, None, :].astype(x.dtype)
    sin = jnp.sin(ang)[:, :, None, :].astype(x.dtype)
    widths = [ATTN_WIDTH, ATTN_WIDTH, ATTN_WIDTH, CONV_WIDTH, CONV_WIDTH, CONV_WIDTH, 2 * SG_WIDTH]
    splits = [int(c) for c in np.cumsum(widths)]
    h = x
    for i in range(DEPTH):
        a = rms_norm(h, norm_mix_g[i])
        z = a @ w_in[i]
        zq, zk, zv, zx, zb, zc, zs, zg = jnp.split(z, splits, axis=-1)
        q = rotary_partial(zq.reshape(bsz, S, ATTN_HEADS, HEAD_DIM), cos, sin)
        k = rotary_partial(zk.reshape(bsz, S, ATTN_HEADS, HEAD_DIM), cos, sin)
        v = zv.reshape(bsz, S, ATTN_HEADS, HEAD_DIM)
        ya = dilated_mixture(q, k, v)
        yb = zb * short_conv(zc * zx, conv_w[i])
        yc = spatial_gating(zs, sg_ln_g[i], sg_ln_b[i], sg_w[i], sg_b[i])
        gates = jax.nn.sigmoid(zg.reshape(bsz, S, N_BRANCH, D_MODEL))
        m = (gates[:, :, 0] * (ya @ w_branch_a[i])
             + gates[:, :, 1] * (yb @ w_branch_b[i])
             + gates[:, :, 2] * (yc @ w_branch_c[i]))
        h = h + m @ w_out[i]
        c = rms_norm(h, norm_mlp_g[i])
        h = h + jnp.square(jax.nn.relu(c @ w_up[i])) @ w_down[i]
        e = rms_norm(h, norm_ple_g[i])
        h = h + jax.nn.sigmoid(e @ w_ple_gate[i]) * (p[i] @ w_ple_proj[i])
    return rms_norm(h, norm_final_g)
```

```python
import math
from contextlib import ExitStack

import numpy as np
import concourse.bass as bass
import concourse.mybir as mybir
from concourse.bass_utils import run_bass_kernel_spmd

F32 = mybir.dt.float32
BF16 = mybir.dt.bfloat16
I32 = mybir.dt.int32
AF = mybir.ActivationFunctionType
ALU = mybir.AluOpType
AX = mybir.AxisListType

S = 4096
D = 1024
T = 512
NT = S // T
KC = D // 128
NL = 4
INW = 7936
DFF = 4096
PLE = 256
GROUPS = ((128, 1), (512, 4), (2048, 16))
RMS_EPS = 1e-6
LN_EPS = 1e-5
MASKV = -240000.0
C_Q, C_K, C_V = 0, 768, 1536
C_X, C_B, C_C = 2304, 2816, 3328
C_SU, C_SV = 3840, 4352
C_G = 4864

ENGS = ("pe", "act", "dve", "pool", "sp")
DBG = 99


class Res:
    __slots__ = ("writers", "readers", "excl")

    def __init__(self):
        self.writers = {}
        self.readers = {}
        self.excl = False


class Prog:
    def __init__(self, nc):
        self.nc = nc
        self.streams = {e: [] for e in ENGS}
        self.count = {e: 0 for e in ENGS}
        self.seen = {e: {} for e in ENGS}
        self.semh = {}
        self.res = {}
        self.ndma = 0

    def R(self, *key):
        r = self.res.get(key)
        if r is None:
            r = self.res[key] = Res()
        return r

    def newsem(self, key):
        self.count[key] = 0
        return key

    def _deps(self, eng, reads, writes):
        deps = {}
        for r in reads:
            for k, v in r.writers.items():
                if deps.get(k, 0) < v:
                    deps[k] = v
            if r.excl:
                for k, v in r.readers.items():
                    if k != eng and deps.get(k, 0) < v:
                        deps[k] = v
        for w in writes:
            for dct in (w.writers, w.readers):
                for k, v in dct.items():
                    if k != eng and deps.get(k, 0) < v:
                        deps[k] = v
        if eng == "pe":
            deps.pop("pe", None)
        seen = self.seen[eng]
        st = self.streams[eng]
        for k, v in deps.items():
            if seen.get(k, 0) < v:
                st.append(("w", k, v))
                seen[k] = v

    def op(self, eng, fn, reads=(), writes=()):
        self._deps(eng, reads, writes)
        self.count[eng] += 1
        v = self.count[eng]
        self.streams[eng].append(("o", fn, eng, 1))
        for r in reads:
            r.readers[eng] = v
        for w in writes:
            w.writers = {eng: v}
            w.readers = {}

    def dma(self, out, in_, sem, reads=(), writes=(), q="sp"):
        self._deps(q, reads, writes)
        self.count[sem] += 16
        v = self.count[sem]
        self.streams[q].append(("o", lambda e, o=out, i=in_: e.dma_start(out=o, in_=i), sem, 16))
        for r in reads:
            r.readers[sem] = v
        for w in writes:
            w.writers = {sem: v}
            w.readers = {}
        self.ndma += 1

    def barrier(self):
        tot = dict(self.count)
        for e in ENGS:
            seen = self.seen[e]
            for k, v in tot.items():
                if k == e and e == "pe":
                    continue
                if v > 0 and seen.get(k, 0) < v:
                    self.streams[e].append(("w", k, v))
                    seen[k] = v
        for r in self.res.values():
            r.writers = {}
            r.readers = {}

    def emit(self):
        nc = self.nc
        with ExitStack() as es:
            for k in self.count:
                self.semh[k] = es.enter_context(nc.semaphore("s_" + str(k)))
            block = es.enter_context(nc.Block())

            def replay(e, name):
                semh = self.semh
                for it in self.streams[name]:
                    if it[0] == "w":
                        e.wait_ge(semh[it[1]], it[2])
                    else:
                        ins = it[1](e)
                        ins.then_inc(semh[it[2]], it[3])

            @block.tensor
            def _(e):
                replay(e, "pe")

            @block.scalar
            def _(e):
                replay(e, "act")

            @block.vector
            def _(e):
                replay(e, "dve")

            @block.gpsimd
            def _(e):
                replay(e, "pool")

            @block.sync
            def _(e):
                replay(e, "sp")


class Arena:
    def __init__(self, nc, limit=229000):
        self.nc = nc
        self.off = 16640
        self.n = 0
        self.limit = limit

    def alloc(self, shape, dtype, name="t"):
        esz = 4 if dtype in (F32, I32) else 2
        nbytes = esz * int(np.prod(shape[1:]))
        nbytes = (nbytes + 63) // 64 * 64
        off = self.off
        self.off += nbytes
        assert self.off <= self.limit, ("SBUF arena overflow", name, self.off)
        self.n += 1
        return self.nc.alloc_sbuf_tensor_at("%s_%d" % (name, self.n), list(shape), dtype, offset=off)

    def mark(self):
        return self.off

    def reset(self, m):
        self.off = m


def build_program(nl=NL, final_norm=True, stop=99):
    nc = bass.Bass("TRN2", target_bir_lowering=False)
    P = Prog(nc)
    A = Arena(nc)

    def din(name, shape, dt=F32):
        return nc.dram_tensor(name, list(shape), dt, kind="ExternalInput").ap()

    def dscr(name, shape, dt):
        return nc.dram_tensor(name, list(shape), dt, kind="Internal").ap()

    xT = din("xT", [D, S])
    pT = din("pT", [NL, PLE, S])
    pos = din("pos", [1, S], I32)
    w_in = din("w_in", [NL, D, INW])
    w_a = din("w_a", [NL, 768, D])
    w_b = din("w_b", [NL, 512, D])
    w_c = din("w_c", [NL, 512, D])
    w_out = din("w_out", [NL, D, D])
    w_up = din("w_up", [NL, D, DFF])
    w_down = din("w_down", [NL, DFF, D])
    w_pg = din("w_pg", [NL, D, D])
    w_pe = din("w_pe", [NL, PLE, D])
    sgwT = din("sgwT", [NL, 4, 128, 128])
    gains = din("gains", [128, NL * 3 + 1, KC])
    convw = din("convw", [128, NL, 4, 3])
    lng = din("lng", [NL, 1, 512])
    lnb = din("lnb", [NL, 1, 512])
    sgb = din("sgb", [NL, 1, 512])
    cmat = din("cmat", [128, 4, 128])
    cmask = din("cmask", [128, 2, 512])
    invf = din("invf", [128, 1])
    outT = nc.dram_tensor("outT", [D, S], F32, kind="ExternalOutput").ap()

    hT = dscr("hT", [D, S], F32)
    cosD = dscr("cosD", [128, S], F32)
    sinD = dscr("sinD", [128, S], F32)
    qkT = dscr("qkT", [12, 128, S], BF16)
    Vd = dscr("Vd", [3, 128, 32 * 256], BF16)
    ybT = dscr("ybT", [4, 128, S], BF16)
    ycT = dscr("ycT", [4, 128, S], BF16)
    gT = dscr("gT", [24, 128, S], BF16)
    yaT = dscr("yaT", [6, 128, S], BF16)
    fT = dscr("fT", [32, 128, S], BF16)
    cTd = dscr("cTd", [KC, 128, S], BF16)
    aTd = dscr("aTd", [KC, 128, S], BF16)

    PS = [nc.alloc_psum_tensor("ps%d" % i, [128, 512], F32) for i in range(8)]
    PSR = [P.R("ps", i) for i in range(8)]
    for r_ in PSR:
        r_.excl = True

    cm = A.alloc([128, 4, 128], BF16, "cm")
    ident, rotT, onesb, trilb = (cm[:, i, :] for i in range(4))
    cmf = A.alloc([128, 4, 128], F32, "cmf")
    onesf = cmf[:, 2, :]
    maskb = A.alloc([128, 2, 512], BF16, "maskb")
    gsb = A.alloc([128, NL * 3 + 1, KC], F32, "gsb")
    cwsb = A.alloc([128, NL, 4, 3], F32, "cwsb")
    invfsb = A.alloc([128, 1], F32, "invf")
    Rconst = P.R("const")
    base_mark = A.mark()

    cnt = {"sem": 0}

    def dsem():
        cnt["sem"] += 1
        key = "d%d" % cnt["sem"]
        if key not in P.count:
            P.newsem(key)
        return key

    _pbar = P.barrier

    def barrier():
        _pbar()
        cnt["sem"] = 0

    P.barrier = barrier

    class Slots:
        def __init__(self, n, shape, dtype, name):
            self.t = [A.alloc(shape, dtype, name) for _ in range(n)]
            self.r = [P.R(name, id(self), i) for i in range(n)]
            self.s = [dsem() for _ in range(n)]
            self.i = -1
            self.n = n

        def next(self):
            self.i = (self.i + 1) % self.n
            return self.t[self.i], self.r[self.i], self.s[self.i]

    psrot = {}

    def next_ps(lo=0, hi=8):
        i = psrot.get((lo, hi), lo)
        psrot[(lo, hi)] = lo + (i + 1 - lo) % (hi - lo)
        return PS[i], PSR[i]

    def setup():
        m = A.mark()
        s0 = dsem()
        mf = A.alloc([128, 2, 512], F32, "mf")
        P.dma(cmf[:], cmat, s0, writes=[Rconst])
        P.dma(mf[:], cmask, s0, writes=[Rconst])
        P.dma(gsb[:], gains, s0, writes=[Rconst])
        P.dma(cwsb[:], convw, s0, writes=[Rconst])
        P.dma(invfsb[:], invf, s0, writes=[Rconst])
        P.op("dve", lambda e: e.tensor_copy(out=cm[:], in_=cmf[:]), reads=[Rconst], writes=[Rconst])
        P.op("dve", lambda e: e.tensor_copy(out=maskb[:], in_=mf[:]), reads=[Rconst], writes=[Rconst])
        posi = A.alloc([128, S], I32, "posi")
        ang = A.alloc([128, S], F32, "ang")
        t1 = A.alloc([128, S], F32, "t1")
        ki = A.alloc([128, S], I32, "ki")
        kf = A.alloc([128, S], F32, "kf")
        r_ang, r_t1, r_ki, r_kf = (P.R("su", i) for i in range(4))
        P.dma(posi[:], pos.broadcast_to([128, S]), dsem(), writes=[r_ki])
        s_st = dsem()
        P.op("dve", lambda e: e.tensor_copy(out=t1[:], in_=posi[:]), reads=[r_ki], writes=[r_t1])
        P.op("dve", lambda e: e.tensor_scalar(out=ang[:], in0=t1[:], scalar1=invfsb[:, 0:1], scalar2=None,
                                              op0=ALU.mult), reads=[r_t1, Rconst], writes=[r_ang])
        C1 = 6.28125
        C2 = 2.0 * math.pi - 6.28125
        i2p = 1.0 / (2.0 * math.pi)
        for which, (shiftk, shifta, dst) in enumerate(((0.0, 0.0, sinD), (0.25, 0.5 * math.pi, cosD))):
            P.op("dve", lambda e, sk=shiftk: e.tensor_scalar(out=t1[:], in0=ang[:], scalar1=i2p, scalar2=sk,
                                                             op0=ALU.mult, op1=ALU.add),
                 reads=[r_ang], writes=[r_t1])
            P.op("dve", lambda e: e.tensor_copy(out=ki[:], in_=t1[:]), reads=[r_t1], writes=[r_ki])
            P.op("dve", lambda e: e.tensor_copy(out=kf[:], in_=ki[:]), reads=[r_ki], writes=[r_kf])
            P.op("dve", lambda e: e.scalar_tensor_tensor(out=t1[:], in0=kf[:], scalar=-C1, in1=ang[:],
                                                         op0=ALU.mult, op1=ALU.add),
                 reads=[r_kf, r_ang], writes=[r_t1])
            P.op("dve", lambda e: e.scalar_tensor_tensor(out=t1[:], in0=kf[:], scalar=-C2, in1=t1[:],
                                                         op0=ALU.mult, op1=ALU.add),
                 reads=[r_kf, r_t1], writes=[r_t1])
            P.op("dve", lambda e, sa=shifta: e.tensor_scalar(out=t1[:], in0=t1[:], scalar1=sa, scalar2=3.1415925,
                                                             op0=ALU.add, op1=ALU.min),
                 reads=[r_t1], writes=[r_t1])
            P.op("dve", lambda e: e.tensor_scalar(out=t1[:], in0=t1[:], scalar1=-3.1415925, scalar2=None,
                                                  op0=ALU.max), reads=[r_t1], writes=[r_t1])
            P.op("act", lambda e: e.activation(out=kf[:], in_=t1[:], func=AF.Sin), reads=[r_t1], writes=[r_kf])
            P.dma(dst, kf[:], s_st, reads=[r_kf], writes=[P.R("trig", which)])
        P.barrier()
        A.reset(m)

    class WLoader:
        def __init__(self, kcmax=8, ncmax=256):
            self.stg = Slots(2, [128, kcmax, ncmax], F32, "wstg")
            self.kcmax = kcmax
            self.ncmax = ncmax
            self.tog = 0

        def issue(self, wsrc, c0, ncols, dst, dres, kc0=0, kcn=None):
            K = wsrc.shape[0]
            if kcn is None:
                kcn = K // 128
            src3 = wsrc.rearrange("(kc p) n -> p kc n", p=128)
            tok = []
            for k0 in range(kc0, kc0 + kcn, self.kcmax):
                kn = min(self.kcmax, kc0 + kcn - k0)
                for cc in range(0, ncols, self.ncmax):
                    cn = min(self.ncmax, ncols - cc)
                    st, sr, ss = self.stg.next()
                    P.dma(st[:, 0:kn, 0:cn], src3[:, k0:k0 + kn, c0 + cc:c0 + cc + cn], ss, writes=[sr])
                    tok.append((st[:, 0:kn, 0:cn], sr, dst[:, k0:k0 + kn, cc:cc + cn], dres))
            return tok

        def convert(self, tok):
            for (src, sr, dst, dres) in tok:
                self.tog ^= 1
                if self.tog:
                    P.op("act", lambda e, o=dst, i=src: e.activation(out=o, in_=i, func=AF.Copy), reads=[sr], writes=[dres])
                else:
                    P.op("dve", lambda e, o=dst, i=src: e.tensor_copy(out=o, in_=i), reads=[sr], writes=[dres])

        def load(self, wsrc, c0, ncols, dst, dres, kc0=0, kcn=None):
            K = wsrc.shape[0]
            if kcn is None:
                kcn = K // 128
            for k0 in range(kc0, kc0 + kcn, self.kcmax):
                kn = min(self.kcmax, kc0 + kcn - k0)
                for cc in range(0, ncols, self.ncmax):
                    cn = min(self.ncmax, ncols - cc)
                    tok = self.issue(wsrc, c0 + cc, cn, dst, dres, k0, kn)
                    tok = [(a, b, dst[:, k0:k0 + kn, cc:cc + cn], d) for (a, b, _, d) in tok]
                    self.convert(tok)

    class WPipe:
        def __init__(self, wl, wsl, W, blocks):
            self.wl, self.wsl, self.W, self.blocks = wl, wsl, W, blocks
            self.cur = None
            self.nxt = None
            self.tok = None

        def _issue(self, i):
            c0, ncols = self.blocks[i]
            wt, wr, _ = self.wsl.next()
            tok = self.wl.issue(self.W, c0, ncols, wt, wr)
            return (wt, wr), tok

        def begin(self, i):
            if i == 0:
                self.cur, tok = self._issue(0)
                self.wl.convert(tok)
            else:
                assert self.nxt is not None
                self.cur = self.nxt
            self.nxt = None
            if i + 1 < len(self.blocks):
                self.nxt, self.tok = self._issue(i + 1)
            return self.cur

        def mid(self):
            if self.tok is not None:
                self.wl.convert(self.tok)
                self.tok = None

    bg = {"q": [], "tog": 0}
    bg_stg = [A.alloc([128, 8, 256], F32, "bgstg") for _ in range(2)]
    bg_res = [P.R("bgstg", i) for i in range(2)]
    bg_sem = [P.newsem("bg%d" % i) for i in range(2)]
    bg_i = {"i": 0}

    def bg_queue(wsrc, dst, dres):
        K, N = wsrc.shape
        src3 = wsrc.rearrange("(kc p) n -> p kc n", p=128)
        for k0 in range(0, K // 128, 8):
            kn = min(8, K // 128 - k0)
            for cc in range(0, N, 256):
                cn = min(256, N - cc)
                bg["q"].append((src3[:, k0:k0 + kn, cc:cc + cn], dst[:, k0:k0 + kn, cc:cc + cn], dres, kn, cn))

    def bg_pump(n=1):
        for _ in range(n):
            if not bg["q"]:
                return
            src, dst, dres, kn, cn = bg["q"].pop(0)
            i = bg_i["i"]
            bg_i["i"] ^= 1
            st = bg_stg[i]
            P.dma(st[:, 0:kn, 0:cn], src, bg_sem[i], writes=[bg_res[i]])
            bg["tog"] ^= 1
            if bg["tog"]:
                P.op("act", lambda e, o=dst, i_=st[:, 0:kn, 0:cn]: e.activation(out=o, in_=i_, func=AF.Copy),
                     reads=[bg_res[i]], writes=[dres])
            else:
                P.op("dve", lambda e, o=dst, i_=st[:, 0:kn, 0:cn]: e.tensor_copy(out=o, in_=i_),
                     reads=[bg_res[i]], writes=[dres])

    def bg_flush():
        bg_pump(len(bg["q"]))

    def norm_tile(hs, hr, gidx, out_ap_fn, out_res, tmp, wr_extra=()):
        sq, rstd = tmp["sq"], tmp["rstd"]
        rs = tmp["res"]
        P.op("act", lambda e: e.activation(out=sq[:], in_=hs[:], func=AF.Square), reads=[hr], writes=[rs[0]])
        ps, pr = next_ps()

        def fnn(e, ps=ps):
            ins = None
            for c in range(KC):
                ins = e.matmul(ps[:], lhsT=onesb, rhs=sq[:, c, :], start=(c == 0), stop=(c == KC - 1))
            return ins
        P.op("pe", fnn, reads=[rs[0], Rconst], writes=[pr])
        P.op("dve", lambda e: e.tensor_scalar(out=rstd[:], in0=ps[:], scalar1=1.0 / D, scalar2=RMS_EPS,
                                              op0=ALU.mult, op1=ALU.add), reads=[pr], writes=[rs[2]])
        P.op("act", lambda e: e.activation(out=rstd[:], in_=rstd[:], func=AF.Sqrt), reads=[rs[2]], writes=[rs[2]])
        P.op("dve", lambda e: e.reciprocal(out=rstd[:], in_=rstd[:]), reads=[rs[2]], writes=[rs[2]])
        for c in range(KC):
            P.op("dve", lambda e, c=c: e.scalar_tensor_tensor(out=out_ap_fn(c), in0=hs[:, c, :],
                                                            scalar=gsb[:, gidx, c:c + 1], in1=rstd[:],
                                                            op0=ALU.mult, op1=ALU.mult),
                 reads=[hr, rs[2], Rconst], writes=[out_res] + list(wr_extra))

    def norm_tmp():
        return {"sq": A.alloc([128, KC, T], BF16, "sq"),
                "rstd": A.alloc([128, T], F32, "rstd"), "res": [P.R("nt", A.n, i) for i in range(3)]}

    def hview(src, t):
        return src.rearrange("(c p) s -> p c s", p=128)[:, :, t * T:(t + 1) * T]

    def norm_phase(src, gidx, actT, act_res):
        m = A.mark()
        hsl = Slots(2, [128, KC, T], F32, "hs")
        tmp = norm_tmp()
        for t in range(NT):
            hs, hr, hsem = hsl.next()
            P.dma(hs[:], hview(src, t), hsem, reads=[P.R("hT", t)], writes=[hr])
            norm_tile(hs, hr, gidx, lambda c, t=t: actT[:, c, t * T:(t + 1) * T], act_res[t], tmp)
        P.barrier()
        A.reset(m)

    def load_act(dsrc, actT, act_res, key):
        for t in range(NT):
            sl = slice(t * T, (t + 1) * T)
            P.dma(actT[:, :, sl], dsrc.rearrange("c p s -> p c s")[:, :, sl], dsem(), reads=[P.R(key, t)], writes=[act_res[t]])

    def final_phase(src, gidx):
        m = A.mark()
        hsl = Slots(2, [128, KC, T], F32, "hs")
        osl = Slots(2, [128, KC, T], F32, "os")
        tmp = norm_tmp()
        for t in range(NT):
            hs, hr, hsem = hsl.next()
            ot, orr, osem = osl.next()
            P.dma(hs[:], hview(src, t), hsem, reads=[P.R("hT", t)], writes=[hr])
            if final_norm:
                norm_tile(hs, hr, gidx, lambda c, ot=ot: ot[:, c, :], orr, tmp)
                P.dma(hview(outT, t), ot[:], osem, reads=[orr], writes=[P.R("outT", t)])
            else:
                P.dma(hview(outT, t), hs[:], hsem, reads=[hr], writes=[P.R("outT", t)])
        P.barrier()
        A.reset(m)

    def mm_group(ps, pr, wt, wres, kcn, col0, actT, act_res_t, t, ncols=128):
        def fn(e):
            ins = None
            for kc in range(kcn):
                ins = e.matmul(ps[:ncols, :], lhsT=wt[:, kc, col0:col0 + ncols], rhs=actT[:, kc, t * T:(t + 1) * T],
                               start=(kc == 0), stop=(kc == kcn - 1))
            return ins
        P.op("pe", fn, reads=[wres, act_res_t], writes=[pr])

    def phase_qkv(l, aT, aR, sub=9):
        m = A.mark()
        cs = A.alloc([128, S], F32, "cos")
        sn = A.alloc([128, S], F32, "sin")
        rtr = P.R("trigsb")
        s0 = dsem()
        P.dma(cs[:], cosD, s0, reads=[P.R("trig", 1)], writes=[rtr])
        P.dma(sn[:], sinD, s0, reads=[P.R("trig", 0)], writes=[rtr])
        wl = WLoader()
        wsl = Slots(2, [128, KC, 256], BF16, "wqk")
        zq = Slots(2, [128, T], BF16, "zq")
        t1s = Slots(2, [128, T], F32, "t1")
        t2s = Slots(2, [128, T], F32, "t2")
        rows = Slots(2, [128, S], BF16, "qrow")
        W = w_in[l]
        wpipe = WPipe(wl, wsl, W, [(blk * 256, 256) for blk in range(6)] + [(C_V + g * 256, 256) for g in range(3)])
        for blk in range(6 if sub >= 1 else 0):
            wt, wr = wpipe.begin(blk)
            for ci in range(2):
                if ci == 1:
                    wpipe.mid()
                oc = blk * 2 + ci
                row, rr, rsem = rows.next()
                for t in range(NT):
                    if DBG < 2:
                        continue
                    ps, pr = next_ps()
                    mm_group(ps, pr, wt, wr, KC, ci * 128, aT, aR[t], t)
                    if DBG < 3:
                        continue
                    z, zr, _ = zq.next()
                    P.op("act", lambda e, z=z, ps=ps: e.activation(out=z[:], in_=ps[:], func=AF.Copy),
                         reads=[pr], writes=[zr])
                    ps2, pr2 = next_ps()
                    P.op("pe", lambda e, ps2=ps2, z=z: e.matmul(ps2[:], lhsT=rotT, rhs=z[:], start=True, stop=True),
                         reads=[zr, Rconst], writes=[pr2])
                    if DBG < 4:
                        continue
                    a1, a1r, _ = t1s.next()
                    a2, a2r, _ = t2s.next()
                    P.op("dve", lambda e, a1=a1, ps=ps, t=t: e.tensor_tensor(out=a1[:], in0=ps[:],
                                                                              in1=cs[:, t * T:(t + 1) * T], op=ALU.mult),
                         reads=[pr, rtr], writes=[a1r])
                    P.op("dve", lambda e, a2=a2, ps2=ps2, t=t: e.tensor_tensor(out=a2[:], in0=ps2[:],
                                                                                in1=sn[:, t * T:(t + 1) * T], op=ALU.mult),
                         reads=[pr2, rtr], writes=[a2r])
                    if DBG < 5:
                        continue
                    P.op("dve", lambda e, a1=a1, a2=a2, row=row, t=t: e.tensor_tensor(
                        out=row[:, t * T:(t + 1) * T], in0=a1[:], in1=a2[:], op=ALU.add),
                        reads=[a1r, a2r], writes=[rr])
                if DBG >= 6:
                    P.dma(qkT[oc], row[:], rsem, reads=[rr], writes=[P.R("qkT", oc)])
        vsl = Slots(2, [128, 32, 256], BF16, "vsb")
        for g, (win, dil) in enumerate(GROUPS if sub >= 2 else ()):
            wt, wr = wpipe.begin(6 + g)
            vt, vr, vsem = vsl.next()
            nb = S // (dil * 128)
            for r in range(dil):
                for n in range(nb):
                    bi = r * nb + n
                    if bi == 16:
                        wpipe.mid()
                    t0 = r + dil * 128 * n
                    ps, pr = next_ps()

                    def fn(e, ps=ps, t0=t0, dil=dil, wt=wt):
                        ins = None
                        for kc in range(KC):
                            ins = e.matmul(ps[:, 0:256], lhsT=aT[:, kc, t0:t0 + dil * 127 + 1:dil], rhs=wt[:, kc, :],
                                           start=(kc == 0), stop=(kc == KC - 1))
                        return ins
                    P.op("pe", fn, reads=[wr] + aR, writes=[pr])
                    P.op("act", lambda e, ps=ps, vt=vt, bi=bi: e.activation(out=vt[:, bi, :], in_=ps[:, 0:256],
                                                                          func=AF.Copy), reads=[pr], writes=[vr])
            P.dma(Vd[g], vt[:].rearrange("p b c -> p (b c)"), vsem, reads=[vr], writes=[P.R("Vd", g)])
        P.barrier()
        A.reset(m)

    def phase_conv_gates(l, aT, aR):
        m = A.mark()
        wl = WLoader()
        wsl = Slots(2, [128, KC, 256], BF16, "wcg")
        W = w_in[l]
        zxs = A.alloc([128, S], F32, "zxs")
        ub = A.alloc([128, S + 2], F32, "ub")
        rzx, rub = P.R("zxs"), P.R("ub")
        rows = Slots(2, [128, S], BF16, "ybrow")
        P.op("pool", lambda e: e.memset(ub[:, 0:2], 0.0), writes=[rub])
        blocks = []
        for c in range(4):
            blocks += [(C_X + c * 128, 128), (C_C + c * 128, 128), (C_B + c * 128, 128)]
        blocks += [(C_G + blk * 256, 256) for blk in range(12)]
        wpipe = WPipe(wl, wsl, W, blocks)
        for c in range(4):
            wt, wr = wpipe.begin(3 * c)
            for t in range(NT):
                if t == 4:
                    wpipe.mid()
                ps, pr = next_ps()
                mm_group(ps, pr, wt, wr, KC, 0, aT, aR[t], t)
                P.op("act", lambda e, ps=ps, t=t: e.activation(out=zxs[:, t * T:(t + 1) * T], in_=ps[:], func=AF.Copy),
                     reads=[pr], writes=[rzx])
            wt, wr = wpipe.begin(3 * c + 1)
            for t in range(NT):
                if t == 4:
                    wpipe.mid()
                ps, pr = next_ps()
                mm_group(ps, pr, wt, wr, KC, 0, aT, aR[t], t)
                P.op("dve", lambda e, ps=ps, t=t: e.tensor_tensor(out=ub[:, 2 + t * T:2 + (t + 1) * T], in0=ps[:],
                                                                  in1=zxs[:, t * T:(t + 1) * T], op=ALU.mult),
                     reads=[pr, rzx], writes=[rub])
            wt, wr = wpipe.begin(3 * c + 2)
            P.op("pool", lambda e, c=c: e.tensor_scalar(out=zxs[:], in0=ub[:, 0:S], scalar1=cwsb[:, l, c, 0:1],
                                                        scalar2=None, op0=ALU.mult),
                 reads=[rub, Rconst], writes=[rzx])
            P.op("dve", lambda e, c=c: e.scalar_tensor_tensor(out=zxs[:], in0=ub[:, 1:S + 1], scalar=cwsb[:, l, c, 1:2],
                                                              in1=zxs[:], op0=ALU.mult, op1=ALU.add),
                 reads=[rub, rzx, Rconst], writes=[rzx])
            P.op("dve", lambda e, c=c: e.scalar_tensor_tensor(out=zxs[:], in0=ub[:, 2:S + 2], scalar=cwsb[:, l, c, 2:3],
                                                              in1=zxs[:], op0=ALU.mult, op1=ALU.add),
                 reads=[rub, rzx, Rconst], writes=[rzx])
            row, rr, rsem = rows.next()
            for t in range(NT):
                if t == 4:
                    wpipe.mid()
                ps, pr = next_ps()
                mm_group(ps, pr, wt, wr, KC, 0, aT, aR[t], t)
                P.op("dve", lambda e, ps=ps, t=t, row=row: e.tensor_tensor(out=row[:, t * T:(t + 1) * T], in0=ps[:],
                                                                           in1=zxs[:, t * T:(t + 1) * T], op=ALU.mult),
                     reads=[pr, rzx], writes=[rr])
            P.dma(ybT[c], row[:], rsem, reads=[rr], writes=[P.R("ybT", c)])
        for blk in range(12):
            wt, wr = wpipe.begin(12 + blk)
            for ci in range(2):
                if ci == 1:
                    wpipe.mid()
                oc = blk * 2 + ci
                row, rr, rsem = rows.next()
                for t in range(NT):
                    ps, pr = next_ps()
                    mm_group(ps, pr, wt, wr, KC, ci * 128, aT, aR[t], t)
                    P.op("act", lambda e, ps=ps, t=t, row=row: e.activation(out=row[:, t * T:(t + 1) * T], in_=ps[:],
                                                                            func=AF.Sigmoid), reads=[pr], writes=[rr])
                P.dma(gT[oc], row[:], rsem, reads=[rr], writes=[P.R("gT", oc)])
        P.barrier()
        A.reset(m)

    def phase_sgu(l, aT, aR):
        m = A.mark()
        wl = WLoader()
        W = w_in[l]
        uT = A.alloc([128, 4, S], BF16, "uT")
        ruT = [P.R("uT", b) for b in range(32)]
        wsl = Slots(2, [128, KC, 256], BF16, "wsu")
        s0 = dsem()
        lngs = A.alloc([128, 512], F32, "lng")
        lnbs = A.alloc([128, 512], F32, "lnb")
        sgbs = A.alloc([128, 512], F32, "sgb")
        wsf = A.alloc([128, 4, 128], F32, "wsf")
        wsb = A.alloc([128, 4, 128], BF16, "wsb")
        rc = P.R("sgconst")
        P.dma(lngs[:], lng[l].broadcast_to([128, 512]), s0, writes=[rc])
        P.dma(lnbs[:], lnb[l].broadcast_to([128, 512]), s0, writes=[rc])
        P.dma(sgbs[:], sgb[l].broadcast_to([128, 512]), s0, writes=[rc])
        P.dma(wsf[:], sgwT[l].rearrange("g s t -> s g t"), s0, writes=[rc])
        P.op("dve", lambda e: e.tensor_tensor(out=wsb[:], in0=wsf[:],
                                              in1=cmf[:, 3:4, :].broadcast_to([128, 4, 128]), op=ALU.mult),
             reads=[rc, Rconst], writes=[rc])
        wpipe = WPipe(wl, wsl, W, [(C_SU + blk * 256, 256) for blk in range(2)])
        for blk in range(2):
            wt, wr = wpipe.begin(blk)
            for ci in range(2):
                if ci == 1:
                    wpipe.mid()
                c = blk * 2 + ci
                for t in range(NT):
                    ps, pr = next_ps()
                    mm_group(ps, pr, wt, wr, KC, ci * 128, aT, aR[t], t)
                    P.op("act", lambda e, ps=ps, t=t, c=c: e.activation(out=uT[:, c, t * T:(t + 1) * T], in_=ps[:],
                                                                        func=AF.Gelu),
                         reads=[pr], writes=ruT[4 * t:4 * t + 4])
        wv = A.alloc([128, KC, 512], BF16, "wv")
        rwv = P.R("wv")
        wl.load(W, C_SV, 512, wv, rwv)
        gvs = Slots(2, [128, 4, 512], F32, "gv")
        sqs = Slots(1, [128, 4, 512], F32, "gsq")
        vbs = Slots(2, [128, 4, 512], BF16, "vb")
        sts = Slots(2, [128, 8, 4], F32, "st")
        tts = Slots(2, [128, 512], F32, "tt")
        for t in range(NT):
            gv, gr, _ = gvs.next()
            sq, sr, _ = sqs.next()
            vb, vbr, _ = vbs.next()
            st, str_, _ = sts.next()
            for bb in range(4):
                b = 4 * t + bb
                ps, pr = next_ps(0, 4)

                def fn(e, ps=ps, b=b):
                    ins = None
                    for kc in range(KC):
                        ins = e.matmul(ps[:], lhsT=aT[:, kc, b * 128:(b + 1) * 128], rhs=wv[:, kc, :],
                                       start=(kc == 0), stop=(kc == KC - 1))
                    return ins
                P.op("pe", fn, reads=[rwv, aR[t]], writes=[pr])
                P.op("act", lambda e, gv=gv, ps=ps, bb=bb: e.activation(out=gv[:, bb, :], in_=ps[:], func=AF.Gelu),
                     reads=[pr], writes=[gr])
            P.op("act", lambda e, gv=gv, sq=sq: e.activation(out=sq[:], in_=gv[:], func=AF.Square), reads=[gr], writes=[sr])
            P.op("dve", lambda e, gv=gv, st=st: e.tensor_reduce(out=st[:, 0, :], in_=gv[:], axis=AX.X, op=ALU.add),
                 reads=[gr], writes=[str_])
            P.op("dve", lambda e, sq=sq, st=st: e.tensor_reduce(out=st[:, 1, :], in_=sq[:], axis=AX.X, op=ALU.add),
                 reads=[sr, str_], writes=[str_])
            P.op("dve", lambda e, st=st: e.tensor_scalar(out=st[:, 2:4, :], in0=st[:, 0:2, :], scalar1=1.0 / 512, scalar2=None,
                                                         op0=ALU.mult), reads=[str_], writes=[str_])
            P.op("dve", lambda e, st=st: e.tensor_tensor(out=st[:, 4, :], in0=st[:, 2, :], in1=st[:, 2, :], op=ALU.mult),
                 reads=[str_], writes=[str_])
            P.op("dve", lambda e, st=st: e.scalar_tensor_tensor(out=st[:, 5, :], in0=st[:, 3, :], scalar=LN_EPS, in1=st[:, 4, :],
                                                                op0=ALU.add, op1=ALU.subtract), reads=[str_], writes=[str_])
            P.op("act", lambda e, st=st: e.activation(out=st[:, 6, :], in_=st[:, 5, :], func=AF.Sqrt), reads=[str_], writes=[str_])
            P.op("dve", lambda e, st=st: e.reciprocal(out=st[:, 7, :], in_=st[:, 6, :]), reads=[str_], writes=[str_])
            for bb in range(4):
                P.op("dve", lambda e, st=st, gv=gv, bb=bb: e.tensor_scalar(
                    out=gv[:, bb, :], in0=gv[:, bb, :], scalar1=st[:, 2, bb:bb + 1], scalar2=st[:, 7, bb:bb + 1],
                    op0=ALU.subtract, op1=ALU.mult), reads=[gr, str_], writes=[gr])
            P.op("pool", lambda e, gv=gv: e.tensor_tensor(out=gv[:, 0:2, :], in0=gv[:, 0:2, :],
                                                          in1=lngs[:].unsqueeze(1).broadcast_to([128, 2, 512]), op=ALU.mult),
                 reads=[gr, rc], writes=[gr])
            P.op("dve", lambda e, gv=gv: e.tensor_tensor(out=gv[:, 2:4, :], in0=gv[:, 2:4, :],
                                                         in1=lngs[:].unsqueeze(1).broadcast_to([128, 2, 512]), op=ALU.mult),
                 reads=[gr, rc], writes=[gr])
            P.op("pool", lambda e, gv=gv, vb=vb: e.tensor_tensor(out=vb[:, 0:2, :], in0=gv[:, 0:2, :],
                                                                 in1=lnbs[:].unsqueeze(1).broadcast_to([128, 2, 512]), op=ALU.add),
                 reads=[gr, rc], writes=[vbr])
            P.op("dve", lambda e, gv=gv, vb=vb: e.tensor_tensor(out=vb[:, 2:4, :], in0=gv[:, 2:4, :],
                                                                in1=lnbs[:].unsqueeze(1).broadcast_to([128, 2, 512]), op=ALU.add),
                 reads=[gr, rc], writes=[vbr])
            for bb in range(4):
                b = 4 * t + bb
                ps2, pr2 = next_ps(4, 8)

                def fn2(e, ps2=ps2, vb=vb, bb=bb):
                    ins = None
                    for g in range(4):
                        ins = e.matmul(ps2[:, g * 128:(g + 1) * 128], lhsT=vb[:, bb, g * 128:(g + 1) * 128], rhs=wsb[:, g, :],
                                       start=True, stop=True)
                    return ins
                P.op("pe", fn2, reads=[vbr, rc], writes=[pr2])
                tt, ttr, _ = tts.next()
                P.op("dve", lambda e, tt=tt, ps2=ps2: e.tensor_tensor(out=tt[:], in0=ps2[:], in1=sgbs[:], op=ALU.add),
                     reads=[pr2, rc], writes=[ttr])
                ub_ = uT[:, :, b * 128:(b + 1) * 128]
                P.op("dve", lambda e, tt=tt, ub_=ub_: e.tensor_tensor(out=ub_, in0=tt[:].rearrange("p (g t) -> p g t", g=4),
                                                                      in1=ub_, op=ALU.mult),
                     reads=[ttr, ruT[b]], writes=[ruT[b]])
        s1 = dsem()
        for c in range(4):
            P.dma(ycT[c], uT[:, c, :], s1, reads=ruT, writes=[P.R("ycT", c)])
        P.barrier()
        A.reset(m)

    def phase_attn(l):
        m = A.mark()
        U = A.alloc([128, 6, S], BF16, "U")
        Dn = A.alloc([128, 2, S], F32, "Dn")
        rU = [P.R("U", t) for t in range(NT)]
        rD = [P.R("Dn", t) for t in range(NT)]
        qs = A.alloc([128, 2, S], BF16, "qs")
        ks = A.alloc([128, 2, S], BF16, "ks")
        vs = A.alloc([128, 32, 256], BF16, "vs")
        rq = [P.R("qs", 0), P.R("qs", 1)]
        rk = [P.R("ks", 0), P.R("ks", 1)]
        rv = P.R("vs")
        sq_ = [dsem(), dsem()]
        sk_ = [dsem(), dsem()]
        sv_ = dsem()
        pts = [Slots(2, [128, 512], BF16, "pT%d" % i) for i in range(2)]
        s0 = dsem()
        order = (2, 1, 0)
        for gi, g in enumerate(order):
            win, dil = GROUPS[g]
            nb = S // (dil * 128)
            for cc in range(2):
                P.dma(qs[:, cc, :], qkT[2 * g + cc], sq_[cc], reads=[P.R("qkT", 2 * g + cc)], writes=[rq[cc]])
                P.dma(ks[:, cc, :], qkT[6 + 2 * g + cc], sk_[cc], reads=[P.R("qkT", 6 + 2 * g + cc)], writes=[rk[cc]])
            P.dma(vs[:].rearrange("p b c -> p (b c)"), Vd[g], sv_, reads=[P.R("Vd", g)], writes=[rv])
            for r in range(dil):
                for n in range(nb):
                    t0 = r + dil * 128 * n
                    span = dil * 128
                    qsl = slice(t0, t0 + dil * 127 + 1, dil)
                    if (r * nb + n) % 4 == 1:
                        bg_pump(1)
                    kbs = []
                    if n > 0:
                        kbs.append((0, slice(t0 - span, t0 - span + dil * 127 + 1, dil), r * nb + n - 1))
                    kbs.append((1, qsl, r * nb + n))
                    lo = 0 if n > 0 else 256
                    psE, prE = next_ps(0, 6)
                    psO, prO = next_ps(0, 6)

                    def fn(e, psE=psE, psO=psO, kbs=kbs, qsl=qsl):
                        ins = None
                        for (mi, ksl, vbi) in kbs:
                            for ps in (psE, psO):
                                e.matmul(ps[:, mi * 256:(mi + 1) * 256], lhsT=ident, rhs=maskb[:, mi, 0:256],
                                         start=True, stop=False, skip_group_check=True)
                        for (mi, ksl, vbi) in kbs:
                            for jj in range(2):
                                for par, ps in ((0, psE), (1, psO)):
                                    pb = par * 64
                                    c0 = mi * 256 + jj * 128
                                    ins = e.matmul(ps[:, c0:c0 + 128], lhsT=ks[pb:pb + 64, jj, ksl],
                                                   rhs=qs[pb:pb + 64, jj, qsl], start=False, stop=True,
                                                   skip_group_check=True)
                        return ins
                    P.op("pe", fn, reads=rq + rk + [Rconst], writes=[prE, prO])
                    ptE, ptEr, _ = pts[0].next()
                    ptO, ptOr, _ = pts[1].next()
                    P.op("act", lambda e, pt=ptE, ps=psE, lo=lo: e.activation(out=pt[:, lo:512], in_=ps[:, lo:512],
                                                                             func=AF.Exp, scale=0.125),
                         reads=[prE], writes=[ptEr])
                    P.op("act", lambda e, pt=ptO, ps=psO, lo=lo: e.activation(out=pt[:, lo:512], in_=ps[:, lo:512],
                                                                             func=AF.Exp, scale=0.125),
                         reads=[prO], writes=[ptOr])
                    pso, pro = next_ps(6, 8)

                    def fn3(e, pso=pso, kbs=kbs, ptE=ptE, ptO=ptO):
                        ins = None
                        nk = len(kbs)
                        for j in range(4):
                            pb = (j % 2) * 64
                            pr_ = j // 2
                            pt = ptE if j % 2 == 0 else ptO
                            kw = {"tile_position": (0, pb)} if pb else {}
                            for i, (mi, ksl, vbi) in enumerate(kbs):
                                c0 = mi * 256 + pr_ * 128
                                ins = e.matmul(pso[pb:pb + 64, pr_ * 128:(pr_ + 1) * 128],
                                               lhsT=vs[:, vbi, j * 64:(j + 1) * 64], rhs=pt[:, c0:c0 + 128],
                                               start=(i == 0), stop=(i == nk - 1), skip_group_check=True, **kw)
                            for i, (mi, ksl, vbi) in enumerate(kbs):
                                c0 = mi * 256 + pr_ * 128
                                ins = e.matmul(pso[pb:pb + 64, 256 + pr_ * 128:256 + (pr_ + 1) * 128],
                                               lhsT=onesb[:, 0:64], rhs=pt[:, c0:c0 + 128],
                                               start=(i == 0), stop=(i == nk - 1), skip_group_check=True, **kw)
                        return ins
                    P.op("pe", fn3, reads=[rv, Rconst, ptEr, ptOr], writes=[pro])
                    tiles = sorted(set(range(t0 // T, (t0 + dil * 127) // T + 1)))
                    P.op("act", lambda e, pso=pso, g=g, qsl=qsl: e.activation(
                        out=U[:, 2 * g:2 * g + 2, qsl], in_=pso[:, 0:256].rearrange("p (a q) -> p a q", a=2), func=AF.Copy),
                        reads=[pro], writes=[rU[t] for t in tiles])
                    if gi == 0:
                        P.op("dve", lambda e, pso=pso, qsl=qsl: e.tensor_copy(
                            out=Dn[:, :, qsl], in_=pso[:, 256:512].rearrange("p (a q) -> p a q", a=2)),
                            reads=[pro], writes=[rD[t] for t in tiles])
                    else:
                        P.op("dve", lambda e, pso=pso, qsl=qsl: e.tensor_tensor(
                            out=Dn[:, :, qsl], in0=pso[:, 256:512].rearrange("p (a q) -> p a q", a=2),
                            in1=Dn[:, :, qsl], op=ALU.add),
                            reads=[pro] + [rD[t] for t in tiles], writes=[rD[t] for t in tiles])
        s1 = dsem()
        for t in range(NT):
            sl = slice(t * T, (t + 1) * T)
            P.op("dve", lambda e, sl=sl: e.reciprocal(out=Dn[:, :, sl], in_=Dn[:, :, sl]), reads=[rD[t]], writes=[rD[t]])
            for g in range(3):
                eng = "pool" if g == 1 else "dve"
                P.op(eng, lambda e, sl=sl, g=g: e.tensor_tensor(out=U[:, 2 * g:2 * g + 2, sl], in0=U[:, 2 * g:2 * g + 2, sl],
                                                                in1=Dn[:, :, sl], op=ALU.mult),
                     reads=[rU[t], rD[t]], writes=[rU[t]])
        for c in range(6):
            P.dma(yaT[c], U[:, c, :], s1, reads=rU, writes=[P.R("yaT", c)])
        P.barrier()
        A.reset(m)

    def alloc_merge_w(l):
        wa = A.alloc([128, 6, D], BF16, "wa")
        wb_ = A.alloc([128, 4, D], BF16, "wb")
        wc_ = A.alloc([128, 4, D], BF16, "wc")
        wo = A.alloc([128, 8, D], BF16, "wo")
        rw = P.R("wmerge", l)
        bg_queue(w_a[l], wa, rw)
        bg_queue(w_b[l], wb_, rw)
        bg_queue(w_c[l], wc_, rw)
        bg_queue(w_out[l], wo, rw)
        return wa, wb_, wc_, wo, rw

    def phase_merge(l, hsrc, wts):
        m = A.mark()
        wa, wb_, wc_, wo, rw = wts
        bg_flush()
        ysl = Slots(2, [128, 14, T], BF16, "ys")
        gsl = Slots(2, [128, 3, 4, T], BF16, "gs")
        csl = Slots(2, [128, KC, T], BF16, "ct")
        ntmp = norm_tmp()
        hsl = Slots(2, [128, KC, T], F32, "hs")
        mT = A.alloc([128, KC, T], BF16, "mT")
        rm = P.R("mT")
        tsl = [Slots(2, [128, T], F32, "mt%d" % i) for i in range(3)]
        def load(t):
            sl = slice(t * T, (t + 1) * T)
            yt, yr, ysem = ysl.next()
            hs, hr, hsem = hsl.next()
            P.dma(yt[:, 0:6, :], yaT.rearrange("c p s -> p c s")[:, :, sl], ysem,
                  reads=[P.R("yaT", c) for c in range(6)], writes=[yr])
            P.dma(yt[:, 6:10, :], ybT.rearrange("c p s -> p c s")[:, :, sl], ysem,
                  reads=[P.R("ybT", c) for c in range(4)], writes=[yr])
            P.dma(yt[:, 10:14, :], ycT.rearrange("c p s -> p c s")[:, :, sl], ysem,
                  reads=[P.R("ycT", c) for c in range(4)], writes=[yr])
            P.dma(hs[:], hview(hsrc, t), hsem, reads=[P.R("hT", t)], writes=[hr])
            return yt, yr, hs, hr, hsem

        def loadg(t, half):
            sl = slice(t * T, (t + 1) * T)
            gt, gr, gsem = gsl.next()
            for b3 in range(3):
                P.dma(gt[:, b3, :, :], gT.rearrange("c p s -> p c s")[:, b3 * 8 + half * 4:b3 * 8 + half * 4 + 4, sl], gsem,
                      reads=[P.R("gT", c) for c in range(b3 * 8 + half * 4, b3 * 8 + half * 4 + 4)],
                      writes=[P.R("gsq", id(gt), b3)])
            return gt, [P.R("gsq", id(gt), b3) for b3 in range(3)]

        nxt = load(0)
        gnx = loadg(0, 0)
        for t in range(NT):
            yt, yr, hs, hr, hsem = nxt
            if t + 1 < NT:
                nxt = load(t + 1)
            for oc in range(KC):
                if oc % 4 == 0:
                    gt, grl = gnx
                    if oc == 0:
                        gnx = loadg(t, 1)
                    elif t + 1 < NT:
                        gnx = loadg(t + 1, 0)
                prods = []
                for bi, (wt, k0, kn) in enumerate(((wa, 0, 6), (wb_, 6, 4), (wc_, 10, 4))):
                    ps, pr = next_ps()

                    def fn(e, ps=ps, wt=wt, k0=k0, kn=kn, yt=yt, oc=oc):
                        ins = None
                        for kc in range(kn):
                            ins = e.matmul(ps[:], lhsT=wt[:, kc, oc * 128:(oc + 1) * 128], rhs=yt[:, k0 + kc, :],
                                           start=(kc == 0), stop=(kc == kn - 1))
                        return ins
                    P.op("pe", fn, reads=[rw, yr], writes=[pr])
                    tt, ttr, _ = tsl[bi].next()
                    P.op("dve", lambda e, tt=tt, ps=ps, gt=gt, bi=bi, oc=oc: e.tensor_tensor(
                        out=tt[:], in0=ps[:], in1=gt[:, bi, oc % 4, :], op=ALU.mult), reads=[pr] + grl, writes=[ttr])
                    prods.append((tt, ttr))
                (ta, ra), (tb, rb), (tc, rc_) = prods
                P.op("dve", lambda e, ta=ta, tb=tb: e.tensor_tensor(out=ta[:], in0=ta[:], in1=tb[:], op=ALU.add),
                     reads=[ra, rb], writes=[ra])
                P.op("pool", lambda e, ta=ta, tc=tc, oc=oc: e.tensor_tensor(out=mT[:, oc, :], in0=ta[:], in1=tc[:], op=ALU.add),
                     reads=[ra, rc_], writes=[rm])
            for oc in range(KC):
                ps, pr = next_ps()

                def fn(e, ps=ps, oc=oc):
                    ins = None
                    for kc in range(KC):
                        ins = e.matmul(ps[:], lhsT=wo[:, kc, oc * 128:(oc + 1) * 128], rhs=mT[:, kc, :],
                                       start=(kc == 0), stop=(kc == KC - 1))
                    return ins
                P.op("pe", fn, reads=[rw, rm], writes=[pr])
                P.op("dve", lambda e, ps=ps, hs=hs, oc=oc: e.tensor_tensor(out=hs[:, oc, :], in0=ps[:], in1=hs[:, oc, :],
                                                                           op=ALU.add), reads=[pr, hr], writes=[hr])
            P.dma(hview(hT, t), hs[:], hsem, reads=[hr], writes=[P.R("hT", t)])
            ct, ctr, ctsem = csl.next()
            norm_tile(hs, hr, l * 3 + 1, lambda c, ct=ct: ct[:, c, :], ctr, ntmp)
            P.dma(cTd.rearrange("c p s -> p c s")[:, :, t * T:(t + 1) * T], ct[:], ctsem, reads=[ctr], writes=[P.R("cTd", t)])
        P.barrier()
        A.reset(m)

    def phase_ffn_up(l, cT, cR):
        m = A.mark()
        wl = WLoader()
        wsl = Slots(2, [128, KC, 256], BF16, "wup")
        rows = Slots(2, [128, S], BF16, "frow")
        rl = Slots(3, [128, T], F32, "relu")
        wpipe = WPipe(wl, wsl, w_up[l], [(blk * 256, 256) for blk in range(16)])
        for blk in range(16):
            wt, wr = wpipe.begin(blk)
            for ci in range(2):
                if ci == 1:
                    wpipe.mid()
                fc = blk * 2 + ci
                bg_pump(1)
                row, rr, rsem = rows.next()
                for t in range(NT):
                    ps, pr = next_ps()
                    mm_group(ps, pr, wt, wr, KC, ci * 128, cT, cR[t], t)
                    rt, rtr, _ = rl.next()
                    P.op("act", lambda e, rt=rt, ps=ps: e.activation(out=rt[:], in_=ps[:], func=AF.Relu), reads=[pr], writes=[rtr])
                    P.op("dve", lambda e, rt=rt, row=row, t=t: e.tensor_tensor(out=row[:, t * T:(t + 1) * T], in0=rt[:],
                                                                               in1=rt[:], op=ALU.mult),
                         reads=[rtr], writes=[rr])
                P.dma(fT[fc], row[:], rsem, reads=[rr], writes=[P.R("fT", fc)])
        P.barrier()
        A.reset(m)

    def phase_ffn_down(l, wd, rw):
        m = A.mark()
        bg_flush_except = None
        fsl = Slots(2, [128, 32, T], BF16, "fs")
        hsl = Slots(2, [128, KC, T], F32, "hs")
        def load(t):
            sl = slice(t * T, (t + 1) * T)
            ft, fr, fsem = fsl.next()
            hs, hr, hsem = hsl.next()
            bg_pump(1)
            for q4 in range(4):
                P.dma(ft[:, 8 * q4:8 * q4 + 8, :], fT.rearrange("c p s -> p c s")[:, 8 * q4:8 * q4 + 8, sl], fsem,
                      reads=[P.R("fT", c) for c in range(8 * q4, 8 * q4 + 8)], writes=[P.R("fsq", id(ft), q4)])
            P.dma(hs[:], hview(hT, t), hsem, reads=[P.R("hT", t)], writes=[hr])
            return ft, [P.R("fsq", id(ft), q4) for q4 in range(4)], hs, hr, hsem

        nxt = load(0)
        for t in range(NT):
            ft, frl, hs, hr, hsem = nxt
            if t + 1 < NT:
                nxt = load(t + 1)
            for oc in range(KC):
                ps, pr = next_ps()

                def fn(e, ps=ps, oc=oc, ft=ft):
                    ins = None
                    for kc in range(32):
                        ins = e.matmul(ps[:], lhsT=wd[:, kc, oc * 128:(oc + 1) * 128], rhs=ft[:, kc, :],
                                       start=(kc == 0), stop=(kc == 31))
                    return ins
                P.op("pe", fn, reads=[rw] + frl, writes=[pr])
                P.op("dve", lambda e, ps=ps, hs=hs, oc=oc: e.tensor_tensor(out=hs[:, oc, :], in0=ps[:], in1=hs[:, oc, :],
                                                                           op=ALU.add), reads=[pr, hr], writes=[hr])
            P.dma(hview(hT, t), hs[:], hsem, reads=[hr], writes=[P.R("hT", t)])
        P.barrier()
        A.reset(m)

    def phase_ple(l, wg, wp, rw, dead_lo, fuse):
        m = A.mark()
        bg_flush()
        keep = A.off
        A.off = dead_lo
        hsl_ = Slots(2, [128, KC, T], F32, "hs")
        if fuse == "final":
            osl = Slots(2, [128, KC, T], F32, "os")
        elif fuse == "next":
            osl = Slots(2, [128, KC, T], BF16, "as")
        assert A.off <= dead_lo + 64 * 1024
        A.off = keep
        hsl = hsl_
        psl = Slots(2, [128, 2, T], F32, "pf")
        pbs = Slots(2, [128, 2, T], BF16, "pb")
        eT = A.alloc([128, KC, T], BF16, "eT")
        re_ = P.R("eT")
        tmp = norm_tmp()
        sgs = Slots(2, [128, T], F32, "sg")
        tts = Slots(2, [128, T], F32, "pt")
        def load(t):
            sl = slice(t * T, (t + 1) * T)
            hs, hr, hsem = hsl.next()
            pf, pfr, pfsem = psl.next()
            P.dma(hs[:], hview(hT, t), hsem, reads=[P.R("hT", t)], writes=[hr])
            P.dma(pf[:], pT[l].rearrange("(c p) s -> p c s", p=128)[:, :, sl], pfsem, writes=[pfr])
            return hs, hr, hsem, pf, pfr

        nxt = load(0)
        for t in range(NT):
            hs, hr, hsem, pf, pfr = nxt
            if t + 1 < NT:
                nxt = load(t + 1)
            pb, pbr, _ = pbs.next()
            P.op("pool", lambda e, pb=pb, pf=pf: e.tensor_copy(out=pb[:], in_=pf[:]), reads=[pfr], writes=[pbr])
            norm_tile(hs, hr, l * 3 + 2, lambda c: eT[:, c, :], re_, tmp)
            for oc in range(KC):
                ps, pr = next_ps()

                def fn(e, ps=ps, oc=oc):
                    ins = None
                    for kc in range(KC):
                        ins = e.matmul(ps[:], lhsT=wg[:, kc, oc * 128:(oc + 1) * 128], rhs=eT[:, kc, :],
                                       start=(kc == 0), stop=(kc == KC - 1))
                    return ins
                P.op("pe", fn, reads=[rw, re_], writes=[pr])
                ps2, pr2 = next_ps()

                def fn2(e, ps2=ps2, oc=oc, pb=pb):
                    ins = None
                    for kc in range(2):
                        ins = e.matmul(ps2[:], lhsT=wp[:, kc, oc * 128:(oc + 1) * 128], rhs=pb[:, kc, :],
                                       start=(kc == 0), stop=(kc == 1))
                    return ins
                P.op("pe", fn2, reads=[rw, pbr], writes=[pr2])
                sg, sgr, _ = sgs.next()
                tt, ttr, _ = tts.next()
                P.op("act", lambda e, sg=sg, ps=ps: e.activation(out=sg[:], in_=ps[:], func=AF.Sigmoid), reads=[pr], writes=[sgr])
                P.op("dve", lambda e, tt=tt, ps2=ps2, sg=sg: e.tensor_tensor(out=tt[:], in0=ps2[:], in1=sg[:], op=ALU.mult),
                     reads=[pr2, sgr], writes=[ttr])
                P.op("dve", lambda e, tt=tt, hs=hs, oc=oc: e.tensor_tensor(out=hs[:, oc, :], in0=tt[:], in1=hs[:, oc, :],
                                                                           op=ALU.add), reads=[ttr, hr], writes=[hr])
            if fuse != "final":
                P.dma(hview(hT, t), hs[:], hsem, reads=[hr], writes=[P.R("hT", t)])
            if fuse is not None:
                ot, otr, otsem = osl.next()
                if fuse == "next":
                    norm_tile(hs, hr, (l + 1) * 3 + 0, lambda c, ot=ot: ot[:, c, :], otr, tmp)
                    P.dma(aTd.rearrange("c p s -> p c s")[:, :, t * T:(t + 1) * T], ot[:], otsem, reads=[otr],
                          writes=[P.R("aTd", t)])
                else:
                    norm_tile(hs, hr, NL * 3, lambda c, ot=ot: ot[:, c, :], otr, tmp)
                    P.dma(hview(outT, t), ot[:], otsem, reads=[otr], writes=[P.R("outT", t)])
        P.barrier()
        A.reset(m)

    setup()
    wrote_h = False
    fused_final = False
    for l in range(nl):
        m = A.mark()
        aT = A.alloc([128, KC, S], BF16, "aT")
        aR = [P.R("aT", l, t) for t in range(NT)]
        if l == 0:
            norm_phase(xT, 0, aT, aR)
        else:
            load_act(aTd, aT, aR, "aTd")
        phase_qkv(l, aT, aR)
        phase_conv_gates(l, aT, aR)
        phase_sgu(l, aT, aR)
        A.reset(m)
        wts = alloc_merge_w(l)
        phase_attn(l)
        phase_merge(l, xT if l == 0 else hT, wts)
        wrote_h = True
        bg_flush()
        A.reset(m)
        wd = A.alloc([128, 32, D], BF16, "wd")
        rwd = P.R("wd", l)
        bg_queue(w_down[l], wd, rwd)
        m1 = A.mark()
        cT = A.alloc([128, KC, S], BF16, "cT")
        cR = [P.R("cT", l, t) for t in range(NT)]
        load_act(cTd, cT, cR, "cTd")
        phase_ffn_up(l, cT, cR)
        bg_flush()
        A.reset(m1)
        wg = A.alloc([128, 8, D], BF16, "wg")
        wp = A.alloc([128, 2, D], BF16, "wp")
        rwp = P.R("wple", l)
        bg_queue(w_pg[l], wg, rwp)
        bg_queue(w_pe[l], wp, rwp)
        phase_ffn_down(l, wd, rwd)
        last = (l == nl - 1)
        fuse = ("final" if (final_norm and nl == NL) else None) if last else "next"
        fused_final = fused_final or fuse == "final"
        phase_ple(l, wg, wp, rwp, m, fuse)
        bg_flush()
        A.reset(m)
    if not fused_final:
        final_phase(hT if wrote_h else xT, NL * 3)
    P.emit()
    return nc, P


def _consts():
    ident = np.eye(128, dtype=np.float32)
    rot = np.zeros((128, 128), np.float32)
    for base in (0, 64):
        for i in range(8):
            rot[base + i, base + 8 + i] = -1.0
            rot[base + 8 + i, base + i] = 1.0
    rotT = rot.T.copy()
    ones = np.ones((128, 128), np.float32)
    s = np.arange(128)[:, None]
    t = np.arange(128)[None, :]
    tril = (s <= t).astype(np.float32)
    cmat = np.stack([ident, rotT, ones, tril], axis=1).astype(np.float32)
    k = np.arange(128)[:, None]
    q = np.arange(128)[None, :]
    prev = np.where(k >= q, 0.0, MASKV).astype(np.float32)
    cur = np.where(k <= q, 0.0, MASKV).astype(np.float32)
    cmask = np.stack([np.tile(prev, (1, 4)), np.tile(cur, (1, 4))], axis=1).astype(np.float32)
    inv_freq = (np.float32(500000.0) ** (-(np.arange(0, 16, 2, dtype=np.float32) / np.float32(16)))).astype(np.float32)
    invf = np.zeros((128, 1), np.float32)
    for p in range(128):
        if p % 64 < 16:
            invf[p, 0] = inv_freq[(p % 64) % 8]
    return cmat, cmask, invf


_CACHE = {}


def _prep_shared(inp):
    f = lambda a: np.ascontiguousarray(np.asarray(a, dtype=np.float32))
    cmat, cmask, invf = _consts()
    gl = []
    for l in range(NL):
        for nm in ("norm_mix_g", "norm_mlp_g", "norm_ple_g"):
            gl.append(np.asarray(inp[nm][l], np.float32).reshape(KC, 128).T)
    gl.append(np.asarray(inp["norm_final_g"], np.float32).reshape(KC, 128).T)
    gains = np.ascontiguousarray(np.stack(gl, axis=1))
    cw = np.asarray(inp["conv_w"], np.float32)
    convw = np.ascontiguousarray(cw.reshape(NL, 3, 4, 128).transpose(3, 0, 2, 1))
    sgwT = np.ascontiguousarray(np.asarray(inp["sg_w"], np.float32).transpose(0, 1, 3, 2))
    shared = {
        "w_in": f(inp["w_in"]), "w_a": f(inp["w_branch_a"]), "w_b": f(inp["w_branch_b"]),
        "w_c": f(inp["w_branch_c"]), "w_out": f(inp["w_out"]), "w_up": f(inp["w_up"]),
        "w_down": f(inp["w_down"]), "w_pg": f(inp["w_ple_gate"]), "w_pe": f(inp["w_ple_proj"]),
        "sgwT": sgwT, "gains": gains, "convw": convw,
        "lng": f(inp["sg_ln_g"]).reshape(NL, 1, 512), "lnb": f(inp["sg_ln_b"]).reshape(NL, 1, 512),
        "sgb": f(inp["sg_b"]).reshape(NL, 1, 512),
        "cmat": cmat, "cmask": cmask, "invf": invf,
    }
    return shared


def run(inp, nl=NL, final_norm=True, cores=8, trace=False, stop=99):
    key = (nl, final_norm, stop)
    if key not in _CACHE:
        _CACHE[key] = build_program(nl, final_norm, stop)[0]
    nc = _CACHE[key]
    shared = _prep_shared(inp)
    x = np.asarray(inp["x"], np.float32)
    p = np.asarray(inp["p"], np.float32)
    posn = np.asarray(inp["positions"], np.int32)
    in_maps = []
    for b in range(cores):
        mp = dict(shared)
        mp["xT"] = np.ascontiguousarray(x[b].T)
        mp["pT"] = np.ascontiguousarray(p[:, b].transpose(0, 2, 1))
        mp["pos"] = np.ascontiguousarray(posn[b].reshape(1, S))
        in_maps.append(mp)
    res = run_bass_kernel_spmd(nc, in_maps, core_ids=list(range(cores)), **({"trace": True} if trace else {}))
    out = np.stack([np.ascontiguousarray(r["outT"].T) for r in res.results], axis=0)
    return out.astype(np.float32), res


def kernel(**inputs):
    out, _ = run(inputs)
    return out
```

```python
import math
from contextlib import ExitStack

import numpy as np
import concourse.bass as bass
import concourse.mybir as mybir
from concourse.bass_utils import run_bass_kernel_spmd

F32 = mybir.dt.float32
BF16 = mybir.dt.bfloat16
I32 = mybir.dt.int32
AF = mybir.ActivationFunctionType
ALU = mybir.AluOpType
AX = mybir.AxisListType

S = 4096
D = 1024
T = 512
NT = S // T
KC = D // 128
NL = 4
INW = 7936
DFF = 4096
PLE = 256
GROUPS = ((128, 1), (512, 4), (2048, 16))
RMS_EPS = 1e-6
LN_EPS = 1e-5
MASKV = -240000.0
C_Q, C_K, C_V = 0, 768, 1536
C_X, C_B, C_C = 2304, 2816, 3328
C_SU, C_SV = 3840, 4352
C_G = 4864

ENGS = ("pe", "act", "dve", "pool", "sp")
DBG = 99


class Res:
    __slots__ = ("writers", "readers", "excl")

    def __init__(self):
        self.writers = {}
        self.readers = {}
        self.excl = False


class Prog:
    def __init__(self, nc):
        self.nc = nc
        self.streams = {e: [] for e in ENGS}
        self.count = {e: 0 for e in ENGS}
        self.seen = {e: {} for e in ENGS}
        self.semh = {}
        self.res = {}
        self.ndma = 0

    def R(self, *key):
        r = self.res.get(key)
        if r is None:
            r = self.res[key] = Res()
        return r

    def newsem(self, key):
        self.count[key] = 0
        return key

    def _deps(self, eng, reads, writes):
        deps = {}
        for r in reads:
            for k, v in r.writers.items():
                if deps.get(k, 0) < v:
                    deps[k] = v
            if r.excl:
                for k, v in r.readers.items():
                    if k != eng and deps.get(k, 0) < v:
                        deps[k] = v
        for w in writes:
            for dct in (w.writers, w.readers):
                for k, v in dct.items():
                    if k != eng and deps.get(k, 0) < v:
                        deps[k] = v
        if eng == "pe":
            deps.pop("pe", None)
        seen = self.seen[eng]
        st = self.streams[eng]
        for k, v in deps.items():
            if seen.get(k, 0) < v:
                st.append(("w", k, v))
                seen[k] = v

    def op(self, eng, fn, reads=(), writes=()):
        self._deps(eng, reads, writes)
        self.count[eng] += 1
        v = self.count[eng]
        self.streams[eng].append(("o", fn, eng, 1))
        for r in reads:
            r.readers[eng] = v
        for w in writes:
            w.writers = {eng: v}
            w.readers = {}

    def dma(self, out, in_, sem, reads=(), writes=(), q="sp"):
        self._deps(q, reads, writes)
        self.count[sem] += 16
        v = self.count[sem]
        self.streams[q].append(("o", lambda e, o=out, i=in_: e.dma_start(out=o, in_=i), sem, 16))
        for r in reads:
            r.readers[sem] = v
        for w in writes:
            w.writers = {sem: v}
            w.readers = {}
        self.ndma += 1

    def barrier(self):
        tot = dict(self.count)
        for e in ENGS:
            seen = self.seen[e]
            for k, v in tot.items():
                if k == e and e == "pe":
                    continue
                if v > 0 and seen.get(k, 0) < v:
                    self.streams[e].append(("w", k, v))
                    seen[k] = v
        for r in self.res.values():
            r.writers = {}
            r.readers = {}

    def emit(self):
        nc = self.nc
        with ExitStack() as es:
            for k in self.count:
                self.semh[k] = es.enter_context(nc.semaphore("s_" + str(k)))
            block = es.enter_context(nc.Block())

            def replay(e, name):
                semh = self.semh
                for it in self.streams[name]:
                    if it[0] == "w":
                        e.wait_ge(semh[it[1]], it[2])
                    else:
                        ins = it[1](e)
                        ins.then_inc(semh[it[2]], it[3])

            @block.tensor
            def _(e):
                replay(e, "pe")

            @block.scalar
            def _(e):
                replay(e, "act")

            @block.vector
            def _(e):
                replay(e, "dve")

            @block.gpsimd
            def _(e):
                replay(e, "pool")

            @block.sync
            def _(e):
                replay(e, "sp")


class Arena:
    def __init__(self, nc, limit=229000):
        self.nc = nc
        self.off = 16640
        self.n = 0
        self.limit = limit

    def alloc(self, shape, dtype, name="t"):
        esz = 4 if dtype in (F32, I32) else 2
        nbytes = esz * int(np.prod(shape[1:]))
        nbytes = (nbytes + 63) // 64 * 64
        off = self.off
        self.off += nbytes
        assert self.off <= self.limit, ("SBUF arena overflow", name, self.off)
        self.n += 1
        return self.nc.alloc_sbuf_tensor_at("%s_%d" % (name, self.n), list(shape), dtype, offset=off)

    def mark(self):
        return self.off

    def reset(self, m):
        self.off = m


def build_program(nl=NL, final_norm=True, stop=99):
    nc = bass.Bass("TRN2", target_bir_lowering=False)
    P = Prog(nc)
    A = Arena(nc)

    def din(name, shape, dt=F32):
        return nc.dram_tensor(name, list(shape), dt, kind="ExternalInput").ap()

    def dscr(name, shape, dt):
        return nc.dram_tensor(name, list(shape), dt, kind="Internal").ap()

    xT = din("xT", [D, S])
    pT = din("pT", [NL, PLE, S])
    pos = din("pos", [1, S], I32)
    w_in = din("w_in", [NL, D, INW])
    w_a = din("w_a", [NL, 768, D])
    w_b = din("w_b", [NL, 512, D])
    w_c = din("w_c", [NL, 512, D])
    w_out = din("w_out", [NL, D, D])
    w_up = din("w_up", [NL, D, DFF])
    w_down = din("w_down", [NL, DFF, D])
    w_pg = din("w_pg", [NL, D, D])
    w_pe = din("w_pe", [NL, PLE, D])
    sgwT = din("sgwT", [NL, 4, 128, 128])
    gains = din("gains", [128, NL * 3 + 1, KC])
    convw = din("convw", [128, NL, 4, 3])
    lng = din("lng", [NL, 1, 512])
    lnb = din("lnb", [NL, 1, 512])
    sgb = din("sgb", [NL, 1, 512])
    cmat = din("cmat", [128, 4, 128])
    cmask = din("cmask", [128, 2, 512])
    invf = din("invf", [128, 1])
    outT = nc.dram_tensor("outT", [D, S], F32, kind="ExternalOutput").ap()

    hT = dscr("hT", [D, S], F32)
    cosD = dscr("cosD", [128, S], F32)
    sinD = dscr("sinD", [128, S], F32)
    qkT = dscr("qkT", [12, 128, S], BF16)
    Vd = dscr("Vd", [3, 128, 32 * 256], BF16)
    ybT = dscr("ybT", [4, 128, S], BF16)
    ycT = dscr("ycT", [4, 128, S], BF16)
    gT = dscr("gT", [24, 128, S], BF16)
    yaT = dscr("yaT", [6, 128, S], BF16)
    fT = dscr("fT", [32, 128, S], BF16)
    cTd = dscr("cTd", [KC, 128, S], BF16)
    aTd = dscr("aTd", [KC, 128, S], BF16)

    PS = [nc.alloc_psum_tensor("ps%d" % i, [128, 512], F32) for i in range(8)]
    PSR = [P.R("ps", i) for i in range(8)]
    for r_ in PSR:
        r_.excl = True

    cm = A.alloc([128, 4, 128], BF16, "cm")
    ident, rotT, onesb, trilb = (cm[:, i, :] for i in range(4))
    cmf = A.alloc([128, 4, 128], F32, "cmf")
    onesf = cmf[:, 2, :]
    maskb = A.alloc([128, 2, 512], BF16, "maskb")
    gsb = A.alloc([128, NL * 3 + 1, KC], F32, "gsb")
    cwsb = A.alloc([128, NL, 4, 3], F32, "cwsb")
    invfsb = A.alloc([128, 1], F32, "invf")
    epsb = A.alloc([128, 1], F32, "epsb")
    Rconst = P.R("const")
    base_mark = A.mark()

    cnt = {"sem": 0}

    def dsem():
        cnt["sem"] += 1
        key = "d%d" % cnt["sem"]
        if key not in P.count:
            P.newsem(key)
        return key

    _pbar = P.barrier

    def barrier():
        _pbar()
        cnt["sem"] = 0

    P.barrier = barrier

    class Slots:
        def __init__(self, n, shape, dtype, name):
            self.t = [A.alloc(shape, dtype, name) for _ in range(n)]
            self.r = [P.R(name, id(self), i) for i in range(n)]
            self.s = [dsem() for _ in range(n)]
            self.i = -1
            self.n = n

        def next(self):
            self.i = (self.i + 1) % self.n
            return self.t[self.i], self.r[self.i], self.s[self.i]

    psrot = {}

    def next_ps(lo=0, hi=8):
        i = psrot.get((lo, hi), lo)
        psrot[(lo, hi)] = lo + (i + 1 - lo) % (hi - lo)
        return PS[i], PSR[i]

    def setup():
        m = A.mark()
        s0 = dsem()
        mf = A.alloc([128, 2, 512], F32, "mf")
        P.dma(cmf[:], cmat, s0, writes=[Rconst])
        P.dma(mf[:], cmask, s0, writes=[Rconst])
        P.dma(gsb[:], gains, s0, writes=[Rconst])
        P.dma(cwsb[:], convw, s0, writes=[Rconst])
        P.dma(invfsb[:], invf, s0, writes=[Rconst])
        P.op("dve", lambda e: e.memset(epsb[:], RMS_EPS), reads=[Rconst], writes=[Rconst])
        P.op("dve", lambda e: e.tensor_copy(out=cm[:], in_=cmf[:]), reads=[Rconst], writes=[Rconst])
        P.op("dve", lambda e: e.tensor_copy(out=maskb[:], in_=mf[:]), reads=[Rconst], writes=[Rconst])
        posi = A.alloc([128, S], I32, "posi")
        ang = A.alloc([128, S], F32, "ang")
        t1 = A.alloc([128, S], F32, "t1")
        ki = A.alloc([128, S], I32, "ki")
        kf = A.alloc([128, S], F32, "kf")
        r_ang, r_t1, r_ki, r_kf = (P.R("su", i) for i in range(4))
        P.dma(posi[:], pos.broadcast_to([128, S]), dsem(), writes=[r_ki])
        s_st = dsem()
        P.op("dve", lambda e: e.tensor_copy(out=t1[:], in_=posi[:]), reads=[r_ki], writes=[r_t1])
        P.op("dve", lambda e: e.tensor_scalar(out=ang[:], in0=t1[:], scalar1=invfsb[:, 0:1], scalar2=None,
                                              op0=ALU.mult), reads=[r_t1, Rconst], writes=[r_ang])
        C1 = 6.28125
        C2 = 2.0 * math.pi - 6.28125
        i2p = 1.0 / (2.0 * math.pi)
        for which, (shiftk, shifta, dst) in enumerate(((0.0, 0.0, sinD), (0.25, 0.5 * math.pi, cosD))):
            P.op("dve", lambda e, sk=shiftk: e.tensor_scalar(out=t1[:], in0=ang[:], scalar1=i2p, scalar2=sk,
                                                             op0=ALU.mult, op1=ALU.add),
                 reads=[r_ang], writes=[r_t1])
            P.op("dve", lambda e: e.tensor_copy(out=ki[:], in_=t1[:]), reads=[r_t1], writes=[r_ki])
            P.op("dve", lambda e: e.tensor_copy(out=kf[:], in_=ki[:]), reads=[r_ki], writes=[r_kf])
            P.op("dve", lambda e: e.scalar_tensor_tensor(out=t1[:], in0=kf[:], scalar=-C1, in1=ang[:],
                                                         op0=ALU.mult, op1=ALU.add),
                 reads=[r_kf, r_ang], writes=[r_t1])
            P.op("dve", lambda e: e.scalar_tensor_tensor(out=t1[:], in0=kf[:], scalar=-C2, in1=t1[:],
                                                         op0=ALU.mult, op1=ALU.add),
                 reads=[r_kf, r_t1], writes=[r_t1])
            P.op("dve", lambda e, sa=shifta: e.tensor_scalar(out=t1[:], in0=t1[:], scalar1=sa, scalar2=3.1415925,
                                                             op0=ALU.add, op1=ALU.min),
                 reads=[r_t1], writes=[r_t1])
            P.op("dve", lambda e: e.tensor_scalar(out=t1[:], in0=t1[:], scalar1=-3.1415925, scalar2=None,
                                                  op0=ALU.max), reads=[r_t1], writes=[r_t1])
            P.op("act", lambda e: e.activation(out=kf[:], in_=t1[:], func=AF.Sin), reads=[r_t1], writes=[r_kf])
            P.dma(dst, kf[:], s_st, reads=[r_kf], writes=[P.R("trig", which)])
        P.barrier()
        A.reset(m)

    class WLoader:
        def __init__(self, kcmax=8, ncmax=256):
            self.stg = Slots(2, [128, kcmax, ncmax], F32, "wstg")
            self.kcmax = kcmax
            self.ncmax = ncmax
            self.tog = 0

        def issue(self, wsrc, c0, ncols, dst, dres, kc0=0, kcn=None):
            K = wsrc.shape[0]
            if kcn is None:
                kcn = K // 128
            src3 = wsrc.rearrange("(kc p) n -> p kc n", p=128)
            tok = []
            for k0 in range(kc0, kc0 + kcn, self.kcmax):
                kn = min(self.kcmax, kc0 + kcn - k0)
                for cc in range(0, ncols, self.ncmax):
                    cn = min(self.ncmax, ncols - cc)
                    st, sr, ss = self.stg.next()
                    P.dma(st[:, 0:kn, 0:cn], src3[:, k0:k0 + kn, c0 + cc:c0 + cc + cn], ss, writes=[sr])
                    tok.append((st[:, 0:kn, 0:cn], sr, dst[:, k0:k0 + kn, cc:cc + cn], dres))
            return tok

        def convert(self, tok):
            for (src, sr, dst, dres) in tok:
                self.tog ^= 1
                if self.tog:
                    P.op("act", lambda e, o=dst, i=src: e.activation(out=o, in_=i, func=AF.Copy), reads=[sr], writes=[dres])
                else:
                    P.op("dve", lambda e, o=dst, i=src: e.tensor_copy(out=o, in_=i), reads=[sr], writes=[dres])

        def load(self, wsrc, c0, ncols, dst, dres, kc0=0, kcn=None):
            K = wsrc.shape[0]
            if kcn is None:
                kcn = K // 128
            for k0 in range(kc0, kc0 + kcn, self.kcmax):
                kn = min(self.kcmax, kc0 + kcn - k0)
                for cc in range(0, ncols, self.ncmax):
                    cn = min(self.ncmax, ncols - cc)
                    tok = self.issue(wsrc, c0 + cc, cn, dst, dres, k0, kn)
                    tok = [(a, b, dst[:, k0:k0 + kn, cc:cc + cn], d) for (a, b, _, d) in tok]
                    self.convert(tok)

    class WPipe:
        def __init__(self, wl, wsl, W, blocks):
            self.wl, self.wsl, self.W, self.blocks = wl, wsl, W, blocks
            self.cur = None
            self.nxt = None
            self.tok = None

        def _issue(self, i):
            c0, ncols = self.blocks[i]
            wt, wr, _ = self.wsl.next()
            tok = self.wl.issue(self.W, c0, ncols, wt, wr)
            return (wt, wr), tok

        def begin(self, i):
            if i == 0:
                self.cur, tok = self._issue(0)
                self.wl.convert(tok)
            else:
                assert self.nxt is not None
                self.cur = self.nxt
            self.nxt = None
            if i + 1 < len(self.blocks):
                self.nxt, self.tok = self._issue(i + 1)
            return self.cur

        def mid(self):
            if self.tok is not None:
                self.wl.convert(self.tok)
                self.tok = None

    bg = {"q": [], "tog": 0}
    bg_stg = [A.alloc([128, 8, 256], F32, "bgstg") for _ in range(2)]
    bg_res = [P.R("bgstg", i) for i in range(2)]
    bg_sem = [P.newsem("bg%d" % i) for i in range(2)]
    bg_i = {"i": 0}

    def bg_queue(wsrc, dst, dres):
        K, N = wsrc.shape
        src3 = wsrc.rearrange("(kc p) n -> p kc n", p=128)
        for k0 in range(0, K // 128, 8):
            kn = min(8, K // 128 - k0)
            for cc in range(0, N, 256):
                cn = min(256, N - cc)
                bg["q"].append((src3[:, k0:k0 + kn, cc:cc + cn], dst[:, k0:k0 + kn, cc:cc + cn], dres, kn, cn))

    def bg_pump(n=1):
        for _ in range(n):
            if not bg["q"]:
                return
            src, dst, dres, kn, cn = bg["q"].pop(0)
            i = bg_i["i"]
            bg_i["i"] ^= 1
            st = bg_stg[i]
            P.dma(st[:, 0:kn, 0:cn], src, bg_sem[i], writes=[bg_res[i]])
            bg["tog"] ^= 1
            if bg["tog"]:
                P.op("act", lambda e, o=dst, i_=st[:, 0:kn, 0:cn]: e.activation(out=o, in_=i_, func=AF.Copy),
                     reads=[bg_res[i]], writes=[dres])
            else:
                P.op("dve", lambda e, o=dst, i_=st[:, 0:kn, 0:cn]: e.tensor_copy(out=o, in_=i_),
                     reads=[bg_res[i]], writes=[dres])

    def bg_flush():
        bg_pump(len(bg["q"]))

    def norm_square(hs, hr, tmp):
        sq, rs = tmp["sq"], tmp["res"]
        P.op("act", lambda e: e.activation(out=sq[:], in_=hs[:], func=AF.Square), reads=[hr], writes=[rs[0]])

    def norm_rstd(tmp):
        sq, rstd, rs = tmp["sq"], tmp["rstd"], tmp["res"]
        ps, pr = next_ps()

        def fnn(e, ps=ps):
            ins = None
            for c in range(KC):
                ins = e.matmul(ps[:], lhsT=onesb, rhs=sq[:, c, :], start=(c == 0), stop=(c == KC - 1))
            return ins
        P.op("pe", fnn, reads=[rs[0], Rconst], writes=[pr])
        P.op("act", lambda e, ps=ps: e.activation(out=rstd[:], in_=ps[:], func=AF.Ln, bias=epsb[:, 0:1], scale=1.0 / D),
             reads=[pr, Rconst], writes=[rs[2]])
        P.op("act", lambda e: e.activation(out=rstd[:], in_=rstd[:], func=AF.Exp, scale=-0.5), reads=[rs[2]], writes=[rs[2]])

    def norm_apply(hs, hr, gidx, out_ap_fn, out_res, tmp):
        rstd, rs = tmp["rstd"], tmp["res"]
        for c in range(KC):
            P.op("dve", lambda e, c=c: e.scalar_tensor_tensor(out=out_ap_fn(c), in0=hs[:, c, :],
                                                            scalar=gsb[:, gidx, c:c + 1], in1=rstd[:],
                                                            op0=ALU.mult, op1=ALU.mult),
                 reads=[hr, rs[2], Rconst], writes=[out_res])

    def norm_tile(hs, hr, gidx, out_ap_fn, out_res, tmp, wr_extra=()):
        norm_square(hs, hr, tmp)
        norm_rstd(tmp)
        norm_apply(hs, hr, gidx, out_ap_fn, out_res, tmp)

    def norm_tmp():
        return {"sq": A.alloc([128, KC, T], BF16, "sq"),
                "rstd": A.alloc([128, T], F32, "rstd"), "res": [P.R("nt", A.n, i) for i in range(3)]}

    def hview(src, t):
        return src.rearrange("(c p) s -> p c s", p=128)[:, :, t * T:(t + 1) * T]

    def norm_phase(src, gidx, actT, act_res):
        m = A.mark()
        hsl = Slots(2, [128, KC, T], F32, "hs")
        tmp = norm_tmp()
        for t in range(NT):
            hs, hr, hsem = hsl.next()
            P.dma(hs[:], hview(src, t), hsem, reads=[P.R("hT", t)], writes=[hr])
            norm_tile(hs, hr, gidx, lambda c, t=t: actT[:, c, t * T:(t + 1) * T], act_res[t], tmp)
        P.barrier()
        A.reset(m)

    def load_act(dsrc, actT, act_res, key):
        for t in range(NT):
            sl = slice(t * T, (t + 1) * T)
            P.dma(actT[:, :, sl], dsrc.rearrange("c p s -> p c s")[:, :, sl], dsem(), reads=[P.R(key, t)], writes=[act_res[t]])

    def final_phase(src, gidx):
        m = A.mark()
        hsl = Slots(2, [128, KC, T], F32, "hs")
        osl = Slots(2, [128, KC, T], F32, "os")
        tmp = norm_tmp()
        for t in range(NT):
            hs, hr, hsem = hsl.next()
            ot, orr, osem = osl.next()
            P.dma(hs[:], hview(src, t), hsem, reads=[P.R("hT", t)], writes=[hr])
            if final_norm:
                norm_tile(hs, hr, gidx, lambda c, ot=ot: ot[:, c, :], orr, tmp)
                P.dma(hview(outT, t), ot[:], osem, reads=[orr], writes=[P.R("outT", t)])
            else:
                P.dma(hview(outT, t), hs[:], hsem, reads=[hr], writes=[P.R("outT", t)])
        P.barrier()
        A.reset(m)

    def mm_group(ps, pr, wt, wres, kcn, col0, actT, act_res_t, t, ncols=128):
        def fn(e):
            ins = None
            for kc in range(kcn):
                ins = e.matmul(ps[:ncols, :], lhsT=wt[:, kc, col0:col0 + ncols], rhs=actT[:, kc, t * T:(t + 1) * T],
                               start=(kc == 0), stop=(kc == kcn - 1))
            return ins
        P.op("pe", fn, reads=[wres, act_res_t], writes=[pr])

    def phase_qkv(l, aT, aR, sub=9):
        m = A.mark()
        cs = A.alloc([128, S], F32, "cos")
        sn = A.alloc([128, S], F32, "sin")
        rtr = P.R("trigsb")
        s0 = dsem()
        P.dma(cs[:], cosD, s0, reads=[P.R("trig", 1)], writes=[rtr])
        P.dma(sn[:], sinD, s0, reads=[P.R("trig", 0)], writes=[rtr])
        wl = WLoader()
        wsl = Slots(2, [128, KC, 256], BF16, "wqk")
        zq = Slots(2, [128, T], BF16, "zq")
        t1s = Slots(2, [128, T], F32, "t1")
        t2s = Slots(2, [128, T], F32, "t2")
        rows = Slots(2, [128, S], BF16, "qrow")
        W = w_in[l]
        wpipe = WPipe(wl, wsl, W, [(blk * 256, 256) for blk in range(6)] + [(C_V + g * 256, 256) for g in range(3)])
        for blk in range(6 if sub >= 1 else 0):
            wt, wr = wpipe.begin(blk)
            for ci in range(2):
                if ci == 1:
                    wpipe.mid()
                oc = blk * 2 + ci
                row, rr, rsem = rows.next()
                for t in range(NT):
                    if DBG < 2:
                        continue
                    ps, pr = next_ps()
                    mm_group(ps, pr, wt, wr, KC, ci * 128, aT, aR[t], t)
                    if DBG < 3:
                        continue
                    z, zr, _ = zq.next()
                    P.op("act", lambda e, z=z, ps=ps: e.activation(out=z[:], in_=ps[:], func=AF.Copy),
                         reads=[pr], writes=[zr])
                    ps2, pr2 = next_ps()
                    P.op("pe", lambda e, ps2=ps2, z=z: e.matmul(ps2[:], lhsT=rotT, rhs=z[:], start=True, stop=True),
                         reads=[zr, Rconst], writes=[pr2])
                    if DBG < 4:
                        continue
                    a1, a1r, _ = t1s.next()
                    a2, a2r, _ = t2s.next()
                    P.op("dve", lambda e, a1=a1, ps=ps, t=t: e.tensor_tensor(out=a1[:], in0=ps[:],
                                                                              in1=cs[:, t * T:(t + 1) * T], op=ALU.mult),
                         reads=[pr, rtr], writes=[a1r])
                    P.op("dve", lambda e, a2=a2, ps2=ps2, t=t: e.tensor_tensor(out=a2[:], in0=ps2[:],
                                                                                in1=sn[:, t * T:(t + 1) * T], op=ALU.mult),
                         reads=[pr2, rtr], writes=[a2r])
                    if DBG < 5:
                        continue
                    P.op("dve", lambda e, a1=a1, a2=a2, row=row, t=t: e.tensor_tensor(
                        out=row[:, t * T:(t + 1) * T], in0=a1[:], in1=a2[:], op=ALU.add),
                        reads=[a1r, a2r], writes=[rr])
                if DBG >= 6:
                    P.dma(qkT[oc], row[:], rsem, reads=[rr], writes=[P.R("qkT", oc)])
        vsl = Slots(2, [128, 32, 256], BF16, "vsb")
        for g, (win, dil) in enumerate(GROUPS if sub >= 2 else ()):
            wt, wr = wpipe.begin(6 + g)
            vt, vr, vsem = vsl.next()
            nb = S // (dil * 128)
            for r in range(dil):
                for n in range(nb):
                    bi = r * nb + n
                    if bi == 16:
                        wpipe.mid()
                    t0 = r + dil * 128 * n
                    ps, pr = next_ps()

                    def fn(e, ps=ps, t0=t0, dil=dil, wt=wt):
                        ins = None
                        for kc in range(KC):
                            ins = e.matmul(ps[:, 0:256], lhsT=aT[:, kc, t0:t0 + dil * 127 + 1:dil], rhs=wt[:, kc, :],
                                           start=(kc == 0), stop=(kc == KC - 1))
                        return ins
                    P.op("pe", fn, reads=[wr] + aR, writes=[pr])
                    P.op("act", lambda e, ps=ps, vt=vt, bi=bi: e.activation(out=vt[:, bi, :], in_=ps[:, 0:256],
                                                                          func=AF.Copy), reads=[pr], writes=[vr])
            P.dma(Vd[g], vt[:].rearrange("p b c -> p (b c)"), vsem, reads=[vr], writes=[P.R("Vd", g)])
        P.barrier()
        A.reset(m)

    def phase_conv_gates(l, aT, aR):
        m = A.mark()
        wl = WLoader()
        wsl = Slots(2, [128, KC, 256], BF16, "wcg")
        W = w_in[l]
        zxs = A.alloc([128, S], F32, "zxs")
        ub = A.alloc([128, S + 2], F32, "ub")
        rzx, rub = P.R("zxs"), P.R("ub")
        rows = Slots(2, [128, S], BF16, "ybrow")
        P.op("pool", lambda e: e.memset(ub[:, 0:2], 0.0), writes=[rub])
        blocks = []
        for c in range(4):
            blocks += [(C_X + c * 128, 128), (C_C + c * 128, 128), (C_B + c * 128, 128)]
        blocks += [(C_G + blk * 256, 256) for blk in range(12)]
        wpipe = WPipe(wl, wsl, W, blocks)
        for c in range(4):
            wt, wr = wpipe.begin(3 * c)
            for t in range(NT):
                if t == 4:
                    wpipe.mid()
                ps, pr = next_ps()
                mm_group(ps, pr, wt, wr, KC, 0, aT, aR[t], t)
                P.op("act", lambda e, ps=ps, t=t: e.activation(out=zxs[:, t * T:(t + 1) * T], in_=ps[:], func=AF.Copy),
                     reads=[pr], writes=[rzx])
            wt, wr = wpipe.begin(3 * c + 1)
            for t in range(NT):
                if t == 4:
                    wpipe.mid()
                ps, pr = next_ps()
                mm_group(ps, pr, wt, wr, KC, 0, aT, aR[t], t)
                P.op("dve", lambda e, ps=ps, t=t: e.tensor_tensor(out=ub[:, 2 + t * T:2 + (t + 1) * T], in0=ps[:],
                                                                  in1=zxs[:, t * T:(t + 1) * T], op=ALU.mult),
                     reads=[pr, rzx], writes=[rub])
            wt, wr = wpipe.begin(3 * c + 2)
            P.op("pool", lambda e, c=c: e.tensor_scalar(out=zxs[:], in0=ub[:, 0:S], scalar1=cwsb[:, l, c, 0:1],
                                                        scalar2=None, op0=ALU.mult),
                 reads=[rub, Rconst], writes=[rzx])
            P.op("dve", lambda e, c=c: e.scalar_tensor_tensor(out=zxs[:], in0=ub[:, 1:S + 1], scalar=cwsb[:, l, c, 1:2],
                                                              in1=zxs[:], op0=ALU.mult, op1=ALU.add),
                 reads=[rub, rzx, Rconst], writes=[rzx])
            P.op("dve", lambda e, c=c: e.scalar_tensor_tensor(out=zxs[:], in0=ub[:, 2:S + 2], scalar=cwsb[:, l, c, 2:3],
                                                              in1=zxs[:], op0=ALU.mult, op1=ALU.add),
                 reads=[rub, rzx, Rconst], writes=[rzx])
            row, rr, rsem = rows.next()
            for t in range(NT):
                if t == 4:
                    wpipe.mid()
                ps, pr = next_ps()
                mm_group(ps, pr, wt, wr, KC, 0, aT, aR[t], t)
                P.op("dve", lambda e, ps=ps, t=t, row=row: e.tensor_tensor(out=row[:, t * T:(t + 1) * T], in0=ps[:],
                                                                           in1=zxs[:, t * T:(t + 1) * T], op=ALU.mult),
                     reads=[pr, rzx], writes=[rr])
            P.dma(ybT[c], row[:], rsem, reads=[rr], writes=[P.R("ybT", c)])
        for blk in range(12):
            wt, wr = wpipe.begin(12 + blk)
            for ci in range(2):
                if ci == 1:
                    wpipe.mid()
                oc = blk * 2 + ci
                row, rr, rsem = rows.next()
                for t in range(NT):
                    ps, pr = next_ps()
                    mm_group(ps, pr, wt, wr, KC, ci * 128, aT, aR[t], t)
                    P.op("act", lambda e, ps=ps, t=t, row=row: e.activation(out=row[:, t * T:(t + 1) * T], in_=ps[:],
                                                                            func=AF.Sigmoid), reads=[pr], writes=[rr])
                P.dma(gT[oc], row[:], rsem, reads=[rr], writes=[P.R("gT", oc)])
        P.barrier()
        A.reset(m)

    def phase_sgu(l, aT, aR):
        m = A.mark()
        wl = WLoader()
        W = w_in[l]
        uT = A.alloc([128, 4, S], BF16, "uT")
        ruT = [P.R("uT", b) for b in range(32)]
        wsl = Slots(2, [128, KC, 256], BF16, "wsu")
        s0 = dsem()
        lngs = A.alloc([128, 512], F32, "lng")
        lnbs = A.alloc([128, 512], F32, "lnb")
        sgbs = A.alloc([128, 512], F32, "sgb")
        wsf = A.alloc([128, 4, 128], F32, "wsf")
        wsb = A.alloc([128, 4, 128], BF16, "wsb")
        rc = P.R("sgconst")
        P.dma(lngs[:], lng[l].broadcast_to([128, 512]), s0, writes=[rc])
        P.dma(lnbs[:], lnb[l].broadcast_to([128, 512]), s0, writes=[rc])
        P.dma(sgbs[:], sgb[l].broadcast_to([128, 512]), s0, writes=[rc])
        P.dma(wsf[:], sgwT[l].rearrange("g s t -> s g t"), s0, writes=[rc])
        P.op("dve", lambda e: e.tensor_tensor(out=wsb[:], in0=wsf[:],
                                              in1=cmf[:, 3:4, :].broadcast_to([128, 4, 128]), op=ALU.mult),
             reads=[rc, Rconst], writes=[rc])
        wpipe = WPipe(wl, wsl, W, [(C_SU + blk * 256, 256) for blk in range(2)])
        for blk in range(2):
            wt, wr = wpipe.begin(blk)
            for ci in range(2):
                if ci == 1:
                    wpipe.mid()
                c = blk * 2 + ci
                for t in range(NT):
                    ps, pr = next_ps()
                    mm_group(ps, pr, wt, wr, KC, ci * 128, aT, aR[t], t)
                    P.op("act", lambda e, ps=ps, t=t, c=c: e.activation(out=uT[:, c, t * T:(t + 1) * T], in_=ps[:],
                                                                        func=AF.Gelu),
                         reads=[pr], writes=ruT[4 * t:4 * t + 4])
        wv = A.alloc([128, KC, 512], BF16, "wv")
        rwv = P.R("wv")
        wl.load(W, C_SV, 512, wv, rwv)
        gvs = Slots(2, [128, 4, 512], F32, "gv")
        sqs = Slots(1, [128, 4, 512], F32, "gsq")
        vbs = Slots(2, [128, 4, 512], BF16, "vb")
        sts = Slots(2, [128, 8, 4], F32, "st")
        tts = Slots(2, [128, 512], F32, "tt")
        for t in range(NT):
            gv, gr, _ = gvs.next()
            sq, sr, _ = sqs.next()
            vb, vbr, _ = vbs.next()
            st, str_, _ = sts.next()
            for bb in range(4):
                b = 4 * t + bb
                ps, pr = next_ps(0, 4)

                def fn(e, ps=ps, b=b):
                    ins = None
                    for kc in range(KC):
                        ins = e.matmul(ps[:], lhsT=aT[:, kc, b * 128:(b + 1) * 128], rhs=wv[:, kc, :],
                                       start=(kc == 0), stop=(kc == KC - 1))
                    return ins
                P.op("pe", fn, reads=[rwv, aR[t]], writes=[pr])
                P.op("act", lambda e, gv=gv, ps=ps, bb=bb: e.activation(out=gv[:, bb, :], in_=ps[:], func=AF.Gelu),
                     reads=[pr], writes=[gr])
            P.op("act", lambda e, gv=gv, sq=sq: e.activation(out=sq[:], in_=gv[:], func=AF.Square), reads=[gr], writes=[sr])
            P.op("dve", lambda e, gv=gv, st=st: e.tensor_reduce(out=st[:, 0, :], in_=gv[:], axis=AX.X, op=ALU.add),
                 reads=[gr], writes=[str_])
            P.op("dve", lambda e, sq=sq, st=st: e.tensor_reduce(out=st[:, 1, :], in_=sq[:], axis=AX.X, op=ALU.add),
                 reads=[sr, str_], writes=[str_])
            P.op("dve", lambda e, st=st: e.tensor_scalar(out=st[:, 2:4, :], in0=st[:, 0:2, :], scalar1=1.0 / 512, scalar2=None,
                                                         op0=ALU.mult), reads=[str_], writes=[str_])
            P.op("dve", lambda e, st=st: e.tensor_tensor(out=st[:, 4, :], in0=st[:, 2, :], in1=st[:, 2, :], op=ALU.mult),
                 reads=[str_], writes=[str_])
            P.op("dve", lambda e, st=st: e.scalar_tensor_tensor(out=st[:, 5, :], in0=st[:, 3, :], scalar=LN_EPS, in1=st[:, 4, :],
                                                                op0=ALU.add, op1=ALU.subtract), reads=[str_], writes=[str_])
            P.op("act", lambda e, st=st: e.activation(out=st[:, 6, :], in_=st[:, 5, :], func=AF.Sqrt), reads=[str_], writes=[str_])
            P.op("dve", lambda e, st=st: e.reciprocal(out=st[:, 7, :], in_=st[:, 6, :]), reads=[str_], writes=[str_])
            for bb in range(4):
                P.op("dve", lambda e, st=st, gv=gv, bb=bb: e.tensor_scalar(
                    out=gv[:, bb, :], in0=gv[:, bb, :], scalar1=st[:, 2, bb:bb + 1], scalar2=st[:, 7, bb:bb + 1],
                    op0=ALU.subtract, op1=ALU.mult), reads=[gr, str_], writes=[gr])
            P.op("pool", lambda e, gv=gv: e.tensor_tensor(out=gv[:, 0:2, :], in0=gv[:, 0:2, :],
                                                          in1=lngs[:].unsqueeze(1).broadcast_to([128, 2, 512]), op=ALU.mult),
                 reads=[gr, rc], writes=[gr])
            P.op("dve", lambda e, gv=gv: e.tensor_tensor(out=gv[:, 2:4, :], in0=gv[:, 2:4, :],
                                                         in1=lngs[:].unsqueeze(1).broadcast_to([128, 2, 512]), op=ALU.mult),
                 reads=[gr, rc], writes=[gr])
            P.op("pool", lambda e, gv=gv, vb=vb: e.tensor_tensor(out=vb[:, 0:2, :], in0=gv[:, 0:2, :],
                                                                 in1=lnbs[:].unsqueeze(1).broadcast_to([128, 2, 512]), op=ALU.add),
                 reads=[gr, rc], writes=[vbr])
            P.op("dve", lambda e, gv=gv, vb=vb: e.tensor_tensor(out=vb[:, 2:4, :], in0=gv[:, 2:4, :],
                                                                in1=lnbs[:].unsqueeze(1).broadcast_to([128, 2, 512]), op=ALU.add),
                 reads=[gr, rc], writes=[vbr])
            for bb in range(4):
                b = 4 * t + bb
                ps2, pr2 = next_ps(4, 8)

                def fn2(e, ps2=ps2, vb=vb, bb=bb):
                    ins = None
                    for g in range(4):
                        ins = e.matmul(ps2[:, g * 128:(g + 1) * 128], lhsT=vb[:, bb, g * 128:(g + 1) * 128], rhs=wsb[:, g, :],
                                       start=True, stop=True)
                    return ins
                P.op("pe", fn2, reads=[vbr, rc], writes=[pr2])
                tt, ttr, _ = tts.next()
                P.op("dve", lambda e, tt=tt, ps2=ps2: e.tensor_tensor(out=tt[:], in0=ps2[:], in1=sgbs[:], op=ALU.add),
                     reads=[pr2, rc], writes=[ttr])
                ub_ = uT[:, :, b * 128:(b + 1) * 128]
                P.op("dve", lambda e, tt=tt, ub_=ub_: e.tensor_tensor(out=ub_, in0=tt[:].rearrange("p (g t) -> p g t", g=4),
                                                                      in1=ub_, op=ALU.mult),
                     reads=[ttr, ruT[b]], writes=[ruT[b]])
        s1 = dsem()
        for c in range(4):
            P.dma(ycT[c], uT[:, c, :], s1, reads=ruT, writes=[P.R("ycT", c)])
        P.barrier()
        A.reset(m)

    def phase_attn(l):
        m = A.mark()
        U = A.alloc([128, 6, S], BF16, "U")
        Dn = A.alloc([128, 2, S], F32, "Dn")
        rU = [P.R("U", t) for t in range(NT)]
        rD = [P.R("Dn", t) for t in range(NT)]
        qs = A.alloc([128, 2, S], BF16, "qs")
        ks = A.alloc([128, 2, S], BF16, "ks")
        vs = A.alloc([128, 32, 256], BF16, "vs")
        rq = [P.R("qs", 0), P.R("qs", 1)]
        rk = [P.R("ks", 0), P.R("ks", 1)]
        rv = P.R("vs")
        sq_ = [dsem(), dsem()]
        sk_ = [dsem(), dsem()]
        sv_ = dsem()
        pts = [Slots(2, [128, 512], BF16, "pT%d" % i) for i in range(2)]
        s0 = dsem()
        order = (2, 1, 0)
        for gi, g in enumerate(order):
            win, dil = GROUPS[g]
            nb = S // (dil * 128)
            for cc in range(2):
                P.dma(qs[:, cc, :], qkT[2 * g + cc], sq_[cc], reads=[P.R("qkT", 2 * g + cc)], writes=[rq[cc]])
                P.dma(ks[:, cc, :], qkT[6 + 2 * g + cc], sk_[cc], reads=[P.R("qkT", 6 + 2 * g + cc)], writes=[rk[cc]])
            P.dma(vs[:].rearrange("p b c -> p (b c)"), Vd[g], sv_, reads=[P.R("Vd", g)], writes=[rv])
            for r in range(dil):
                for n in range(nb):
                    t0 = r + dil * 128 * n
                    span = dil * 128
                    qsl = slice(t0, t0 + dil * 127 + 1, dil)
                    if (r * nb + n) % 4 == 1:
                        bg_pump(1)
                    kbs = []
                    if n > 0:
                        kbs.append((0, slice(t0 - span, t0 - span + dil * 127 + 1, dil), r * nb + n - 1))
                    kbs.append((1, qsl, r * nb + n))
                    lo = 0 if n > 0 else 256
                    psE, prE = next_ps(0, 6)
                    psO, prO = next_ps(0, 6)

                    def fn(e, psE=psE, psO=psO, kbs=kbs, qsl=qsl):
                        ins = None
                        for (mi, ksl, vbi) in kbs:
                            for ps in (psE, psO):
                                e.matmul(ps[:, mi * 256:(mi + 1) * 256], lhsT=ident, rhs=maskb[:, mi, 0:256],
                                         start=True, stop=False, skip_group_check=True)
                        for (mi, ksl, vbi) in kbs:
                            for jj in range(2):
                                for par, ps in ((0, psE), (1, psO)):
                                    pb = par * 64
                                    c0 = mi * 256 + jj * 128
                                    ins = e.matmul(ps[:, c0:c0 + 128], lhsT=ks[pb:pb + 64, jj, ksl],
                                                   rhs=qs[pb:pb + 64, jj, qsl], start=False, stop=True,
                                                   skip_group_check=True)
                        return ins
                    P.op("pe", fn, reads=rq + rk + [Rconst], writes=[prE, prO])
                    ptE, ptEr, _ = pts[0].next()
                    ptO, ptOr, _ = pts[1].next()
                    P.op("act", lambda e, pt=ptE, ps=psE, lo=lo: e.activation(out=pt[:, lo:512], in_=ps[:, lo:512],
                                                                             func=AF.Exp, scale=0.125),
                         reads=[prE], writes=[ptEr])
                    P.op("act", lambda e, pt=ptO, ps=psO, lo=lo: e.activation(out=pt[:, lo:512], in_=ps[:, lo:512],
                                                                             func=AF.Exp, scale=0.125),
                         reads=[prO], writes=[ptOr])
                    pso, pro = next_ps(6, 8)

                    def fn3(e, pso=pso, kbs=kbs, ptE=ptE, ptO=ptO):
                        ins = None
                        nk = len(kbs)
                        for j in range(4):
                            pb = (j % 2) * 64
                            pr_ = j // 2
                            pt = ptE if j % 2 == 0 else ptO
                            kw = {"tile_position": (0, pb)} if pb else {}
                            for i, (mi, ksl, vbi) in enumerate(kbs):
                                c0 = mi * 256 + pr_ * 128
                                ins = e.matmul(pso[pb:pb + 64, pr_ * 128:(pr_ + 1) * 128],
                                               lhsT=vs[:, vbi, j * 64:(j + 1) * 64], rhs=pt[:, c0:c0 + 128],
                                               start=(i == 0), stop=(i == nk - 1), skip_group_check=True, **kw)
                            for i, (mi, ksl, vbi) in enumerate(kbs):
                                c0 = mi * 256 + pr_ * 128
                                ins = e.matmul(pso[pb:pb + 64, 256 + pr_ * 128:256 + (pr_ + 1) * 128],
                                               lhsT=onesb[:, 0:64], rhs=pt[:, c0:c0 + 128],
                                               start=(i == 0), stop=(i == nk - 1), skip_group_check=True, **kw)
                        return ins
                    P.op("pe", fn3, reads=[rv, Rconst, ptEr, ptOr], writes=[pro])
                    tiles = sorted(set(range(t0 // T, (t0 + dil * 127) // T + 1)))
                    P.op("act", lambda e, pso=pso, g=g, qsl=qsl: e.activation(
                        out=U[:, 2 * g:2 * g + 2, qsl], in_=pso[:, 0:256].rearrange("p (a q) -> p a q", a=2), func=AF.Copy),
                        reads=[pro], writes=[rU[t] for t in tiles])
                    if gi == 0:
                        P.op("dve", lambda e, pso=pso, qsl=qsl: e.tensor_copy(
                            out=Dn[:, :, qsl], in_=pso[:, 256:512].rearrange("p (a q) -> p a q", a=2)),
                            reads=[pro], writes=[rD[t] for t in tiles])
                    else:
                        P.op("dve", lambda e, pso=pso, qsl=qsl: e.tensor_tensor(
                            out=Dn[:, :, qsl], in0=pso[:, 256:512].rearrange("p (a q) -> p a q", a=2),
                            in1=Dn[:, :, qsl], op=ALU.add),
                            reads=[pro] + [rD[t] for t in tiles], writes=[rD[t] for t in tiles])
        s1 = dsem()
        for t in range(NT):
            sl = slice(t * T, (t + 1) * T)
            P.op("act", lambda e, sl=sl: e.activation(out=Dn[:, :, sl], in_=Dn[:, :, sl], func=AF.Ln), reads=[rD[t]], writes=[rD[t]])
            P.op("act", lambda e, sl=sl: e.activation(out=Dn[:, :, sl], in_=Dn[:, :, sl], func=AF.Exp, scale=-1.0),
                 reads=[rD[t]], writes=[rD[t]])
            for g in range(3):
                eng = "pool" if g == 1 else "dve"
                P.op(eng, lambda e, sl=sl, g=g: e.tensor_tensor(out=U[:, 2 * g:2 * g + 2, sl], in0=U[:, 2 * g:2 * g + 2, sl],
                                                                in1=Dn[:, :, sl], op=ALU.mult),
                     reads=[rU[t], rD[t]], writes=[rU[t]])
        for c in range(6):
            P.dma(yaT[c], U[:, c, :], s1, reads=rU, writes=[P.R("yaT", c)])
        P.barrier()
        A.reset(m)

    def alloc_merge_w(l):
        wa = A.alloc([128, 6, D], BF16, "wa")
        wb_ = A.alloc([128, 4, D], BF16, "wb")
        wc_ = A.alloc([128, 4, D], BF16, "wc")
        wo = A.alloc([128, 8, D], BF16, "wo")
        rw = P.R("wmerge", l)
        bg_queue(w_a[l], wa, rw)
        bg_queue(w_b[l], wb_, rw)
        bg_queue(w_c[l], wc_, rw)
        bg_queue(w_out[l], wo, rw)
        return wa, wb_, wc_, wo, rw

    def phase_merge(l, hsrc, wts):
        m = A.mark()
        wa, wb_, wc_, wo, rw = wts
        bg_flush()
        ysl = Slots(2, [128, 14, T], BF16, "ys")
        gsl = Slots(2, [128, 3, 4, T], BF16, "gs")
        csl = Slots(2, [128, KC, T], BF16, "ct")
        ntmp = norm_tmp()
        hsl = Slots(2, [128, KC, T], F32, "hs")
        mT = A.alloc([128, KC, T], BF16, "mT")
        rm = P.R("mT")
        tsl = [Slots(2, [128, T], F32, "mt%d" % i) for i in range(3)]
        def load(t):
            sl = slice(t * T, (t + 1) * T)
            yt, yr, ysem = ysl.next()
            hs, hr, hsem = hsl.next()
            P.dma(yt[:, 0:6, :], yaT.rearrange("c p s -> p c s")[:, :, sl], ysem,
                  reads=[P.R("yaT", c) for c in range(6)], writes=[yr])
            P.dma(yt[:, 6:10, :], ybT.rearrange("c p s -> p c s")[:, :, sl], ysem,
                  reads=[P.R("ybT", c) for c in range(4)], writes=[yr])
            P.dma(yt[:, 10:14, :], ycT.rearrange("c p s -> p c s")[:, :, sl], ysem,
                  reads=[P.R("ycT", c) for c in range(4)], writes=[yr])
            P.dma(hs[:], hview(hsrc, t), hsem, reads=[P.R("hT", t)], writes=[hr])
            return yt, yr, hs, hr, hsem

        def loadg(t, half):
            sl = slice(t * T, (t + 1) * T)
            gt, gr, gsem = gsl.next()
            for b3 in range(3):
                P.dma(gt[:, b3, :, :], gT.rearrange("c p s -> p c s")[:, b3 * 8 + half * 4:b3 * 8 + half * 4 + 4, sl], gsem,
                      reads=[P.R("gT", c) for c in range(b3 * 8 + half * 4, b3 * 8 + half * 4 + 4)],
                      writes=[P.R("gsq", id(gt), b3)])
            return gt, [P.R("gsq", id(gt), b3) for b3 in range(3)]

        nxt = load(0)
        gnx = loadg(0, 0)
        pend = None

        def finish_norm(pd):
            hs_, hr_, t_ = pd
            ct, ctr, ctsem = csl.next()
            norm_apply(hs_, hr_, l * 3 + 1, lambda c, ct=ct: ct[:, c, :], ctr, ntmp)
            P.dma(cTd.rearrange("c p s -> p c s")[:, :, t_ * T:(t_ + 1) * T], ct[:], ctsem, reads=[ctr],
                  writes=[P.R("cTd", t_)])

        for t in range(NT):
            yt, yr, hs, hr, hsem = nxt
            for oc in range(KC):
                if oc % 4 == 0:
                    gt, grl = gnx
                    if oc == 0:
                        gnx = loadg(t, 1)
                    elif t + 1 < NT:
                        gnx = loadg(t + 1, 0)
                prods = []
                for bi, (wt, k0, kn) in enumerate(((wa, 0, 6), (wb_, 6, 4), (wc_, 10, 4))):
                    ps, pr = next_ps()

                    def fn(e, ps=ps, wt=wt, k0=k0, kn=kn, yt=yt, oc=oc):
                        ins = None
                        for kc in range(kn):
                            ins = e.matmul(ps[:], lhsT=wt[:, kc, oc * 128:(oc + 1) * 128], rhs=yt[:, k0 + kc, :],
                                           start=(kc == 0), stop=(kc == kn - 1))
                        return ins
                    P.op("pe", fn, reads=[rw, yr], writes=[pr])
                    tt, ttr, _ = tsl[bi].next()
                    P.op("dve", lambda e, tt=tt, ps=ps, gt=gt, bi=bi, oc=oc: e.tensor_tensor(
                        out=tt[:], in0=ps[:], in1=gt[:, bi, oc % 4, :], op=ALU.mult), reads=[pr] + grl, writes=[ttr])
                    prods.append((tt, ttr))
                (ta, ra), (tb, rb), (tc, rc_) = prods
                P.op("dve", lambda e, ta=ta, tb=tb: e.tensor_tensor(out=ta[:], in0=ta[:], in1=tb[:], op=ALU.add),
                     reads=[ra, rb], writes=[ra])
                P.op("pool", lambda e, ta=ta, tc=tc, oc=oc: e.tensor_tensor(out=mT[:, oc, :], in0=ta[:], in1=tc[:], op=ALU.add),
                     reads=[ra, rc_], writes=[rm])
                if oc == 0 and pend is not None:
                    finish_norm(pend)
                    pend = None
                if oc == 1 and t + 1 < NT:
                    nxt = load(t + 1)
            for oc in range(KC):
                ps, pr = next_ps()

                def fn(e, ps=ps, oc=oc):
                    ins = None
                    for kc in range(KC):
                        ins = e.matmul(ps[:], lhsT=wo[:, kc, oc * 128:(oc + 1) * 128], rhs=mT[:, kc, :],
                                       start=(kc == 0), stop=(kc == KC - 1))
                    return ins
                P.op("pe", fn, reads=[rw, rm], writes=[pr])
                P.op("dve", lambda e, ps=ps, hs=hs, oc=oc: e.tensor_tensor(out=hs[:, oc, :], in0=ps[:], in1=hs[:, oc, :],
                                                                           op=ALU.add), reads=[pr, hr], writes=[hr])
            P.dma(hview(hT, t), hs[:], hsem, reads=[hr], writes=[P.R("hT", t)])
            norm_square(hs, hr, ntmp)
            norm_rstd(ntmp)
            pend = (hs, hr, t)
        finish_norm(pend)
        P.barrier()
        A.reset(m)

    def phase_ffn_up(l, cT, cR):
        m = A.mark()
        wl = WLoader()
        wsl = Slots(2, [128, KC, 256], BF16, "wup")
        rows = Slots(2, [128, S], BF16, "frow")
        rl = Slots(3, [128, T], F32, "relu")
        wpipe = WPipe(wl, wsl, w_up[l], [(blk * 256, 256) for blk in range(16)])
        for blk in range(16):
            wt, wr = wpipe.begin(blk)
            for ci in range(2):
                if ci == 1:
                    wpipe.mid()
                fc = blk * 2 + ci
                bg_pump(1)
                row, rr, rsem = rows.next()
                for t in range(NT):
                    ps, pr = next_ps()
                    mm_group(ps, pr, wt, wr, KC, ci * 128, cT, cR[t], t)
                    rt, rtr, _ = rl.next()
                    P.op("act", lambda e, rt=rt, ps=ps: e.activation(out=rt[:], in_=ps[:], func=AF.Relu), reads=[pr], writes=[rtr])
                    P.op("dve", lambda e, rt=rt, row=row, t=t: e.tensor_tensor(out=row[:, t * T:(t + 1) * T], in0=rt[:],
                                                                               in1=rt[:], op=ALU.mult),
                         reads=[rtr], writes=[rr])
                P.dma(fT[fc], row[:], rsem, reads=[rr], writes=[P.R("fT", fc)])
        P.barrier()
        A.reset(m)

    def phase_ffn_down(l, wd, rw):
        m = A.mark()
        bg_flush_except = None
        fsl = Slots(2, [128, 32, T], BF16, "fs")
        hsl = Slots(2, [128, KC, T], F32, "hs")
        def load(t):
            sl = slice(t * T, (t + 1) * T)
            ft, fr, fsem = fsl.next()
            hs, hr, hsem = hsl.next()
            bg_pump(1)
            for q4 in range(4):
                P.dma(ft[:, 8 * q4:8 * q4 + 8, :], fT.rearrange("c p s -> p c s")[:, 8 * q4:8 * q4 + 8, sl], fsem,
                      reads=[P.R("fT", c) for c in range(8 * q4, 8 * q4 + 8)], writes=[P.R("fsq", id(ft), q4)])
            P.dma(hs[:], hview(hT, t), hsem, reads=[P.R("hT", t)], writes=[hr])
            return ft, [P.R("fsq", id(ft), q4) for q4 in range(4)], hs, hr, hsem

        nxt = load(0)
        for t in range(NT):
            ft, frl, hs, hr, hsem = nxt
            if t + 1 < NT:
                nxt = load(t + 1)
            for oc in range(KC):
                ps, pr = next_ps()

                def fn(e, ps=ps, oc=oc, ft=ft):
                    ins = None
                    for kc in range(32):
                        ins = e.matmul(ps[:], lhsT=wd[:, kc, oc * 128:(oc + 1) * 128], rhs=ft[:, kc, :],
                                       start=(kc == 0), stop=(kc == 31))
                    return ins
                P.op("pe", fn, reads=[rw] + frl, writes=[pr])
                P.op("dve", lambda e, ps=ps, hs=hs, oc=oc: e.tensor_tensor(out=hs[:, oc, :], in0=ps[:], in1=hs[:, oc, :],
                                                                           op=ALU.add), reads=[pr, hr], writes=[hr])
            P.dma(hview(hT, t), hs[:], hsem, reads=[hr], writes=[P.R("hT", t)])
        P.barrier()
        A.reset(m)

    def phase_ple(l, wg, wp, rw, dead_lo, fuse):
        m = A.mark()
        bg_flush()
        keep = A.off
        A.off = dead_lo
        hsl_ = Slots(2, [128, KC, T], F32, "hs")
        if fuse == "final":
            osl = Slots(2, [128, KC, T], F32, "os")
        elif fuse == "next":
            osl = Slots(2, [128, KC, T], BF16, "as")
        assert A.off <= dead_lo + 64 * 1024
        A.off = keep
        hsl = hsl_
        psl = Slots(2, [128, 2, T], F32, "pf")
        pbs = Slots(2, [128, 2, T], BF16, "pb")
        eTs = [A.alloc([128, KC, T], BF16, "eT") for _ in range(2)]
        reT = [P.R("eT", l, i) for i in range(2)]
        tmp3 = norm_tmp()
        tmpn = norm_tmp()
        sgs = Slots(2, [128, T], F32, "sg")
        tts = Slots(2, [128, T], F32, "pt")

        def load(t):
            sl = slice(t * T, (t + 1) * T)
            hs, hr, hsem = hsl.next()
            pf, pfr, pfsem = psl.next()
            P.dma(hs[:], hview(hT, t), hsem, reads=[P.R("hT", t)], writes=[hr])
            P.dma(pf[:], pT[l].rearrange("(c p) s -> p c s", p=128)[:, :, sl], pfsem, writes=[pfr])
            pb, pbr, _ = pbs.next()
            P.op("pool", lambda e, pb=pb, pf=pf: e.tensor_copy(out=pb[:], in_=pf[:]), reads=[pfr], writes=[pbr])
            return hs, hr, hsem, pb, pbr

        def finish_fused(pd):
            hs_, hr_, t_ = pd
            ot, otr, otsem = osl.next()
            if fuse == "next":
                norm_apply(hs_, hr_, (l + 1) * 3 + 0, lambda c, ot=ot: ot[:, c, :], otr, tmpn)
                P.dma(aTd.rearrange("c p s -> p c s")[:, :, t_ * T:(t_ + 1) * T], ot[:], otsem, reads=[otr],
                      writes=[P.R("aTd", t_)])
            else:
                norm_apply(hs_, hr_, NL * 3, lambda c, ot=ot: ot[:, c, :], otr, tmpn)
                P.dma(hview(outT, t_), ot[:], otsem, reads=[otr], writes=[P.R("outT", t_)])

        cur = load(0)
        norm_square(cur[0], cur[1], tmp3)
        norm_rstd(tmp3)
        norm_apply(cur[0], cur[1], l * 3 + 2, lambda c: eTs[0][:, c, :], reT[0], tmp3)
        pend = None
        nxt = None
        for t in range(NT):
            hs, hr, hsem, pb, pbr = cur
            eT, re_ = eTs[t % 2], reT[t % 2]
            for oc in range(KC):
                ps, pr = next_ps()

                def fn(e, ps=ps, oc=oc, eT=eT):
                    ins = None
                    for kc in range(KC):
                        ins = e.matmul(ps[:], lhsT=wg[:, kc, oc * 128:(oc + 1) * 128], rhs=eT[:, kc, :],
                                       start=(kc == 0), stop=(kc == KC - 1))
                    return ins
                P.op("pe", fn, reads=[rw, re_], writes=[pr])
                ps2, pr2 = next_ps()

                def fn2(e, ps2=ps2, oc=oc, pb=pb):
                    ins = None
                    for kc in range(2):
                        ins = e.matmul(ps2[:], lhsT=wp[:, kc, oc * 128:(oc + 1) * 128], rhs=pb[:, kc, :],
                                       start=(kc == 0), stop=(kc == 1))
                    return ins
                P.op("pe", fn2, reads=[rw, pbr], writes=[pr2])
                sg, sgr, _ = sgs.next()
                tt, ttr, _ = tts.next()
                P.op("act", lambda e, sg=sg, ps=ps: e.activation(out=sg[:], in_=ps[:], func=AF.Sigmoid), reads=[pr], writes=[sgr])
                P.op("dve", lambda e, tt=tt, ps2=ps2, sg=sg: e.tensor_tensor(out=tt[:], in0=ps2[:], in1=sg[:], op=ALU.mult),
                     reads=[pr2, sgr], writes=[ttr])
                P.op("dve", lambda e, tt=tt, hs=hs, oc=oc: e.tensor_tensor(out=hs[:, oc, :], in0=tt[:], in1=hs[:, oc, :],
                                                                           op=ALU.add), reads=[ttr, hr], writes=[hr])
                if oc == 0 and pend is not None:
                    finish_fused(pend)
                    pend = None
                if oc == 0 and t + 1 < NT:
                    nxt = load(t + 1)
                if oc == 2 and t + 1 < NT:
                    norm_square(nxt[0], nxt[1], tmp3)
                if oc == 4 and t + 1 < NT:
                    norm_rstd(tmp3)
                if oc == 6 and t + 1 < NT:
                    norm_apply(nxt[0], nxt[1], l * 3 + 2, lambda c, t=t: eTs[(t + 1) % 2][:, c, :], reT[(t + 1) % 2], tmp3)
            if fuse != "final":
                P.dma(hview(hT, t), hs[:], hsem, reads=[hr], writes=[P.R("hT", t)])
            if fuse is not None:
                norm_square(hs, hr, tmpn)
                norm_rstd(tmpn)
                pend = (hs, hr, t)
            cur = nxt
        if pend is not None:
            finish_fused(pend)
        P.barrier()
        A.reset(m)

    setup()
    wrote_h = False
    fused_final = False
    for l in range(nl):
        m = A.mark()
        aT = A.alloc([128, KC, S], BF16, "aT")
        aR = [P.R("aT", l, t) for t in range(NT)]
        if l == 0:
            norm_phase(xT, 0, aT, aR)
        else:
            load_act(aTd, aT, aR, "aTd")
        phase_qkv(l, aT, aR)
        phase_conv_gates(l, aT, aR)
        phase_sgu(l, aT, aR)
        A.reset(m)
        wts = alloc_merge_w(l)
        phase_attn(l)
        phase_merge(l, xT if l == 0 else hT, wts)
        wrote_h = True
        bg_flush()
        A.reset(m)
        wd = A.alloc([128, 32, D], BF16, "wd")
        rwd = P.R("wd", l)
        bg_queue(w_down[l], wd, rwd)
        m1 = A.mark()
        cT = A.alloc([128, KC, S], BF16, "cT")
        cR = [P.R("cT", l, t) for t in range(NT)]
        load_act(cTd, cT, cR, "cTd")
        phase_ffn_up(l, cT, cR)
        bg_flush()
        A.reset(m1)
        wg = A.alloc([128, 8, D], BF16, "wg")
        wp = A.alloc([128, 2, D], BF16, "wp")
        rwp = P.R("wple", l)
        bg_queue(w_pg[l], wg, rwp)
        bg_queue(w_pe[l], wp, rwp)
        phase_ffn_down(l, wd, rwd)
        last = (l == nl - 1)
        fuse = ("final" if (final_norm and nl == NL) else None) if last else "next"
        fused_final = fused_final or fuse == "final"
        phase_ple(l, wg, wp, rwp, m, fuse)
        bg_flush()
        A.reset(m)
    if not fused_final:
        final_phase(hT if wrote_h else xT, NL * 3)
    P.emit()
    return nc, P


def _consts():
    ident = np.eye(128, dtype=np.float32)
    rot = np.zeros((128, 128), np.float32)
    for base in (0, 64):
        for i in range(8):
            rot[base + i, base + 8 + i] = -1.0
            rot[base + 8 + i, base + i] = 1.0
    rotT = rot.T.copy()
    ones = np.ones((128, 128), np.float32)
    s = np.arange(128)[:, None]
    t = np.arange(128)[None, :]
    tril = (s <= t).astype(np.float32)
    cmat = np.stack([ident, rotT, ones, tril], axis=1).astype(np.float32)
    k = np.arange(128)[:, None]
    q = np.arange(128)[None, :]
    prev = np.where(k >= q, 0.0, MASKV).astype(np.float32)
    cur = np.where(k <= q, 0.0, MASKV).astype(np.float32)
    cmask = np.stack([np.tile(prev, (1, 4)), np.tile(cur, (1, 4))], axis=1).astype(np.float32)
    inv_freq = (np.float32(500000.0) ** (-(np.arange(0, 16, 2, dtype=np.float32) / np.float32(16)))).astype(np.float32)
    invf = np.zeros((128, 1), np.float32)
    for p in range(128):
        if p % 64 < 16:
            invf[p, 0] = inv_freq[(p % 64) % 8]
    return cmat, cmask, invf


_CACHE = {}


def _prep_shared(inp):
    f = lambda a: np.ascontiguousarray(np.asarray(a, dtype=np.float32))
    cmat, cmask, invf = _consts()
    gl = []
    for l in range(NL):
        for nm in ("norm_mix_g", "norm_mlp_g", "norm_ple_g"):
            gl.append(np.asarray(inp[nm][l], np.float32).reshape(KC, 128).T)
    gl.append(np.asarray(inp["norm_final_g"], np.float32).reshape(KC, 128).T)
    gains = np.ascontiguousarray(np.stack(gl, axis=1))
    cw = np.asarray(inp["conv_w"], np.float32)
    convw = np.ascontiguousarray(cw.reshape(NL, 3, 4, 128).transpose(3, 0, 2, 1))
    sgwT = np.ascontiguousarray(np.asarray(inp["sg_w"], np.float32).transpose(0, 1, 3, 2))
    shared = {
        "w_in": f(inp["w_in"]), "w_a": f(inp["w_branch_a"]), "w_b": f(inp["w_branch_b"]),
        "w_c": f(inp["w_branch_c"]), "w_out": f(inp["w_out"]), "w_up": f(inp["w_up"]),
        "w_down": f(inp["w_down"]), "w_pg": f(inp["w_ple_gate"]), "w_pe": f(inp["w_ple_proj"]),
        "sgwT": sgwT, "gains": gains, "convw": convw,
        "lng": f(inp["sg_ln_g"]).reshape(NL, 1, 512), "lnb": f(inp["sg_ln_b"]).reshape(NL, 1, 512),
        "sgb": f(inp["sg_b"]).reshape(NL, 1, 512),
        "cmat": cmat, "cmask": cmask, "invf": invf,
    }
    return shared


def run(inp, nl=NL, final_norm=True, cores=8, trace=False, stop=99):
    key = (nl, final_norm, stop)
    if key not in _CACHE:
        _CACHE[key] = build_program(nl, final_norm, stop)[0]
    nc = _CACHE[key]
    shared = _prep_shared(inp)
    x = np.asarray(inp["x"], np.float32)
    p = np.asarray(inp["p"], np.float32)
    posn = np.asarray(inp["positions"], np.int32)
    in_maps = []
    for b in range(cores):
        mp = dict(shared)
        mp["xT"] = np.ascontiguousarray(x[b].T)
        mp["pT"] = np.ascontiguousarray(p[:, b].transpose(0, 2, 1))
        mp["pos"] = np.ascontiguousarray(posn[b].reshape(1, S))
        in_maps.append(mp)
    res = run_bass_kernel_spmd(nc, in_maps, core_ids=list(range(cores)), **({"trace": True} if trace else {}))
    out = np.stack([np.ascontiguousarray(r["outT"].T) for r in res.results], axis=0)
    return out.astype(np.float32), res


def kernel(**inputs):
    out, _ = run(inputs)
    return out
```

```python
import math
from contextlib import ExitStack

import numpy as np
import concourse.bass as bass
import concourse.mybir as mybir
from concourse.bass_utils import run_bass_kernel_spmd

F32 = mybir.dt.float32
BF16 = mybir.dt.bfloat16
I32 = mybir.dt.int32
AF = mybir.ActivationFunctionType
ALU = mybir.AluOpType
AX = mybir.AxisListType

S = 4096
D = 1024
T = 512
NT = S // T
KC = D // 128
NL = 4
INW = 7936
DFF = 4096
PLE = 256
GROUPS = ((128, 1), (512, 4), (2048, 16))
RMS_EPS = 1e-6
LN_EPS = 1e-5
MASKV = -240000.0
C_Q, C_K, C_V = 0, 768, 1536
C_X, C_B, C_C = 2304, 2816, 3328
C_SU, C_SV = 3840, 4352
C_G = 4864

ENGS = ("pe", "act", "dve", "pool", "sp")
DBG = 99


class Res:
    __slots__ = ("writers", "readers", "excl")

    def __init__(self):
        self.writers = {}
        self.readers = {}
        self.excl = False


class Prog:
    def __init__(self, nc):
        self.nc = nc
        self.streams = {e: [] for e in ENGS}
        self.count = {e: 0 for e in ENGS}
        self.seen = {e: {} for e in ENGS}
        self.semh = {}
        self.res = {}
        self.ndma = 0

    def R(self, *key):
        r = self.res.get(key)
        if r is None:
            r = self.res[key] = Res()
        return r

    def newsem(self, key):
        self.count[key] = 0
        return key

    def _deps(self, eng, reads, writes):
        deps = {}
        for r in reads:
            for k, v in r.writers.items():
                if deps.get(k, 0) < v:
                    deps[k] = v
            if r.excl:
                for k, v in r.readers.items():
                    if k != eng and deps.get(k, 0) < v:
                        deps[k] = v
        for w in writes:
            for dct in (w.writers, w.readers):
                for k, v in dct.items():
                    if k != eng and deps.get(k, 0) < v:
                        deps[k] = v
        if eng == "pe":
            deps.pop("pe", None)
        seen = self.seen[eng]
        st = self.streams[eng]
        for k, v in deps.items():
            if seen.get(k, 0) < v:
                st.append(("w", k, v))
                seen[k] = v

    def op(self, eng, fn, reads=(), writes=()):
        self._deps(eng, reads, writes)
        self.count[eng] += 1
        v = self.count[eng]
        self.streams[eng].append(("o", fn, eng, 1))
        for r in reads:
            r.readers[eng] = v
        for w in writes:
            w.writers = {eng: v}
            w.readers = {}

    def dma(self, out, in_, sem, reads=(), writes=(), q="sp"):
        self._deps(q, reads, writes)
        self.count[sem] += 16
        v = self.count[sem]
        self.streams[q].append(("o", lambda e, o=out, i=in_: e.dma_start(out=o, in_=i), sem, 16))
        for r in reads:
            r.readers[sem] = v
        for w in writes:
            w.writers = {sem: v}
            w.readers = {}
        self.ndma += 1

    def barrier(self):
        tot = dict(self.count)
        for e in ENGS:
            seen = self.seen[e]
            for k, v in tot.items():
                if k == e and e == "pe":
                    continue
                if v > 0 and seen.get(k, 0) < v:
                    self.streams[e].append(("w", k, v))
                    seen[k] = v
        for r in self.res.values():
            r.writers = {}
            r.readers = {}

    def emit(self):
        nc = self.nc
        with ExitStack() as es:
            for k in self.count:
                self.semh[k] = es.enter_context(nc.semaphore("s_" + str(k)))
            block = es.enter_context(nc.Block())

            def replay(e, name):
                semh = self.semh
                for it in self.streams[name]:
                    if it[0] == "w":
                        e.wait_ge(semh[it[1]], it[2])
                    else:
                        ins = it[1](e)
                        ins.then_inc(semh[it[2]], it[3])

            @block.tensor
            def _(e):
                replay(e, "pe")

            @block.scalar
            def _(e):
                replay(e, "act")

            @block.vector
            def _(e):
                replay(e, "dve")

            @block.gpsimd
            def _(e):
                replay(e, "pool")

            @block.sync
            def _(e):
                replay(e, "sp")


class Arena:
    def __init__(self, nc, limit=229000):
        self.nc = nc
        self.off = 16640
        self.n = 0
        self.limit = limit

    def alloc(self, shape, dtype, name="t"):
        esz = 4 if dtype in (F32, I32) else 2
        nbytes = esz * int(np.prod(shape[1:]))
        nbytes = (nbytes + 63) // 64 * 64
        off = self.off
        self.off += nbytes
        assert self.off <= self.limit, ("SBUF arena overflow", name, self.off)
        self.n += 1
        return self.nc.alloc_sbuf_tensor_at("%s_%d" % (name, self.n), list(shape), dtype, offset=off)

    def mark(self):
        return self.off

    def reset(self, m):
        self.off = m


def build_program(nl=NL, final_norm=True, stop=99):
    nc = bass.Bass("TRN2", target_bir_lowering=False)
    P = Prog(nc)
    A = Arena(nc)

    def din(name, shape, dt=F32):
        return nc.dram_tensor(name, list(shape), dt, kind="ExternalInput").ap()

    def dscr(name, shape, dt):
        return nc.dram_tensor(name, list(shape), dt, kind="Internal").ap()

    xT = din("xT", [D, S])
    pT = din("pT", [NL, PLE, S])
    pos = din("pos", [1, S], I32)
    w_in = din("w_in", [NL, D, INW])
    w_a = din("w_a", [NL, 768, D])
    w_b = din("w_b", [NL, 512, D])
    w_c = din("w_c", [NL, 512, D])
    w_out = din("w_out", [NL, D, D])
    w_up = din("w_up", [NL, D, DFF])
    w_down = din("w_down", [NL, DFF, D])
    w_pg = din("w_pg", [NL, D, D])
    w_pe = din("w_pe", [NL, PLE, D])
    sgwT = din("sgwT", [NL, 4, 128, 128])
    gains = din("gains", [128, NL * 3 + 1, KC])
    convw = din("convw", [128, NL, 4, 3])
    lng = din("lng", [NL, 1, 512])
    lnb = din("lnb", [NL, 1, 512])
    sgb = din("sgb", [NL, 1, 512])
    cmat = din("cmat", [128, 4, 128])
    cmask = din("cmask", [128, 2, 512])
    invf = din("invf", [128, 1])
    outT = nc.dram_tensor("outT", [D, S], F32, kind="ExternalOutput").ap()

    hT = dscr("hT", [D, S], F32)
    cosD = dscr("cosD", [128, S], F32)
    sinD = dscr("sinD", [128, S], F32)
    qkT = dscr("qkT", [12, 128, S], BF16)
    Vd = dscr("Vd", [3, 128, 32 * 256], BF16)
    ybT = dscr("ybT", [4, 128, S], BF16)
    ycT = dscr("ycT", [4, 128, S], BF16)
    gT = dscr("gT", [24, 128, S], BF16)
    yaT = dscr("yaT", [6, 128, S], BF16)
    fT = dscr("fT", [32, 128, S], BF16)
    cTd = dscr("cTd", [KC, 128, S], BF16)
    aTd = dscr("aTd", [KC, 128, S], BF16)

    PS = [nc.alloc_psum_tensor("ps%d" % i, [128, 512], F32) for i in range(8)]
    PSR = [P.R("ps", i) for i in range(8)]
    for r_ in PSR:
        r_.excl = True

    cm = A.alloc([128, 4, 128], BF16, "cm")
    ident, rotT, onesb, trilb = (cm[:, i, :] for i in range(4))
    cmf = A.alloc([128, 4, 128], F32, "cmf")
    onesf = cmf[:, 2, :]
    maskb = A.alloc([128, 2, 512], BF16, "maskb")
    gsb = A.alloc([128, NL * 3 + 1, KC], F32, "gsb")
    cwsb = A.alloc([128, NL, 4, 3], F32, "cwsb")
    invfsb = A.alloc([128, 1], F32, "invf")
    epsb = A.alloc([128, 1], F32, "epsb")
    Rconst = P.R("const")
    base_mark = A.mark()

    cnt = {"sem": 0}

    def dsem():
        cnt["sem"] += 1
        key = "d%d" % cnt["sem"]
        if key not in P.count:
            P.newsem(key)
        return key

    _pbar = P.barrier

    def barrier():
        _pbar()
        cnt["sem"] = 0

    P.barrier = barrier

    class Slots:
        def __init__(self, n, shape, dtype, name):
            self.t = [A.alloc(shape, dtype, name) for _ in range(n)]
            self.r = [P.R(name, id(self), i) for i in range(n)]
            self.s = [dsem() for _ in range(n)]
            self.i = -1
            self.n = n

        def next(self):
            self.i = (self.i + 1) % self.n
            return self.t[self.i], self.r[self.i], self.s[self.i]

    psrot = {}

    def next_ps(lo=0, hi=8):
        i = psrot.get((lo, hi), lo)
        psrot[(lo, hi)] = lo + (i + 1 - lo) % (hi - lo)
        return PS[i], PSR[i]

    def setup():
        m = A.mark()
        s0 = dsem()
        mf = A.alloc([128, 2, 512], F32, "mf")
        P.dma(cmf[:], cmat, s0, writes=[Rconst])
        P.dma(mf[:], cmask, s0, writes=[Rconst])
        P.dma(gsb[:], gains, s0, writes=[Rconst])
        P.dma(cwsb[:], convw, s0, writes=[Rconst])
        P.dma(invfsb[:], invf, s0, writes=[Rconst])
        P.op("dve", lambda e: e.memset(epsb[:], RMS_EPS), reads=[Rconst], writes=[Rconst])
        P.op("dve", lambda e: e.tensor_copy(out=cm[:], in_=cmf[:]), reads=[Rconst], writes=[Rconst])
        P.op("dve", lambda e: e.tensor_copy(out=maskb[:], in_=mf[:]), reads=[Rconst], writes=[Rconst])
        posi = A.alloc([128, S], I32, "posi")
        ang = A.alloc([128, S], F32, "ang")
        t1 = A.alloc([128, S], F32, "t1")
        ki = A.alloc([128, S], I32, "ki")
        kf = A.alloc([128, S], F32, "kf")
        r_ang, r_t1, r_ki, r_kf = (P.R("su", i) for i in range(4))
        P.dma(posi[:], pos.broadcast_to([128, S]), dsem(), writes=[r_ki])
        s_st = dsem()
        P.op("dve", lambda e: e.tensor_copy(out=t1[:], in_=posi[:]), reads=[r_ki], writes=[r_t1])
        P.op("dve", lambda e: e.tensor_scalar(out=ang[:], in0=t1[:], scalar1=invfsb[:, 0:1], scalar2=None,
                                              op0=ALU.mult), reads=[r_t1, Rconst], writes=[r_ang])
        C1 = 6.28125
        C2 = 2.0 * math.pi - 6.28125
        i2p = 1.0 / (2.0 * math.pi)
        for which, (shiftk, shifta, dst) in enumerate(((0.0, 0.0, sinD), (0.25, 0.5 * math.pi, cosD))):
            P.op("dve", lambda e, sk=shiftk: e.tensor_scalar(out=t1[:], in0=ang[:], scalar1=i2p, scalar2=sk,
                                                             op0=ALU.mult, op1=ALU.add),
                 reads=[r_ang], writes=[r_t1])
            P.op("dve", lambda e: e.tensor_copy(out=ki[:], in_=t1[:]), reads=[r_t1], writes=[r_ki])
            P.op("dve", lambda e: e.tensor_copy(out=kf[:], in_=ki[:]), reads=[r_ki], writes=[r_kf])
            P.op("dve", lambda e: e.scalar_tensor_tensor(out=t1[:], in0=kf[:], scalar=-C1, in1=ang[:],
                                                         op0=ALU.mult, op1=ALU.add),
                 reads=[r_kf, r_ang], writes=[r_t1])
            P.op("dve", lambda e: e.scalar_tensor_tensor(out=t1[:], in0=kf[:], scalar=-C2, in1=t1[:],
                                                         op0=ALU.mult, op1=ALU.add),
                 reads=[r_kf, r_t1], writes=[r_t1])
            P.op("dve", lambda e, sa=shifta: e.tensor_scalar(out=t1[:], in0=t1[:], scalar1=sa, scalar2=3.1415925,
                                                             op0=ALU.add, op1=ALU.min),
                 reads=[r_t1], writes=[r_t1])
            P.op("dve", lambda e: e.tensor_scalar(out=t1[:], in0=t1[:], scalar1=-3.1415925, scalar2=None,
                                                  op0=ALU.max), reads=[r_t1], writes=[r_t1])
            P.op("act", lambda e: e.activation(out=kf[:], in_=t1[:], func=AF.Sin), reads=[r_t1], writes=[r_kf])
            P.dma(dst, kf[:], s_st, reads=[r_kf], writes=[P.R("trig", which)])
        P.barrier()
        A.reset(m)

    class WLoader:
        def __init__(self, kcmax=8, ncmax=256):
            self.stg = Slots(2, [128, kcmax, ncmax], F32, "wstg")
            self.kcmax = kcmax
            self.ncmax = ncmax
            self.tog = 0

        def issue(self, wsrc, c0, ncols, dst, dres, kc0=0, kcn=None):
            K = wsrc.shape[0]
            if kcn is None:
                kcn = K // 128
            src3 = wsrc.rearrange("(kc p) n -> p kc n", p=128)
            tok = []
            for k0 in range(kc0, kc0 + kcn, self.kcmax):
                kn = min(self.kcmax, kc0 + kcn - k0)
                for cc in range(0, ncols, self.ncmax):
                    cn = min(self.ncmax, ncols - cc)
                    st, sr, ss = self.stg.next()
                    P.dma(st[:, 0:kn, 0:cn], src3[:, k0:k0 + kn, c0 + cc:c0 + cc + cn], ss, writes=[sr])
                    tok.append((st[:, 0:kn, 0:cn], sr, dst[:, k0:k0 + kn, cc:cc + cn], dres))
            return tok

        def convert(self, tok):
            for (src, sr, dst, dres) in tok:
                self.tog ^= 1
                if self.tog:
                    P.op("act", lambda e, o=dst, i=src: e.activation(out=o, in_=i, func=AF.Copy), reads=[sr], writes=[dres])
                else:
                    P.op("dve", lambda e, o=dst, i=src: e.tensor_copy(out=o, in_=i), reads=[sr], writes=[dres])

        def load(self, wsrc, c0, ncols, dst, dres, kc0=0, kcn=None):
            K = wsrc.shape[0]
            if kcn is None:
                kcn = K // 128
            for k0 in range(kc0, kc0 + kcn, self.kcmax):
                kn = min(self.kcmax, kc0 + kcn - k0)
                for cc in range(0, ncols, self.ncmax):
                    cn = min(self.ncmax, ncols - cc)
                    tok = self.issue(wsrc, c0 + cc, cn, dst, dres, k0, kn)
                    tok = [(a, b, dst[:, k0:k0 + kn, cc:cc + cn], d) for (a, b, _, d) in tok]
                    self.convert(tok)

    class WPipe:
        def __init__(self, wl, wsl, W, blocks):
            self.wl, self.wsl, self.W, self.blocks = wl, wsl, W, blocks
            self.cur = None
            self.nxt = None
            self.tok = None

        def _issue(self, i):
            c0, ncols = self.blocks[i]
            wt, wr, _ = self.wsl.next()
            tok = self.wl.issue(self.W, c0, ncols, wt, wr)
            return (wt, wr), tok

        def begin(self, i):
            if i == 0:
                self.cur, tok = self._issue(0)
                self.wl.convert(tok)
            else:
                assert self.nxt is not None
                self.cur = self.nxt
            self.nxt = None
            if i + 1 < len(self.blocks):
                self.nxt, self.tok = self._issue(i + 1)
            return self.cur

        def mid(self):
            if self.tok is not None:
                self.wl.convert(self.tok)
                self.tok = None

    bg = {"q": [], "tog": 0}
    bg_stg = [A.alloc([128, 8, 256], F32, "bgstg") for _ in range(2)]
    bg_res = [P.R("bgstg", i) for i in range(2)]
    bg_sem = [P.newsem("bg%d" % i) for i in range(2)]
    bg_i = {"i": 0}

    def bg_queue(wsrc, dst, dres):
        K, N = wsrc.shape
        src3 = wsrc.rearrange("(kc p) n -> p kc n", p=128)
        for k0 in range(0, K // 128, 8):
            kn = min(8, K // 128 - k0)
            for cc in range(0, N, 256):
                cn = min(256, N - cc)
                bg["q"].append((src3[:, k0:k0 + kn, cc:cc + cn], dst[:, k0:k0 + kn, cc:cc + cn], dres, kn, cn))

    def bg_pump(n=1):
        for _ in range(n):
            if not bg["q"]:
                return
            src, dst, dres, kn, cn = bg["q"].pop(0)
            i = bg_i["i"]
            bg_i["i"] ^= 1
            st = bg_stg[i]
            P.dma(st[:, 0:kn, 0:cn], src, bg_sem[i], writes=[bg_res[i]])
            bg["tog"] ^= 1
            if bg["tog"]:
                P.op("act", lambda e, o=dst, i_=st[:, 0:kn, 0:cn]: e.activation(out=o, in_=i_, func=AF.Copy),
                     reads=[bg_res[i]], writes=[dres])
            else:
                P.op("dve", lambda e, o=dst, i_=st[:, 0:kn, 0:cn]: e.tensor_copy(out=o, in_=i_),
                     reads=[bg_res[i]], writes=[dres])

    def bg_flush():
        bg_pump(len(bg["q"]))

    def norm_square(hs, hr, tmp):
        sq, rs = tmp["sq"], tmp["res"]
        P.op("act", lambda e: e.activation(out=sq[:], in_=hs[:], func=AF.Square), reads=[hr], writes=[rs[0]])

    def norm_rstd(tmp):
        sq, rstd, rs = tmp["sq"], tmp["rstd"], tmp["res"]
        ps, pr = next_ps()

        def fnn(e, ps=ps):
            ins = None
            for c in range(KC):
                ins = e.matmul(ps[:], lhsT=onesb, rhs=sq[:, c, :], start=(c == 0), stop=(c == KC - 1))
            return ins
        P.op("pe", fnn, reads=[rs[0], Rconst], writes=[pr])
        P.op("act", lambda e, ps=ps: e.activation(out=rstd[:], in_=ps[:], func=AF.Ln, bias=epsb[:, 0:1], scale=1.0 / D),
             reads=[pr, Rconst], writes=[rs[2]])
        P.op("act", lambda e: e.activation(out=rstd[:], in_=rstd[:], func=AF.Exp, scale=-0.5), reads=[rs[2]], writes=[rs[2]])

    def norm_apply(hs, hr, gidx, out_ap_fn, out_res, tmp):
        rstd, rs = tmp["rstd"], tmp["res"]
        for c in range(KC):
            P.op("dve", lambda e, c=c: e.scalar_tensor_tensor(out=out_ap_fn(c), in0=hs[:, c, :],
                                                            scalar=gsb[:, gidx, c:c + 1], in1=rstd[:],
                                                            op0=ALU.mult, op1=ALU.mult),
                 reads=[hr, rs[2], Rconst], writes=[out_res])

    def norm_tile(hs, hr, gidx, out_ap_fn, out_res, tmp, wr_extra=()):
        norm_square(hs, hr, tmp)
        norm_rstd(tmp)
        norm_apply(hs, hr, gidx, out_ap_fn, out_res, tmp)

    def norm_tmp():
        return {"sq": A.alloc([128, KC, T], BF16, "sq"),
                "rstd": A.alloc([128, T], F32, "rstd"), "res": [P.R("nt", A.n, i) for i in range(3)]}

    def hview(src, t):
        return src.rearrange("(c p) s -> p c s", p=128)[:, :, t * T:(t + 1) * T]

    def norm_phase(src, gidx, actT, act_res):
        m = A.mark()
        hsl = Slots(2, [128, KC, T], F32, "hs")
        tmp = norm_tmp()
        for t in range(NT):
            hs, hr, hsem = hsl.next()
            P.dma(hs[:], hview(src, t), hsem, reads=[P.R("hT", t)], writes=[hr])
            norm_tile(hs, hr, gidx, lambda c, t=t: actT[:, c, t * T:(t + 1) * T], act_res[t], tmp)
        P.barrier()
        A.reset(m)

    def load_act(dsrc, actT, act_res, key):
        for t in range(NT):
            sl = slice(t * T, (t + 1) * T)
            P.dma(actT[:, :, sl], dsrc.rearrange("c p s -> p c s")[:, :, sl], dsem(), reads=[P.R(key, t)], writes=[act_res[t]])

    def final_phase(src, gidx):
        m = A.mark()
        hsl = Slots(2, [128, KC, T], F32, "hs")
        osl = Slots(2, [128, KC, T], F32, "os")
        tmp = norm_tmp()
        for t in range(NT):
            hs, hr, hsem = hsl.next()
            ot, orr, osem = osl.next()
            P.dma(hs[:], hview(src, t), hsem, reads=[P.R("hT", t)], writes=[hr])
            if final_norm:
                norm_tile(hs, hr, gidx, lambda c, ot=ot: ot[:, c, :], orr, tmp)
                P.dma(hview(outT, t), ot[:], osem, reads=[orr], writes=[P.R("outT", t)])
            else:
                P.dma(hview(outT, t), hs[:], hsem, reads=[hr], writes=[P.R("outT", t)])
        P.barrier()
        A.reset(m)

    def mm_group(ps, pr, wt, wres, kcn, col0, actT, act_res_t, t, ncols=128):
        def fn(e):
            ins = None
            for kc in range(kcn):
                ins = e.matmul(ps[:ncols, :], lhsT=wt[:, kc, col0:col0 + ncols], rhs=actT[:, kc, t * T:(t + 1) * T],
                               start=(kc == 0), stop=(kc == kcn - 1))
            return ins
        P.op("pe", fn, reads=[wres, act_res_t], writes=[pr])

    def phase_qkv(l, aT, aR, sub=9):
        m = A.mark()
        cs = A.alloc([128, S], F32, "cos")
        sn = A.alloc([128, S], F32, "sin")
        rtr = P.R("trigsb")
        s0 = dsem()
        P.dma(cs[:], cosD, s0, reads=[P.R("trig", 1)], writes=[rtr])
        P.dma(sn[:], sinD, s0, reads=[P.R("trig", 0)], writes=[rtr])
        wl = WLoader()
        wsl = Slots(2, [128, KC, 256], BF16, "wqk")
        zq = Slots(2, [128, T], BF16, "zq")
        t1s = Slots(2, [128, T], F32, "t1")
        t2s = Slots(2, [128, T], F32, "t2")
        rows = Slots(2, [128, S], BF16, "qrow")
        W = w_in[l]
        wpipe = WPipe(wl, wsl, W, [(blk * 256, 256) for blk in range(6)] + [(C_V + g * 256, 256) for g in range(3)])
        for blk in range(6 if sub >= 1 else 0):
            wt, wr = wpipe.begin(blk)
            for ci in range(2):
                if ci == 1:
                    wpipe.mid()
                oc = blk * 2 + ci
                row, rr, rsem = rows.next()
                def rot_tail(ps, pr, z, zr, t, row=None, rr=None):
                    ps2, pr2 = next_ps()
                    P.op("pe", lambda e, ps2=ps2, z=z: e.matmul(ps2[:], lhsT=rotT, rhs=z[:], start=True, stop=True),
                         reads=[zr, Rconst], writes=[pr2])
                    a1, a1r, _ = t1s.next()
                    a2, a2r, _ = t2s.next()
                    P.op("dve", lambda e, a1=a1, ps=ps, t=t: e.tensor_tensor(out=a1[:], in0=ps[:],
                                                                              in1=cs[:, t * T:(t + 1) * T], op=ALU.mult),
                         reads=[pr, rtr], writes=[a1r])
                    P.op("dve", lambda e, a2=a2, ps2=ps2, t=t: e.tensor_tensor(out=a2[:], in0=ps2[:],
                                                                                in1=sn[:, t * T:(t + 1) * T], op=ALU.mult),
                         reads=[pr2, rtr], writes=[a2r])
                    P.op("dve", lambda e, a1=a1, a2=a2, row=row, t=t: e.tensor_tensor(
                        out=row[:, t * T:(t + 1) * T], in0=a1[:], in1=a2[:], op=ALU.add),
                        reads=[a1r, a2r], writes=[rr])

                prev = None
                for t in range(NT):
                    ps, pr = next_ps()
                    mm_group(ps, pr, wt, wr, KC, ci * 128, aT, aR[t], t)
                    z, zr, _ = zq.next()
                    P.op("act", lambda e, z=z, ps=ps: e.activation(out=z[:], in_=ps[:], func=AF.Copy),
                         reads=[pr], writes=[zr])
                    if prev is not None:
                        rot_tail(*prev, row=row, rr=rr)
                    prev = (ps, pr, z, zr, t)
                rot_tail(*prev, row=row, rr=rr)
                if DBG >= 6:
                    P.dma(qkT[oc], row[:], rsem, reads=[rr], writes=[P.R("qkT", oc)])
        vsl = Slots(2, [128, 32, 256], BF16, "vsb")
        for g, (win, dil) in enumerate(GROUPS if sub >= 2 else ()):
            wt, wr = wpipe.begin(6 + g)
            vt, vr, vsem = vsl.next()
            nb = S // (dil * 128)
            for r in range(dil):
                for n in range(nb):
                    bi = r * nb + n
                    if bi == 16:
                        wpipe.mid()
                    t0 = r + dil * 128 * n
                    ps, pr = next_ps()

                    def fn(e, ps=ps, t0=t0, dil=dil, wt=wt):
                        ins = None
                        for kc in range(KC):
                            ins = e.matmul(ps[:, 0:256], lhsT=aT[:, kc, t0:t0 + dil * 127 + 1:dil], rhs=wt[:, kc, :],
                                           start=(kc == 0), stop=(kc == KC - 1))
                        return ins
                    P.op("pe", fn, reads=[wr] + aR, writes=[pr])
                    P.op("act", lambda e, ps=ps, vt=vt, bi=bi: e.activation(out=vt[:, bi, :], in_=ps[:, 0:256],
                                                                          func=AF.Copy), reads=[pr], writes=[vr])
            P.dma(Vd[g], vt[:].rearrange("p b c -> p (b c)"), vsem, reads=[vr], writes=[P.R("Vd", g)])
        P.barrier()
        A.reset(m)

    def phase_conv_gates(l, aT, aR):
        m = A.mark()
        wl = WLoader()
        wsl = Slots(2, [128, KC, 256], BF16, "wcg")
        W = w_in[l]
        zxs = A.alloc([128, S], F32, "zxs")
        ub = A.alloc([128, S + 2], F32, "ub")
        rzx, rub = P.R("zxs"), P.R("ub")
        rows = Slots(2, [128, S], BF16, "ybrow")
        P.op("pool", lambda e: e.memset(ub[:, 0:2], 0.0), writes=[rub])
        blocks = []
        for c in range(4):
            blocks += [(C_X + c * 128, 128), (C_C + c * 128, 128), (C_B + c * 128, 128)]
        blocks += [(C_G + blk * 256, 256) for blk in range(12)]
        wpipe = WPipe(wl, wsl, W, blocks)
        for c in range(4):
            wt, wr = wpipe.begin(3 * c)
            for t in range(NT):
                if t == 4:
                    wpipe.mid()
                ps, pr = next_ps()
                mm_group(ps, pr, wt, wr, KC, 0, aT, aR[t], t)
                P.op("act", lambda e, ps=ps, t=t: e.activation(out=zxs[:, t * T:(t + 1) * T], in_=ps[:], func=AF.Copy),
                     reads=[pr], writes=[rzx])
            wt, wr = wpipe.begin(3 * c + 1)
            for t in range(NT):
                if t == 4:
                    wpipe.mid()
                ps, pr = next_ps()
                mm_group(ps, pr, wt, wr, KC, 0, aT, aR[t], t)
                P.op("dve", lambda e, ps=ps, t=t: e.tensor_tensor(out=ub[:, 2 + t * T:2 + (t + 1) * T], in0=ps[:],
                                                                  in1=zxs[:, t * T:(t + 1) * T], op=ALU.mult),
                     reads=[pr, rzx], writes=[rub])
            wt, wr = wpipe.begin(3 * c + 2)
            P.op("pool", lambda e, c=c: e.tensor_scalar(out=zxs[:], in0=ub[:, 0:S], scalar1=cwsb[:, l, c, 0:1],
                                                        scalar2=None, op0=ALU.mult),
                 reads=[rub, Rconst], writes=[rzx])
            P.op("dve", lambda e, c=c: e.scalar_tensor_tensor(out=zxs[:], in0=ub[:, 1:S + 1], scalar=cwsb[:, l, c, 1:2],
                                                              in1=zxs[:], op0=ALU.mult, op1=ALU.add),
                 reads=[rub, rzx, Rconst], writes=[rzx])
            P.op("dve", lambda e, c=c: e.scalar_tensor_tensor(out=zxs[:], in0=ub[:, 2:S + 2], scalar=cwsb[:, l, c, 2:3],
                                                              in1=zxs[:], op0=ALU.mult, op1=ALU.add),
                 reads=[rub, rzx, Rconst], writes=[rzx])
            row, rr, rsem = rows.next()
            for t in range(NT):
                if t == 4:
                    wpipe.mid()
                ps, pr = next_ps()
                mm_group(ps, pr, wt, wr, KC, 0, aT, aR[t], t)
                P.op("dve", lambda e, ps=ps, t=t, row=row: e.tensor_tensor(out=row[:, t * T:(t + 1) * T], in0=ps[:],
                                                                           in1=zxs[:, t * T:(t + 1) * T], op=ALU.mult),
                     reads=[pr, rzx], writes=[rr])
            P.dma(ybT[c], row[:], rsem, reads=[rr], writes=[P.R("ybT", c)])
        for blk in range(12):
            wt, wr = wpipe.begin(12 + blk)
            for ci in range(2):
                if ci == 1:
                    wpipe.mid()
                oc = blk * 2 + ci
                row, rr, rsem = rows.next()
                for t in range(NT):
                    ps, pr = next_ps()
                    mm_group(ps, pr, wt, wr, KC, ci * 128, aT, aR[t], t)
                    P.op("act", lambda e, ps=ps, t=t, row=row: e.activation(out=row[:, t * T:(t + 1) * T], in_=ps[:],
                                                                            func=AF.Sigmoid), reads=[pr], writes=[rr])
                P.dma(gT[oc], row[:], rsem, reads=[rr], writes=[P.R("gT", oc)])
        P.barrier()
        A.reset(m)

    def phase_sgu(l, aT, aR):
        m = A.mark()
        wl = WLoader()
        W = w_in[l]
        uT = A.alloc([128, 4, S], BF16, "uT")
        ruT = [P.R("uT", b) for b in range(32)]
        wsl = Slots(2, [128, KC, 256], BF16, "wsu")
        s0 = dsem()
        lngs = A.alloc([128, 512], F32, "lng")
        lnbs = A.alloc([128, 512], F32, "lnb")
        sgbs = A.alloc([128, 512], F32, "sgb")
        wsf = A.alloc([128, 4, 128], F32, "wsf")
        wsb = A.alloc([128, 4, 128], BF16, "wsb")
        rc = P.R("sgconst")
        P.dma(lngs[:], lng[l].broadcast_to([128, 512]), s0, writes=[rc])
        P.dma(lnbs[:], lnb[l].broadcast_to([128, 512]), s0, writes=[rc])
        P.dma(sgbs[:], sgb[l].broadcast_to([128, 512]), s0, writes=[rc])
        P.dma(wsf[:], sgwT[l].rearrange("g s t -> s g t"), s0, writes=[rc])
        P.op("dve", lambda e: e.tensor_tensor(out=wsb[:], in0=wsf[:],
                                              in1=cmf[:, 3:4, :].broadcast_to([128, 4, 128]), op=ALU.mult),
             reads=[rc, Rconst], writes=[rc])
        wpipe = WPipe(wl, wsl, W, [(C_SU + blk * 256, 256) for blk in range(2)])
        for blk in range(2):
            wt, wr = wpipe.begin(blk)
            for ci in range(2):
                if ci == 1:
                    wpipe.mid()
                c = blk * 2 + ci
                for t in range(NT):
                    ps, pr = next_ps()
                    mm_group(ps, pr, wt, wr, KC, ci * 128, aT, aR[t], t)
                    P.op("act", lambda e, ps=ps, t=t, c=c: e.activation(out=uT[:, c, t * T:(t + 1) * T], in_=ps[:],
                                                                        func=AF.Gelu),
                         reads=[pr], writes=ruT[4 * t:4 * t + 4])
        wv = A.alloc([128, KC, 512], BF16, "wv")
        rwv = P.R("wv")
        wl.load(W, C_SV, 512, wv, rwv)
        gvs = Slots(2, [128, 4, 512], F32, "gv")
        sqs = Slots(1, [128, 4, 512], F32, "gsq")
        vbs = Slots(2, [128, 4, 512], BF16, "vb")
        sts = Slots(2, [128, 8, 4], F32, "st")
        tts = Slots(2, [128, 512], F32, "tt")
        def stage_a(t):
            gv, gr, _ = gvs.next()
            sq, sr, _ = sqs.next()
            vb, vbr, _ = vbs.next()
            st, str_, _ = sts.next()
            for bb in range(4):
                b = 4 * t + bb
                ps, pr = next_ps(0, 4)

                def fn(e, ps=ps, b=b):
                    ins = None
                    for kc in range(KC):
                        ins = e.matmul(ps[:], lhsT=aT[:, kc, b * 128:(b + 1) * 128], rhs=wv[:, kc, :],
                                       start=(kc == 0), stop=(kc == KC - 1))
                    return ins
                P.op("pe", fn, reads=[rwv, aR[t]], writes=[pr])
                P.op("act", lambda e, gv=gv, ps=ps, bb=bb: e.activation(out=gv[:, bb, :], in_=ps[:], func=AF.Gelu),
                     reads=[pr], writes=[gr])
            P.op("act", lambda e, gv=gv, sq=sq: e.activation(out=sq[:], in_=gv[:], func=AF.Square), reads=[gr], writes=[sr])
            P.op("dve", lambda e, gv=gv, st=st: e.tensor_reduce(out=st[:, 0, :], in_=gv[:], axis=AX.X, op=ALU.add),
                 reads=[gr], writes=[str_])
            P.op("dve", lambda e, sq=sq, st=st: e.tensor_reduce(out=st[:, 1, :], in_=sq[:], axis=AX.X, op=ALU.add),
                 reads=[sr, str_], writes=[str_])
            P.op("dve", lambda e, st=st: e.tensor_scalar(out=st[:, 2:4, :], in0=st[:, 0:2, :], scalar1=1.0 / 512, scalar2=None,
                                                         op0=ALU.mult), reads=[str_], writes=[str_])
            P.op("dve", lambda e, st=st: e.tensor_tensor(out=st[:, 4, :], in0=st[:, 2, :], in1=st[:, 2, :], op=ALU.mult),
                 reads=[str_], writes=[str_])
            P.op("dve", lambda e, st=st: e.scalar_tensor_tensor(out=st[:, 5, :], in0=st[:, 3, :], scalar=LN_EPS, in1=st[:, 4, :],
                                                                op0=ALU.add, op1=ALU.subtract), reads=[str_], writes=[str_])
            P.op("act", lambda e, st=st: e.activation(out=st[:, 6, :], in_=st[:, 5, :], func=AF.Sqrt), reads=[str_], writes=[str_])
            P.op("dve", lambda e, st=st: e.reciprocal(out=st[:, 7, :], in_=st[:, 6, :]), reads=[str_], writes=[str_])
            for bb in range(4):
                P.op("dve", lambda e, st=st, gv=gv, bb=bb: e.tensor_scalar(
                    out=gv[:, bb, :], in0=gv[:, bb, :], scalar1=st[:, 2, bb:bb + 1], scalar2=st[:, 7, bb:bb + 1],
                    op0=ALU.subtract, op1=ALU.mult), reads=[gr, str_], writes=[gr])
            P.op("pool", lambda e, gv=gv: e.tensor_tensor(out=gv[:, 0:2, :], in0=gv[:, 0:2, :],
                                                          in1=lngs[:].unsqueeze(1).broadcast_to([128, 2, 512]), op=ALU.mult),
                 reads=[gr, rc], writes=[gr])
            P.op("dve", lambda e, gv=gv: e.tensor_tensor(out=gv[:, 2:4, :], in0=gv[:, 2:4, :],
                                                         in1=lngs[:].unsqueeze(1).broadcast_to([128, 2, 512]), op=ALU.mult),
                 reads=[gr, rc], writes=[gr])
            P.op("pool", lambda e, gv=gv, vb=vb: e.tensor_tensor(out=vb[:, 0:2, :], in0=gv[:, 0:2, :],
                                                                 in1=lnbs[:].unsqueeze(1).broadcast_to([128, 2, 512]), op=ALU.add),
                 reads=[gr, rc], writes=[vbr])
            P.op("dve", lambda e, gv=gv, vb=vb: e.tensor_tensor(out=vb[:, 2:4, :], in0=gv[:, 2:4, :],
                                                                in1=lnbs[:].unsqueeze(1).broadcast_to([128, 2, 512]), op=ALU.add),
                 reads=[gr, rc], writes=[vbr])
            return vb, vbr

        def stage_b(t, vb, vbr):
            for bb in range(4):
                b = 4 * t + bb
                ps2, pr2 = next_ps(4, 8)

                def fn2(e, ps2=ps2, vb=vb, bb=bb):
                    ins = None
                    for g in range(4):
                        ins = e.matmul(ps2[:, g * 128:(g + 1) * 128], lhsT=vb[:, bb, g * 128:(g + 1) * 128], rhs=wsb[:, g, :],
                                       start=True, stop=True)
                    return ins
                P.op("pe", fn2, reads=[vbr, rc], writes=[pr2])
                tt, ttr, _ = tts.next()
                P.op("dve", lambda e, tt=tt, ps2=ps2: e.tensor_tensor(out=tt[:], in0=ps2[:], in1=sgbs[:], op=ALU.add),
                     reads=[pr2, rc], writes=[ttr])
                ub_ = uT[:, :, b * 128:(b + 1) * 128]
                P.op("dve", lambda e, tt=tt, ub_=ub_: e.tensor_tensor(out=ub_, in0=tt[:].rearrange("p (g t) -> p g t", g=4),
                                                                      in1=ub_, op=ALU.mult),
                     reads=[ttr, ruT[b]], writes=[ruT[b]])

        prev = stage_a(0)
        for t in range(NT):
            nxt_ = stage_a(t + 1) if t + 1 < NT else None
            stage_b(t, *prev)
            prev = nxt_
        s1 = dsem()
        for c in range(4):
            P.dma(ycT[c], uT[:, c, :], s1, reads=ruT, writes=[P.R("ycT", c)])
        P.barrier()
        A.reset(m)

    def phase_attn(l):
        m = A.mark()
        U = A.alloc([128, 6, S], BF16, "U")
        Dn = A.alloc([128, 2, S], F32, "Dn")
        rU = [P.R("U", t) for t in range(NT)]
        rD = [P.R("Dn", t) for t in range(NT)]
        qs = A.alloc([128, 2, S], BF16, "qs")
        ks = A.alloc([128, 2, S], BF16, "ks")
        vs = A.alloc([128, 32, 256], BF16, "vs")
        rq = [P.R("qs", 0), P.R("qs", 1)]
        rk = [P.R("ks", 0), P.R("ks", 1)]
        rv = P.R("vs")
        sq_ = [dsem(), dsem()]
        sk_ = [dsem(), dsem()]
        sv_ = dsem()
        pts = [Slots(2, [128, 512], BF16, "pT%d" % i) for i in range(2)]
        s0 = dsem()
        order = (2, 1, 0)
        for gi, g in enumerate(order):
            win, dil = GROUPS[g]
            nb = S // (dil * 128)
            for cc in range(2):
                P.dma(qs[:, cc, :], qkT[2 * g + cc], sq_[cc], reads=[P.R("qkT", 2 * g + cc)], writes=[rq[cc]])
                P.dma(ks[:, cc, :], qkT[6 + 2 * g + cc], sk_[cc], reads=[P.R("qkT", 6 + 2 * g + cc)], writes=[rk[cc]])
            P.dma(vs[:].rearrange("p b c -> p (b c)"), Vd[g], sv_, reads=[P.R("Vd", g)], writes=[rv])
            def scores(r, n):
                t0 = r + dil * 128 * n
                span = dil * 128
                qsl = slice(t0, t0 + dil * 127 + 1, dil)
                if (r * nb + n) % 4 == 1:
                    bg_pump(1)
                kbs = []
                if n > 0:
                    kbs.append((0, slice(t0 - span, t0 - span + dil * 127 + 1, dil), r * nb + n - 1))
                kbs.append((1, qsl, r * nb + n))
                lo = 0 if n > 0 else 256
                psE, prE = next_ps(0, 6)
                psO, prO = next_ps(0, 6)

                def fn(e, psE=psE, psO=psO, kbs=kbs, qsl=qsl):
                    ins = None
                    for (mi, ksl, vbi) in kbs:
                        for ps in (psE, psO):
                            e.matmul(ps[:, mi * 256:(mi + 1) * 256], lhsT=ident, rhs=maskb[:, mi, 0:256],
                                     start=True, stop=False, skip_group_check=True)
                    for (mi, ksl, vbi) in kbs:
                        for jj in range(2):
                            for par, ps in ((0, psE), (1, psO)):
                                pb = par * 64
                                c0 = mi * 256 + jj * 128
                                ins = e.matmul(ps[:, c0:c0 + 128], lhsT=ks[pb:pb + 64, jj, ksl],
                                               rhs=qs[pb:pb + 64, jj, qsl], start=False, stop=True,
                                               skip_group_check=True)
                    return ins
                P.op("pe", fn, reads=rq + rk + [Rconst], writes=[prE, prO])
                ptE, ptEr, _ = pts[0].next()
                ptO, ptOr, _ = pts[1].next()
                P.op("act", lambda e, pt=ptE, ps=psE, lo=lo: e.activation(out=pt[:, lo:512], in_=ps[:, lo:512],
                                                                         func=AF.Exp, scale=0.125),
                     reads=[prE], writes=[ptEr])
                P.op("act", lambda e, pt=ptO, ps=psO, lo=lo: e.activation(out=pt[:, lo:512], in_=ps[:, lo:512],
                                                                         func=AF.Exp, scale=0.125),
                     reads=[prO], writes=[ptOr])
                return (t0, qsl, kbs, ptE, ptEr, ptO, ptOr)

            def pv(st):
                t0, qsl, kbs, ptE, ptEr, ptO, ptOr = st
                pso, pro = next_ps(6, 8)

                def fn3(e, pso=pso, kbs=kbs, ptE=ptE, ptO=ptO):
                    ins = None
                    nk = len(kbs)
                    for j in range(4):
                        pb = (j % 2) * 64
                        pr_ = j // 2
                        pt = ptE if j % 2 == 0 else ptO
                        kw = {"tile_position": (0, pb)} if pb else {}
                        for i, (mi, ksl, vbi) in enumerate(kbs):
                            c0 = mi * 256 + pr_ * 128
                            ins = e.matmul(pso[pb:pb + 64, pr_ * 128:(pr_ + 1) * 128],
                                           lhsT=vs[:, vbi, j * 64:(j + 1) * 64], rhs=pt[:, c0:c0 + 128],
                                           start=(i == 0), stop=(i == nk - 1), skip_group_check=True, **kw)
                        for i, (mi, ksl, vbi) in enumerate(kbs):
                            c0 = mi * 256 + pr_ * 128
                            ins = e.matmul(pso[pb:pb + 64, 256 + pr_ * 128:256 + (pr_ + 1) * 128],
                                           lhsT=onesb[:, 0:64], rhs=pt[:, c0:c0 + 128],
                                           start=(i == 0), stop=(i == nk - 1), skip_group_check=True, **kw)
                    return ins
                P.op("pe", fn3, reads=[rv, Rconst, ptEr, ptOr], writes=[pro])
                tiles = sorted(set(range(t0 // T, (t0 + dil * 127) // T + 1)))
                P.op("act", lambda e, pso=pso, g=g, qsl=qsl: e.activation(
                    out=U[:, 2 * g:2 * g + 2, qsl], in_=pso[:, 0:256].rearrange("p (a q) -> p a q", a=2), func=AF.Copy),
                    reads=[pro], writes=[rU[t] for t in tiles])
                if gi == 0:
                    P.op("dve", lambda e, pso=pso, qsl=qsl: e.tensor_copy(
                        out=Dn[:, :, qsl], in_=pso[:, 256:512].rearrange("p (a q) -> p a q", a=2)),
                        reads=[pro], writes=[rD[t] for t in tiles])
                else:
                    P.op("dve", lambda e, pso=pso, qsl=qsl: e.tensor_tensor(
                        out=Dn[:, :, qsl], in0=pso[:, 256:512].rearrange("p (a q) -> p a q", a=2),
                        in1=Dn[:, :, qsl], op=ALU.add),
                        reads=[pro] + [rD[t] for t in tiles], writes=[rD[t] for t in tiles])

            prev = None
            for r in range(dil):
                for n in range(nb):
                    cur_ = scores(r, n)
                    if prev is not None:
                        pv(prev)
                    prev = cur_
            pv(prev)
        s1 = dsem()
        for t in range(NT):
            sl = slice(t * T, (t + 1) * T)
            P.op("act", lambda e, sl=sl: e.activation(out=Dn[:, :, sl], in_=Dn[:, :, sl], func=AF.Ln), reads=[rD[t]], writes=[rD[t]])
            P.op("act", lambda e, sl=sl: e.activation(out=Dn[:, :, sl], in_=Dn[:, :, sl], func=AF.Exp, scale=-1.0),
                 reads=[rD[t]], writes=[rD[t]])
            for g in range(3):
                eng = "pool" if g == 1 else "dve"
                P.op(eng, lambda e, sl=sl, g=g: e.tensor_tensor(out=U[:, 2 * g:2 * g + 2, sl], in0=U[:, 2 * g:2 * g + 2, sl],
                                                                in1=Dn[:, :, sl], op=ALU.mult),
                     reads=[rU[t], rD[t]], writes=[rU[t]])
        for c in range(6):
            P.dma(yaT[c], U[:, c, :], s1, reads=rU, writes=[P.R("yaT", c)])
        P.barrier()
        A.reset(m)

    def alloc_merge_w(l):
        wa = A.alloc([128, 6, D], BF16, "wa")
        wb_ = A.alloc([128, 4, D], BF16, "wb")
        wc_ = A.alloc([128, 4, D], BF16, "wc")
        wo = A.alloc([128, 8, D], BF16, "wo")
        rw = P.R("wmerge", l)
        bg_queue(w_a[l], wa, rw)
        bg_queue(w_b[l], wb_, rw)
        bg_queue(w_c[l], wc_, rw)
        bg_queue(w_out[l], wo, rw)
        return wa, wb_, wc_, wo, rw

    def phase_merge(l, hsrc, wts):
        m = A.mark()
        wa, wb_, wc_, wo, rw = wts
        bg_flush()
        ysl = Slots(2, [128, 14, T], BF16, "ys")
        gsl = Slots(2, [128, 3, 4, T], BF16, "gs")
        csl = Slots(2, [128, KC, T], BF16, "ct")
        ntmp = norm_tmp()
        hsl = Slots(2, [128, KC, T], F32, "hs")
        mT = A.alloc([128, KC, T], BF16, "mT")
        rm = P.R("mT")
        tsl = [Slots(2, [128, T], F32, "mt%d" % i) for i in range(3)]
        def load(t):
            sl = slice(t * T, (t + 1) * T)
            yt, yr, ysem = ysl.next()
            hs, hr, hsem = hsl.next()
            P.dma(yt[:, 0:6, :], yaT.rearrange("c p s -> p c s")[:, :, sl], ysem,
                  reads=[P.R("yaT", c) for c in range(6)], writes=[yr])
            P.dma(yt[:, 6:10, :], ybT.rearrange("c p s -> p c s")[:, :, sl], ysem,
                  reads=[P.R("ybT", c) for c in range(4)], writes=[yr])
            P.dma(yt[:, 10:14, :], ycT.rearrange("c p s -> p c s")[:, :, sl], ysem,
                  reads=[P.R("ycT", c) for c in range(4)], writes=[yr])
            P.dma(hs[:], hview(hsrc, t), hsem, reads=[P.R("hT", t)], writes=[hr])
            return yt, yr, hs, hr, hsem

        def loadg(t, half):
            sl = slice(t * T, (t + 1) * T)
            gt, gr, gsem = gsl.next()
            for b3 in range(3):
                P.dma(gt[:, b3, :, :], gT.rearrange("c p s -> p c s")[:, b3 * 8 + half * 4:b3 * 8 + half * 4 + 4, sl], gsem,
                      reads=[P.R("gT", c) for c in range(b3 * 8 + half * 4, b3 * 8 + half * 4 + 4)],
                      writes=[P.R("gsq", id(gt), b3)])
            return gt, [P.R("gsq", id(gt), b3) for b3 in range(3)]

        nxt = load(0)
        gnx = loadg(0, 0)
        pend = None

        def finish_norm(pd):
            hs_, hr_, t_ = pd
            ct, ctr, ctsem = csl.next()
            norm_apply(hs_, hr_, l * 3 + 1, lambda c, ct=ct: ct[:, c, :], ctr, ntmp)
            P.dma(cTd.rearrange("c p s -> p c s")[:, :, t_ * T:(t_ + 1) * T], ct[:], ctsem, reads=[ctr],
                  writes=[P.R("cTd", t_)])

        for t in range(NT):
            yt, yr, hs, hr, hsem = nxt
            for oc in range(KC):
                if oc % 4 == 0:
                    gt, grl = gnx
                    if oc == 0:
                        gnx = loadg(t, 1)
                    elif t + 1 < NT:
                        gnx = loadg(t + 1, 0)
                prods = []
                for bi, (wt, k0, kn) in enumerate(((wa, 0, 6), (wb_, 6, 4), (wc_, 10, 4))):
                    ps, pr = next_ps()

                    def fn(e, ps=ps, wt=wt, k0=k0, kn=kn, yt=yt, oc=oc):
                        ins = None
                        for kc in range(kn):
                            ins = e.matmul(ps[:], lhsT=wt[:, kc, oc * 128:(oc + 1) * 128], rhs=yt[:, k0 + kc, :],
                                           start=(kc == 0), stop=(kc == kn - 1))
                        return ins
                    P.op("pe", fn, reads=[rw, yr], writes=[pr])
                    tt, ttr, _ = tsl[bi].next()
                    P.op("dve", lambda e, tt=tt, ps=ps, gt=gt, bi=bi, oc=oc: e.tensor_tensor(
                        out=tt[:], in0=ps[:], in1=gt[:, bi, oc % 4, :], op=ALU.mult), reads=[pr] + grl, writes=[ttr])
                    prods.append((tt, ttr))
                (ta, ra), (tb, rb), (tc, rc_) = prods
                P.op("dve", lambda e, ta=ta, tb=tb: e.tensor_tensor(out=ta[:], in0=ta[:], in1=tb[:], op=ALU.add),
                     reads=[ra, rb], writes=[ra])
                P.op("pool", lambda e, ta=ta, tc=tc, oc=oc: e.tensor_tensor(out=mT[:, oc, :], in0=ta[:], in1=tc[:], op=ALU.add),
                     reads=[ra, rc_], writes=[rm])
                if oc == 0 and pend is not None:
                    norm_rstd(ntmp)
                if oc == 1 and pend is not None:
                    finish_norm(pend)
                    pend = None
                if oc == 2 and t + 1 < NT:
                    nxt = load(t + 1)
            for oc in range(KC):
                ps, pr = next_ps()

                def fn(e, ps=ps, oc=oc):
                    ins = None
                    for kc in range(KC):
                        ins = e.matmul(ps[:], lhsT=wo[:, kc, oc * 128:(oc + 1) * 128], rhs=mT[:, kc, :],
                                       start=(kc == 0), stop=(kc == KC - 1))
                    return ins
                P.op("pe", fn, reads=[rw, rm], writes=[pr])
                P.op("dve", lambda e, ps=ps, hs=hs, oc=oc: e.tensor_tensor(out=hs[:, oc, :], in0=ps[:], in1=hs[:, oc, :],
                                                                           op=ALU.add), reads=[pr, hr], writes=[hr])
            P.dma(hview(hT, t), hs[:], hsem, reads=[hr], writes=[P.R("hT", t)])
            norm_square(hs, hr, ntmp)
            pend = (hs, hr, t)
        norm_rstd(ntmp)
        finish_norm(pend)
        P.barrier()
        A.reset(m)

    def phase_ffn_up(l, cT, cR):
        m = A.mark()
        wl = WLoader()
        wsl = Slots(2, [128, KC, 256], BF16, "wup")
        rows = Slots(2, [128, S], BF16, "frow")
        rl = Slots(3, [128, T], F32, "relu")
        wpipe = WPipe(wl, wsl, w_up[l], [(blk * 256, 256) for blk in range(16)])
        for blk in range(16):
            wt, wr = wpipe.begin(blk)
            for ci in range(2):
                if ci == 1:
                    wpipe.mid()
                fc = blk * 2 + ci
                bg_pump(1)
                row, rr, rsem = rows.next()
                for t in range(NT):
                    ps, pr = next_ps()
                    mm_group(ps, pr, wt, wr, KC, ci * 128, cT, cR[t], t)
                    rt, rtr, _ = rl.next()
                    P.op("act", lambda e, rt=rt, ps=ps: e.activation(out=rt[:], in_=ps[:], func=AF.Relu), reads=[pr], writes=[rtr])
                    P.op("dve", lambda e, rt=rt, row=row, t=t: e.tensor_tensor(out=row[:, t * T:(t + 1) * T], in0=rt[:],
                                                                               in1=rt[:], op=ALU.mult),
                         reads=[rtr], writes=[rr])
                P.dma(fT[fc], row[:], rsem, reads=[rr], writes=[P.R("fT", fc)])
        P.barrier()
        A.reset(m)

    def phase_ffn_down(l, wd, rw):
        m = A.mark()
        bg_flush_except = None
        fsl = Slots(2, [128, 32, T], BF16, "fs")
        hsl = Slots(2, [128, KC, T], F32, "hs")
        def load(t):
            sl = slice(t * T, (t + 1) * T)
            ft, fr, fsem = fsl.next()
            hs, hr, hsem = hsl.next()
            bg_pump(1)
            for q4 in range(4):
                P.dma(ft[:, 8 * q4:8 * q4 + 8, :], fT.rearrange("c p s -> p c s")[:, 8 * q4:8 * q4 + 8, sl], fsem,
                      reads=[P.R("fT", c) for c in range(8 * q4, 8 * q4 + 8)], writes=[P.R("fsq", id(ft), q4)])
            P.dma(hs[:], hview(hT, t), hsem, reads=[P.R("hT", t)], writes=[hr])
            return ft, [P.R("fsq", id(ft), q4) for q4 in range(4)], hs, hr, hsem

        nxt = load(0)
        for t in range(NT):
            ft, frl, hs, hr, hsem = nxt
            if t + 1 < NT:
                nxt = load(t + 1)
            for oc in range(KC):
                ps, pr = next_ps()

                def fn(e, ps=ps, oc=oc, ft=ft):
                    ins = None
                    for kc in range(32):
                        ins = e.matmul(ps[:], lhsT=wd[:, kc, oc * 128:(oc + 1) * 128], rhs=ft[:, kc, :],
                                       start=(kc == 0), stop=(kc == 31))
                    return ins
                P.op("pe", fn, reads=[rw] + frl, writes=[pr])
                P.op("dve", lambda e, ps=ps, hs=hs, oc=oc: e.tensor_tensor(out=hs[:, oc, :], in0=ps[:], in1=hs[:, oc, :],
                                                                           op=ALU.add), reads=[pr, hr], writes=[hr])
            P.dma(hview(hT, t), hs[:], hsem, reads=[hr], writes=[P.R("hT", t)])
        P.barrier()
        A.reset(m)

    def phase_ple(l, wg, wp, rw, dead_lo, fuse):
        m = A.mark()
        bg_flush()
        keep = A.off
        A.off = dead_lo
        hsl_ = Slots(3, [128, KC, T], F32, "hs")
        assert A.off <= dead_lo + 64 * 1024
        A.off = keep
        if fuse == "final":
            osl = Slots(2, [128, KC, T], F32, "os")
        elif fuse == "next":
            osl = Slots(2, [128, KC, T], BF16, "as")
        hsl = hsl_
        psl = Slots(2, [128, 2, T], F32, "pf")
        pbs = Slots(2, [128, 2, T], BF16, "pb")
        eTs = [A.alloc([128, KC, T], BF16, "eT") for _ in range(2)]
        reT = [P.R("eT", l, i) for i in range(2)]
        tmp3 = norm_tmp()
        tmpn = norm_tmp()
        sgs = Slots(2, [128, T], F32, "sg")
        tts = Slots(2, [128, T], F32, "pt")

        def load(t):
            sl = slice(t * T, (t + 1) * T)
            hs, hr, hsem = hsl.next()
            pf, pfr, pfsem = psl.next()
            P.dma(hs[:], hview(hT, t), hsem, reads=[P.R("hT", t)], writes=[hr])
            P.dma(pf[:], pT[l].rearrange("(c p) s -> p c s", p=128)[:, :, sl], pfsem, writes=[pfr])
            pb, pbr, _ = pbs.next()
            P.op("pool", lambda e, pb=pb, pf=pf: e.tensor_copy(out=pb[:], in_=pf[:]), reads=[pfr], writes=[pbr])
            return hs, hr, hsem, pb, pbr

        def finish_fused(pd):
            hs_, hr_, t_ = pd
            ot, otr, otsem = osl.next()
            if fuse == "next":
                norm_apply(hs_, hr_, (l + 1) * 3 + 0, lambda c, ot=ot: ot[:, c, :], otr, tmpn)
                P.dma(aTd.rearrange("c p s -> p c s")[:, :, t_ * T:(t_ + 1) * T], ot[:], otsem, reads=[otr],
                      writes=[P.R("aTd", t_)])
            else:
                norm_apply(hs_, hr_, NL * 3, lambda c, ot=ot: ot[:, c, :], otr, tmpn)
                P.dma(hview(outT, t_), ot[:], otsem, reads=[otr], writes=[P.R("outT", t_)])

        cur = load(0)
        norm_square(cur[0], cur[1], tmp3)
        norm_rstd(tmp3)
        norm_apply(cur[0], cur[1], l * 3 + 2, lambda c: eTs[0][:, c, :], reT[0], tmp3)
        pend = None
        nxt = None
        for t in range(NT):
            hs, hr, hsem, pb, pbr = cur
            eT, re_ = eTs[t % 2], reT[t % 2]
            if t + 1 < NT:
                nxt = load(t + 1)
            for oc in range(KC):
                ps, pr = next_ps()

                def fn(e, ps=ps, oc=oc, eT=eT):
                    ins = None
                    for kc in range(KC):
                        ins = e.matmul(ps[:], lhsT=wg[:, kc, oc * 128:(oc + 1) * 128], rhs=eT[:, kc, :],
                                       start=(kc == 0), stop=(kc == KC - 1))
                    return ins
                P.op("pe", fn, reads=[rw, re_], writes=[pr])
                ps2, pr2 = next_ps()

                def fn2(e, ps2=ps2, oc=oc, pb=pb):
                    ins = None
                    for kc in range(2):
                        ins = e.matmul(ps2[:], lhsT=wp[:, kc, oc * 128:(oc + 1) * 128], rhs=pb[:, kc, :],
                                       start=(kc == 0), stop=(kc == 1))
                    return ins
                P.op("pe", fn2, reads=[rw, pbr], writes=[pr2])
                sg, sgr, _ = sgs.next()
                tt, ttr, _ = tts.next()
                P.op("act", lambda e, sg=sg, ps=ps: e.activation(out=sg[:], in_=ps[:], func=AF.Sigmoid), reads=[pr], writes=[sgr])
                P.op("dve", lambda e, tt=tt, ps2=ps2, sg=sg: e.tensor_tensor(out=tt[:], in0=ps2[:], in1=sg[:], op=ALU.mult),
                     reads=[pr2, sgr], writes=[ttr])
                P.op("dve", lambda e, tt=tt, hs=hs, oc=oc: e.tensor_tensor(out=hs[:, oc, :], in0=tt[:], in1=hs[:, oc, :],
                                                                           op=ALU.add), reads=[ttr, hr], writes=[hr])
                if oc == 0 and pend is not None:
                    norm_rstd(tmpn)
                if oc == 1 and pend is not None:
                    finish_fused(pend)
                    pend = None
                if oc == 2 and t + 1 < NT:
                    norm_square(nxt[0], nxt[1], tmp3)
                if oc == 4 and t + 1 < NT:
                    norm_rstd(tmp3)
                if oc == 6 and t + 1 < NT:
                    norm_apply(nxt[0], nxt[1], l * 3 + 2, lambda c, t=t: eTs[(t + 1) % 2][:, c, :], reT[(t + 1) % 2], tmp3)
            if fuse != "final":
                P.dma(hview(hT, t), hs[:], hsem, reads=[hr], writes=[P.R("hT", t)])
            if fuse is not None:
                norm_square(hs, hr, tmpn)
                pend = (hs, hr, t)
            cur = nxt
        if pend is not None:
            norm_rstd(tmpn)
            finish_fused(pend)
        P.barrier()
        A.reset(m)

    setup()
    wrote_h = False
    fused_final = False
    for l in range(nl):
        m = A.mark()
        aT = A.alloc([128, KC, S], BF16, "aT")
        aR = [P.R("aT", l, t) for t in range(NT)]
        if l == 0:
            norm_phase(xT, 0, aT, aR)
        else:
            load_act(aTd, aT, aR, "aTd")
        phase_qkv(l, aT, aR)
        phase_conv_gates(l, aT, aR)
        phase_sgu(l, aT, aR)
        A.reset(m)
        wts = alloc_merge_w(l)
        phase_attn(l)
        phase_merge(l, xT if l == 0 else hT, wts)
        wrote_h = True
        bg_flush()
        A.reset(m)
        wd = A.alloc([128, 32, D], BF16, "wd")
        rwd = P.R("wd", l)
        bg_queue(w_down[l], wd, rwd)
        m1 = A.mark()
        cT = A.alloc([128, KC, S], BF16, "cT")
        cR = [P.R("cT", l, t) for t in range(NT)]
        load_act(cTd, cT, cR, "cTd")
        phase_ffn_up(l, cT, cR)
        bg_flush()
        A.reset(m1)
        wg = A.alloc([128, 8, D], BF16, "wg")
        wp = A.alloc([128, 2, D], BF16, "wp")
        rwp = P.R("wple", l)
        bg_queue(w_pg[l], wg, rwp)
        bg_queue(w_pe[l], wp, rwp)
        phase_ffn_down(l, wd, rwd)
        last = (l == nl - 1)
        fuse = ("final" if (final_norm and nl == NL) else None) if last else "next"
        fused_final = fused_final or fuse == "final"
        phase_ple(l, wg, wp, rwp, m, fuse)
        bg_flush()
        A.reset(m)
    if not fused_final:
        final_phase(hT if wrote_h else xT, NL * 3)
    P.emit()
    return nc, P


def _consts():
    ident = np.eye(128, dtype=np.float32)
    rot = np.zeros((128, 128), np.float32)
    for base in (0, 64):
        for i in range(8):
            rot[base + i, base + 8 + i] = -1.0
            rot[base + 8 + i, base + i] = 1.0
    rotT = rot.T.copy()
    ones = np.ones((128, 128), np.float32)
    s = np.arange(128)[:, None]
    t = np.arange(128)[None, :]
    tril = (s <= t).astype(np.float32)
    cmat = np.stack([ident, rotT, ones, tril], axis=1).astype(np.float32)
    k = np.arange(128)[:, None]
    q = np.arange(128)[None, :]
    prev = np.where(k >= q, 0.0, MASKV).astype(np.float32)
    cur = np.where(k <= q, 0.0, MASKV).astype(np.float32)
    cmask = np.stack([np.tile(prev, (1, 4)), np.tile(cur, (1, 4))], axis=1).astype(np.float32)
    inv_freq = (np.float32(500000.0) ** (-(np.arange(0, 16, 2, dtype=np.float32) / np.float32(16)))).astype(np.float32)
    invf = np.zeros((128, 1), np.float32)
    for p in range(128):
        if p % 64 < 16:
            invf[p, 0] = inv_freq[(p % 64) % 8]
    return cmat, cmask, invf


_CACHE = {}


def _prep_shared(inp):
    f = lambda a: np.ascontiguousarray(np.asarray(a, dtype=np.float32))
    cmat, cmask, invf = _consts()
    gl = []
    for l in range(NL):
        for nm in ("norm_mix_g", "norm_mlp_g", "norm_ple_g"):
            gl.append(np.asarray(inp[nm][l], np.float32).reshape(KC, 128).T)
    gl.append(np.asarray(inp["norm_final_g"], np.float32).reshape(KC, 128).T)
    gains = np.ascontiguousarray(np.stack(gl, axis=1))
    cw = np.asarray(inp["conv_w"], np.float32)
    convw = np.ascontiguousarray(cw.reshape(NL, 3, 4, 128).transpose(3, 0, 2, 1))
    sgwT = np.ascontiguousarray(np.asarray(inp["sg_w"], np.float32).transpose(0, 1, 3, 2))
    shared = {
        "w_in": f(inp["w_in"]), "w_a": f(inp["w_branch_a"]), "w_b": f(inp["w_branch_b"]),
        "w_c": f(inp["w_branch_c"]), "w_out": f(inp["w_out"]), "w_up": f(inp["w_up"]),
        "w_down": f(inp["w_down"]), "w_pg": f(inp["w_ple_gate"]), "w_pe": f(inp["w_ple_proj"]),
        "sgwT": sgwT, "gains": gains, "convw": convw,
        "lng": f(inp["sg_ln_g"]).reshape(NL, 1, 512), "lnb": f(inp["sg_ln_b"]).reshape(NL, 1, 512),
        "sgb": f(inp["sg_b"]).reshape(NL, 1, 512),
        "cmat": cmat, "cmask": cmask, "invf": invf,
    }
    return shared


def run(inp, nl=NL, final_norm=True, cores=8, trace=False, stop=99):
    key = (nl, final_norm, stop)
    if key not in _CACHE:
        _CACHE[key] = build_program(nl, final_norm, stop)[0]
    nc = _CACHE[key]
    shared = _prep_shared(inp)
    x = np.asarray(inp["x"], np.float32)
    p = np.asarray(inp["p"], np.float32)
    posn = np.asarray(inp["positions"], np.int32)
    in_maps = []
    for b in range(cores):
        mp = dict(shared)
        mp["xT"] = np.ascontiguousarray(x[b].T)
        mp["pT"] = np.ascontiguousarray(p[:, b].transpose(0, 2, 1))
        mp["pos"] = np.ascontiguousarray(posn[b].reshape(1, S))
        in_maps.append(mp)
    res = run_bass_kernel_spmd(nc, in_maps, core_ids=list(range(cores)), **({"trace": True} if trace else {}))
    out = np.stack([np.ascontiguousarray(r["outT"].T) for r in res.results], axis=0)
    return out.astype(np.float32), res


def kernel(**inputs):
    out, _ = run(inputs)
    return out
```

```python
import math
from contextlib import ExitStack

import numpy as np
import concourse.bass as bass
import concourse.mybir as mybir
from concourse.bass_utils import run_bass_kernel_spmd

F32 = mybir.dt.float32
BF16 = mybir.dt.bfloat16
I32 = mybir.dt.int32
AF = mybir.ActivationFunctionType
ALU = mybir.AluOpType
AX = mybir.AxisListType

S = 4096
D = 1024
T = 512
NT = S // T
KC = D // 128
NL = 4
INW = 7936
DFF = 4096
PLE = 256
GROUPS = ((128, 1), (512, 4), (2048, 16))
RMS_EPS = 1e-6
LN_EPS = 1e-5
MASKV = -240000.0
C_Q, C_K, C_V = 0, 768, 1536
C_X, C_B, C_C = 2304, 2816, 3328
C_SU, C_SV = 3840, 4352
C_G = 4864

ENGS = ("pe", "act", "dve", "pool", "sp")
DBG = 99


class Res:
    __slots__ = ("writers", "readers", "excl")

    def __init__(self):
        self.writers = {}
        self.readers = {}
        self.excl = False


class Prog:
    def __init__(self, nc):
        self.nc = nc
        self.streams = {e: [] for e in ENGS}
        self.count = {e: 0 for e in ENGS}
        self.seen = {e: {} for e in ENGS}
        self.semh = {}
        self.res = {}
        self.ndma = 0

    def R(self, *key):
        r = self.res.get(key)
        if r is None:
            r = self.res[key] = Res()
        return r

    def newsem(self, key):
        self.count[key] = 0
        return key

    def _deps(self, eng, reads, writes):
        deps = {}
        for r in reads:
            for k, v in r.writers.items():
                if deps.get(k, 0) < v:
                    deps[k] = v
            if r.excl:
                for k, v in r.readers.items():
                    if k != eng and deps.get(k, 0) < v:
                        deps[k] = v
        for w in writes:
            for dct in (w.writers, w.readers):
                for k, v in dct.items():
                    if k != eng and deps.get(k, 0) < v:
                        deps[k] = v
        if eng == "pe":
            deps.pop("pe", None)
        seen = self.seen[eng]
        st = self.streams[eng]
        for k, v in deps.items():
            if seen.get(k, 0) < v:
                st.append(("w", k, v))
                seen[k] = v

    def op(self, eng, fn, reads=(), writes=()):
        self._deps(eng, reads, writes)
        self.count[eng] += 1
        v = self.count[eng]
        self.streams[eng].append(("o", fn, eng, 1))
        for r in reads:
            r.readers[eng] = v
        for w in writes:
            w.writers = {eng: v}
            w.readers = {}

    def dma(self, out, in_, sem, reads=(), writes=(), q="sp"):
        self._deps(q, reads, writes)
        self.count[sem] += 16
        v = self.count[sem]
        self.streams[q].append(("o", lambda e, o=out, i=in_: e.dma_start(out=o, in_=i), sem, 16))
        for r in reads:
            r.readers[sem] = v
        for w in writes:
            w.writers = {sem: v}
            w.readers = {}
        self.ndma += 1

    def barrier(self):
        tot = dict(self.count)
        for e in ENGS:
            seen = self.seen[e]
            for k, v in tot.items():
                if k == e and e == "pe":
                    continue
                if v > 0 and seen.get(k, 0) < v:
                    self.streams[e].append(("w", k, v))
                    seen[k] = v
        for r in self.res.values():
            r.writers = {}
            r.readers = {}

    def emit(self):
        nc = self.nc
        with ExitStack() as es:
            for k in self.count:
                self.semh[k] = es.enter_context(nc.semaphore("s_" + str(k)))
            block = es.enter_context(nc.Block())

            def replay(e, name):
                semh = self.semh
                for it in self.streams[name]:
                    if it[0] == "w":
                        e.wait_ge(semh[it[1]], it[2])
                    else:
                        ins = it[1](e)
                        ins.then_inc(semh[it[2]], it[3])

            @block.tensor
            def _(e):
                replay(e, "pe")

            @block.scalar
            def _(e):
                replay(e, "act")

            @block.vector
            def _(e):
                replay(e, "dve")

            @block.gpsimd
            def _(e):
                replay(e, "pool")

            @block.sync
            def _(e):
                replay(e, "sp")


class Arena:
    def __init__(self, nc, limit=229000):
        self.nc = nc
        self.off = 16640
        self.n = 0
        self.limit = limit

    def alloc(self, shape, dtype, name="t"):
        esz = 4 if dtype in (F32, I32) else 2
        nbytes = esz * int(np.prod(shape[1:]))
        nbytes = (nbytes + 63) // 64 * 64
        off = self.off
        self.off += nbytes
        assert self.off <= self.limit, ("SBUF arena overflow", name, self.off)
        self.n += 1
        return self.nc.alloc_sbuf_tensor_at("%s_%d" % (name, self.n), list(shape), dtype, offset=off)

    def mark(self):
        return self.off

    def reset(self, m):
        self.off = m


def build_program(nl=NL, final_norm=True, stop=99):
    nc = bass.Bass("TRN2", target_bir_lowering=False)
    P = Prog(nc)
    A = Arena(nc)

    def din(name, shape, dt=F32):
        return nc.dram_tensor(name, list(shape), dt, kind="ExternalInput").ap()

    def dscr(name, shape, dt):
        return nc.dram_tensor(name, list(shape), dt, kind="Internal").ap()

    xT = din("xT", [D, S])
    pT = din("pT", [NL, PLE, S])
    pos = din("pos", [1, S], I32)
    w_in = din("w_in", [NL, D, INW])
    w_a = din("w_a", [NL, 768, D])
    w_b = din("w_b", [NL, 512, D])
    w_c = din("w_c", [NL, 512, D])
    w_out = din("w_out", [NL, D, D])
    w_up = din("w_up", [NL, D, DFF])
    w_down = din("w_down", [NL, DFF, D])
    w_pg = din("w_pg", [NL, D, D])
    w_pe = din("w_pe", [NL, PLE, D])
    sgwT = din("sgwT", [NL, 4, 128, 128])
    gains = din("gains", [128, NL * 3 + 1, KC])
    convw = din("convw", [128, NL, 4, 3])
    lng = din("lng", [NL, 1, 512])
    lnb = din("lnb", [NL, 1, 512])
    sgb = din("sgb", [NL, 1, 512])
    cmat = din("cmat", [128, 4, 128])
    cmask = din("cmask", [128, 2, 512])
    invf = din("invf", [128, 1])
    outT = nc.dram_tensor("outT", [D, S], F32, kind="ExternalOutput").ap()

    hT = dscr("hT", [D, S], F32)
    cosD = dscr("cosD", [128, S], F32)
    sinD = dscr("sinD", [128, S], F32)
    qkT = dscr("qkT", [12, 128, S], BF16)
    Vd = dscr("Vd", [3, 128, 32 * 256], BF16)
    ybT = dscr("ybT", [4, 128, S], BF16)
    ycT = dscr("ycT", [4, 128, S], BF16)
    gT = dscr("gT", [24, 128, S], BF16)
    yaT = dscr("yaT", [6, 128, S], BF16)
    fT = dscr("fT", [32, 128, S], BF16)
    cTd = dscr("cTd", [KC, 128, S], BF16)
    aTd = dscr("aTd", [KC, 128, S], BF16)

    PS = [nc.alloc_psum_tensor("ps%d" % i, [128, 512], F32) for i in range(8)]
    PSR = [P.R("ps", i) for i in range(8)]
    for r_ in PSR:
        r_.excl = True

    cm = A.alloc([128, 4, 128], BF16, "cm")
    ident, rotT, onesb, trilb = (cm[:, i, :] for i in range(4))
    cmf = A.alloc([128, 4, 128], F32, "cmf")
    onesf = cmf[:, 2, :]
    maskb = A.alloc([128, 2, 512], BF16, "maskb")
    gsb = A.alloc([128, NL * 3 + 1, KC], F32, "gsb")
    cwsb = A.alloc([128, NL, 4, 3], F32, "cwsb")
    invfsb = A.alloc([128, 1], F32, "invf")
    epsb = A.alloc([128, 1], F32, "epsb")
    Rconst = P.R("const")
    base_mark = A.mark()

    cnt = {"sem": 0}

    def dsem():
        cnt["sem"] += 1
        key = "d%d" % cnt["sem"]
        if key not in P.count:
            P.newsem(key)
        return key

    _pbar = P.barrier

    def barrier():
        _pbar()
        cnt["sem"] = 0

    P.barrier = barrier

    class Slots:
        def __init__(self, n, shape, dtype, name):
            self.t = [A.alloc(shape, dtype, name) for _ in range(n)]
            self.r = [P.R(name, id(self), i) for i in range(n)]
            self.s = [dsem() for _ in range(n)]
            self.i = -1
            self.n = n

        def next(self):
            self.i = (self.i + 1) % self.n
            return self.t[self.i], self.r[self.i], self.s[self.i]

    psrot = {}

    def next_ps(lo=0, hi=8):
        i = psrot.get((lo, hi), lo)
        psrot[(lo, hi)] = lo + (i + 1 - lo) % (hi - lo)
        return PS[i], PSR[i]

    def setup():
        m = A.mark()
        s0 = dsem()
        mf = A.alloc([128, 2, 512], F32, "mf")
        P.dma(cmf[:], cmat, s0, writes=[Rconst])
        P.dma(mf[:], cmask, s0, writes=[Rconst])
        P.dma(gsb[:], gains, s0, writes=[Rconst])
        P.dma(cwsb[:], convw, s0, writes=[Rconst])
        P.dma(invfsb[:], invf, s0, writes=[Rconst])
        P.op("dve", lambda e: e.memset(epsb[:], RMS_EPS), reads=[Rconst], writes=[Rconst])
        P.op("dve", lambda e: e.tensor_copy(out=cm[:], in_=cmf[:]), reads=[Rconst], writes=[Rconst])
        P.op("dve", lambda e: e.tensor_copy(out=maskb[:], in_=mf[:]), reads=[Rconst], writes=[Rconst])
        posi = A.alloc([128, S], I32, "posi")
        ang = A.alloc([128, S], F32, "ang")
        t1 = A.alloc([128, S], F32, "t1")
        ki = A.alloc([128, S], I32, "ki")
        kf = A.alloc([128, S], F32, "kf")
        r_ang, r_t1, r_ki, r_kf = (P.R("su", i) for i in range(4))
        P.dma(posi[:], pos.broadcast_to([128, S]), dsem(), writes=[r_ki])
        s_st = dsem()
        P.op("dve", lambda e: e.tensor_copy(out=t1[:], in_=posi[:]), reads=[r_ki], writes=[r_t1])
        P.op("dve", lambda e: e.tensor_scalar(out=ang[:], in0=t1[:], scalar1=invfsb[:, 0:1], scalar2=None,
                                              op0=ALU.mult), reads=[r_t1, Rconst], writes=[r_ang])
        C1 = 6.28125
        C2 = 2.0 * math.pi - 6.28125
        i2p = 1.0 / (2.0 * math.pi)
        for which, (shiftk, shifta, dst) in enumerate(((0.0, 0.0, sinD), (0.25, 0.5 * math.pi, cosD))):
            P.op("dve", lambda e, sk=shiftk: e.tensor_scalar(out=t1[:], in0=ang[:], scalar1=i2p, scalar2=sk,
                                                             op0=ALU.mult, op1=ALU.add),
                 reads=[r_ang], writes=[r_t1])
            P.op("dve", lambda e: e.tensor_copy(out=ki[:], in_=t1[:]), reads=[r_t1], writes=[r_ki])
            P.op("dve", lambda e: e.tensor_copy(out=kf[:], in_=ki[:]), reads=[r_ki], writes=[r_kf])
            P.op("dve", lambda e: e.scalar_tensor_tensor(out=t1[:], in0=kf[:], scalar=-C1, in1=ang[:],
                                                         op0=ALU.mult, op1=ALU.add),
                 reads=[r_kf, r_ang], writes=[r_t1])
            P.op("dve", lambda e: e.scalar_tensor_tensor(out=t1[:], in0=kf[:], scalar=-C2, in1=t1[:],
                                                         op0=ALU.mult, op1=ALU.add),
                 reads=[r_kf, r_t1], writes=[r_t1])
            P.op("dve", lambda e, sa=shifta: e.tensor_scalar(out=t1[:], in0=t1[:], scalar1=sa, scalar2=3.1415925,
                                                             op0=ALU.add, op1=ALU.min),
                 reads=[r_t1], writes=[r_t1])
            P.op("dve", lambda e: e.tensor_scalar(out=t1[:], in0=t1[:], scalar1=-3.1415925, scalar2=None,
                                                  op0=ALU.max), reads=[r_t1], writes=[r_t1])
            P.op("act", lambda e: e.activation(out=kf[:], in_=t1[:], func=AF.Sin), reads=[r_t1], writes=[r_kf])
            P.dma(dst, kf[:], s_st, reads=[r_kf], writes=[P.R("trig", which)])
        P.barrier()
        A.reset(m)

    class WLoader:
        def __init__(self, kcmax=8, ncmax=256):
            self.stg = Slots(2, [128, kcmax, ncmax], F32, "wstg")
            self.kcmax = kcmax
            self.ncmax = ncmax
            self.tog = 0

        def issue(self, wsrc, c0, ncols, dst, dres, kc0=0, kcn=None):
            K = wsrc.shape[0]
            if kcn is None:
                kcn = K // 128
            src3 = wsrc.rearrange("(kc p) n -> p kc n", p=128)
            tok = []
            for k0 in range(kc0, kc0 + kcn, self.kcmax):
                kn = min(self.kcmax, kc0 + kcn - k0)
                for cc in range(0, ncols, self.ncmax):
                    cn = min(self.ncmax, ncols - cc)
                    st, sr, ss = self.stg.next()
                    P.dma(st[:, 0:kn, 0:cn], src3[:, k0:k0 + kn, c0 + cc:c0 + cc + cn], ss, writes=[sr])
                    tok.append((st[:, 0:kn, 0:cn], sr, dst[:, k0:k0 + kn, cc:cc + cn], dres))
            return tok

        def convert(self, tok):
            for (src, sr, dst, dres) in tok:
                self.tog ^= 1
                if self.tog:
                    P.op("act", lambda e, o=dst, i=src: e.activation(out=o, in_=i, func=AF.Copy), reads=[sr], writes=[dres])
                else:
                    P.op("dve", lambda e, o=dst, i=src: e.tensor_copy(out=o, in_=i), reads=[sr], writes=[dres])

        def load(self, wsrc, c0, ncols, dst, dres, kc0=0, kcn=None):
            K = wsrc.shape[0]
            if kcn is None:
                kcn = K // 128
            for k0 in range(kc0, kc0 + kcn, self.kcmax):
                kn = min(self.kcmax, kc0 + kcn - k0)
                for cc in range(0, ncols, self.ncmax):
                    cn = min(self.ncmax, ncols - cc)
                    tok = self.issue(wsrc, c0 + cc, cn, dst, dres, k0, kn)
                    tok = [(a, b, dst[:, k0:k0 + kn, cc:cc + cn], d) for (a, b, _, d) in tok]
                    self.convert(tok)

    class WPipe:
        def __init__(self, wl, wsl, W, blocks):
            self.wl, self.wsl, self.W, self.blocks = wl, wsl, W, blocks
            self.cur = None
            self.nxt = None
            self.tok = None

        def _issue(self, i):
            c0, ncols = self.blocks[i]
            wt, wr, _ = self.wsl.next()
            tok = self.wl.issue(self.W, c0, ncols, wt, wr)
            return (wt, wr), tok

        def begin(self, i):
            if i == 0:
                self.cur, tok = self._issue(0)
                self.wl.convert(tok)
            else:
                assert self.nxt is not None
                self.cur = self.nxt
            self.nxt = None
            if i + 1 < len(self.blocks):
                self.nxt, self.tok = self._issue(i + 1)
            return self.cur

        def mid(self):
            if self.tok is not None:
                self.wl.convert(self.tok)
                self.tok = None

    bg = {"q": [], "tog": 0}
    bg_stg = [A.alloc([128, 8, 256], F32, "bgstg") for _ in range(2)]
    bg_res = [P.R("bgstg", i) for i in range(2)]
    bg_sem = [P.newsem("bg%d" % i) for i in range(2)]
    bg_i = {"i": 0}

    def bg_queue(wsrc, dst, dres):
        K, N = wsrc.shape
        src3 = wsrc.rearrange("(kc p) n -> p kc n", p=128)
        for k0 in range(0, K // 128, 8):
            kn = min(8, K // 128 - k0)
            for cc in range(0, N, 256):
                cn = min(256, N - cc)
                bg["q"].append((src3[:, k0:k0 + kn, cc:cc + cn], dst[:, k0:k0 + kn, cc:cc + cn], dres, kn, cn))

    def bg_pump(n=1):
        for _ in range(n):
            if not bg["q"]:
                return
            src, dst, dres, kn, cn = bg["q"].pop(0)
            i = bg_i["i"]
            bg_i["i"] ^= 1
            st = bg_stg[i]
            P.dma(st[:, 0:kn, 0:cn], src, bg_sem[i], writes=[bg_res[i]])
            bg["tog"] ^= 1
            if bg["tog"]:
                P.op("act", lambda e, o=dst, i_=st[:, 0:kn, 0:cn]: e.activation(out=o, in_=i_, func=AF.Copy),
                     reads=[bg_res[i]], writes=[dres])
            else:
                P.op("dve", lambda e, o=dst, i_=st[:, 0:kn, 0:cn]: e.tensor_copy(out=o, in_=i_),
                     reads=[bg_res[i]], writes=[dres])

    def bg_flush():
        bg_pump(len(bg["q"]))

    def norm_square(hs, hr, tmp):
        sq, rs = tmp["sq"], tmp["res"]
        P.op("act", lambda e: e.activation(out=sq[:], in_=hs[:], func=AF.Square), reads=[hr], writes=[rs[0]])

    def norm_rstd(tmp):
        sq, rstd, rs = tmp["sq"], tmp["rstd"], tmp["res"]
        ps, pr = next_ps()

        def fnn(e, ps=ps):
            ins = None
            for c in range(KC):
                ins = e.matmul(ps[:], lhsT=onesb, rhs=sq[:, c, :], start=(c == 0), stop=(c == KC - 1))
            return ins
        P.op("pe", fnn, reads=[rs[0], Rconst], writes=[pr])
        P.op("act", lambda e, ps=ps: e.activation(out=rstd[:], in_=ps[:], func=AF.Ln, bias=epsb[:, 0:1], scale=1.0 / D),
             reads=[pr, Rconst], writes=[rs[2]])
        P.op("act", lambda e: e.activation(out=rstd[:], in_=rstd[:], func=AF.Exp, scale=-0.5), reads=[rs[2]], writes=[rs[2]])

    def norm_apply(hs, hr, gidx, out_ap_fn, out_res, tmp):
        rstd, rs = tmp["rstd"], tmp["res"]
        for c in range(KC):
            P.op("dve", lambda e, c=c: e.scalar_tensor_tensor(out=out_ap_fn(c), in0=hs[:, c, :],
                                                            scalar=gsb[:, gidx, c:c + 1], in1=rstd[:],
                                                            op0=ALU.mult, op1=ALU.mult),
                 reads=[hr, rs[2], Rconst], writes=[out_res])

    def norm_tile(hs, hr, gidx, out_ap_fn, out_res, tmp, wr_extra=()):
        norm_square(hs, hr, tmp)
        norm_rstd(tmp)
        norm_apply(hs, hr, gidx, out_ap_fn, out_res, tmp)

    def norm_tmp():
        return {"sq": A.alloc([128, KC, T], BF16, "sq"),
                "rstd": A.alloc([128, T], F32, "rstd"), "res": [P.R("nt", A.n, i) for i in range(3)]}

    def hview(src, t):
        return src.rearrange("(c p) s -> p c s", p=128)[:, :, t * T:(t + 1) * T]

    def norm_phase(src, gidx, actT, act_res):
        m = A.mark()
        hsl = Slots(2, [128, KC, T], F32, "hs")
        tmp = norm_tmp()
        for t in range(NT):
            hs, hr, hsem = hsl.next()
            P.dma(hs[:], hview(src, t), hsem, reads=[P.R("hT", t)], writes=[hr])
            norm_tile(hs, hr, gidx, lambda c, t=t: actT[:, c, t * T:(t + 1) * T], act_res[t], tmp)
        P.barrier()
        A.reset(m)

    def load_act(dsrc, actT, act_res, key):
        for t in range(NT):
            sl = slice(t * T, (t + 1) * T)
            P.dma(actT[:, :, sl], dsrc.rearrange("c p s -> p c s")[:, :, sl], dsem(), reads=[P.R(key, t)], writes=[act_res[t]])

    def final_phase(src, gidx):
        m = A.mark()
        hsl = Slots(2, [128, KC, T], F32, "hs")
        osl = Slots(2, [128, KC, T], F32, "os")
        tmp = norm_tmp()
        for t in range(NT):
            hs, hr, hsem = hsl.next()
            ot, orr, osem = osl.next()
            P.dma(hs[:], hview(src, t), hsem, reads=[P.R("hT", t)], writes=[hr])
            if final_norm:
                norm_tile(hs, hr, gidx, lambda c, ot=ot: ot[:, c, :], orr, tmp)
                P.dma(hview(outT, t), ot[:], osem, reads=[orr], writes=[P.R("outT", t)])
            else:
                P.dma(hview(outT, t), hs[:], hsem, reads=[hr], writes=[P.R("outT", t)])
        P.barrier()
        A.reset(m)

    def mm_group(ps, pr, wt, wres, kcn, col0, actT, act_res_t, t, ncols=128):
        def fn(e):
            ins = None
            for kc in range(kcn):
                ins = e.matmul(ps[:ncols, :], lhsT=wt[:, kc, col0:col0 + ncols], rhs=actT[:, kc, t * T:(t + 1) * T],
                               start=(kc == 0), stop=(kc == kcn - 1))
            return ins
        P.op("pe", fn, reads=[wres, act_res_t], writes=[pr])

    def phase_qkv(l, aT, aR, sub=9, preload=None):
        m = A.mark()
        cs = A.alloc([128, S], F32, "cos")
        sn = A.alloc([128, S], F32, "sin")
        rtr = P.R("trigsb")
        wl = WLoader()
        wsl = Slots(2, [128, KC, 256], BF16, "wqk")
        zq = Slots(2, [128, T], BF16, "zq")
        t1s = Slots(2, [128, T], F32, "t1")
        t2s = Slots(2, [128, T], F32, "t2")
        rows = Slots(2, [128, S], BF16, "qrow")
        W = w_in[l]
        wpipe = WPipe(wl, wsl, W, [(blk * 256, 256) for blk in range(6)] + [(C_V + g * 256, 256) for g in range(3)])
        first = wpipe.begin(0)
        if preload is not None:
            preload()
        s0 = dsem()
        P.dma(cs[:], cosD, s0, reads=[P.R("trig", 1)], writes=[rtr])
        P.dma(sn[:], sinD, s0, reads=[P.R("trig", 0)], writes=[rtr])
        for blk in range(6 if sub >= 1 else 0):
            wt, wr = first if blk == 0 else wpipe.begin(blk)
            for ci in range(2):
                if ci == 1:
                    wpipe.mid()
                oc = blk * 2 + ci
                row, rr, rsem = rows.next()
                def rot_tail(ps, pr, z, zr, t, row=None, rr=None):
                    ps2, pr2 = next_ps()
                    P.op("pe", lambda e, ps2=ps2, z=z: e.matmul(ps2[:], lhsT=rotT, rhs=z[:], start=True, stop=True),
                         reads=[zr, Rconst], writes=[pr2])
                    a1, a1r, _ = t1s.next()
                    a2, a2r, _ = t2s.next()
                    P.op("dve", lambda e, a1=a1, ps=ps, t=t: e.tensor_tensor(out=a1[:], in0=ps[:],
                                                                              in1=cs[:, t * T:(t + 1) * T], op=ALU.mult),
                         reads=[pr, rtr], writes=[a1r])
                    P.op("dve", lambda e, a2=a2, ps2=ps2, t=t: e.tensor_tensor(out=a2[:], in0=ps2[:],
                                                                                in1=sn[:, t * T:(t + 1) * T], op=ALU.mult),
                         reads=[pr2, rtr], writes=[a2r])
                    P.op("dve", lambda e, a1=a1, a2=a2, row=row, t=t: e.tensor_tensor(
                        out=row[:, t * T:(t + 1) * T], in0=a1[:], in1=a2[:], op=ALU.add),
                        reads=[a1r, a2r], writes=[rr])

                prev = None
                for t in range(NT):
                    ps, pr = next_ps()
                    mm_group(ps, pr, wt, wr, KC, ci * 128, aT, aR[t], t)
                    z, zr, _ = zq.next()
                    P.op("act", lambda e, z=z, ps=ps: e.activation(out=z[:], in_=ps[:], func=AF.Copy),
                         reads=[pr], writes=[zr])
                    if prev is not None:
                        rot_tail(*prev, row=row, rr=rr)
                    prev = (ps, pr, z, zr, t)
                rot_tail(*prev, row=row, rr=rr)
                if DBG >= 6:
                    P.dma(qkT[oc], row[:], rsem, reads=[rr], writes=[P.R("qkT", oc)])
        vsl = Slots(2, [128, 32, 256], BF16, "vsb")
        for g, (win, dil) in enumerate(GROUPS if sub >= 2 else ()):
            wt, wr = wpipe.begin(6 + g)
            vt, vr, vsem = vsl.next()
            nb = S // (dil * 128)
            for r in range(dil):
                for n in range(nb):
                    bi = r * nb + n
                    if bi == 16:
                        wpipe.mid()
                    t0 = r + dil * 128 * n
                    ps, pr = next_ps()

                    def fn(e, ps=ps, t0=t0, dil=dil, wt=wt):
                        ins = None
                        for kc in range(KC):
                            ins = e.matmul(ps[:, 0:256], lhsT=aT[:, kc, t0:t0 + dil * 127 + 1:dil], rhs=wt[:, kc, :],
                                           start=(kc == 0), stop=(kc == KC - 1))
                        return ins
                    P.op("pe", fn, reads=[wr] + aR, writes=[pr])
                    P.op("act", lambda e, ps=ps, vt=vt, bi=bi: e.activation(out=vt[:, bi, :], in_=ps[:, 0:256],
                                                                          func=AF.Copy), reads=[pr], writes=[vr])
            P.dma(Vd[g], vt[:].rearrange("p b c -> p (b c)"), vsem, reads=[vr], writes=[P.R("Vd", g)])
        P.barrier()
        A.reset(m)

    def phase_conv_gates(l, aT, aR):
        m = A.mark()
        wl = WLoader()
        wsl = Slots(2, [128, KC, 256], BF16, "wcg")
        W = w_in[l]
        zxs = A.alloc([128, S], F32, "zxs")
        ub = A.alloc([128, S + 2], F32, "ub")
        rzx, rub = P.R("zxs"), P.R("ub")
        rows = Slots(2, [128, S], BF16, "ybrow")
        P.op("pool", lambda e: e.memset(ub[:, 0:2], 0.0), writes=[rub])
        blocks = []
        for c in range(4):
            blocks += [(C_X + c * 128, 128), (C_C + c * 128, 128), (C_B + c * 128, 128)]
        blocks += [(C_G + blk * 256, 256) for blk in range(12)]
        wpipe = WPipe(wl, wsl, W, blocks)
        for c in range(4):
            wt, wr = wpipe.begin(3 * c)
            for t in range(NT):
                if t == 4:
                    wpipe.mid()
                ps, pr = next_ps()
                mm_group(ps, pr, wt, wr, KC, 0, aT, aR[t], t)
                P.op("act", lambda e, ps=ps, t=t: e.activation(out=zxs[:, t * T:(t + 1) * T], in_=ps[:], func=AF.Copy),
                     reads=[pr], writes=[rzx])
            wt, wr = wpipe.begin(3 * c + 1)
            for t in range(NT):
                if t == 4:
                    wpipe.mid()
                ps, pr = next_ps()
                mm_group(ps, pr, wt, wr, KC, 0, aT, aR[t], t)
                P.op("dve", lambda e, ps=ps, t=t: e.tensor_tensor(out=ub[:, 2 + t * T:2 + (t + 1) * T], in0=ps[:],
                                                                  in1=zxs[:, t * T:(t + 1) * T], op=ALU.mult),
                     reads=[pr, rzx], writes=[rub])
                P.op("dve", lambda e, c=c, t=t: e.tensor_scalar(out=zxs[:, t * T:(t + 1) * T], in0=ub[:, t * T:(t + 1) * T],
                                                               scalar1=cwsb[:, l, c, 0:1], scalar2=None, op0=ALU.mult),
                     reads=[rub, Rconst], writes=[rzx])
                P.op("dve", lambda e, c=c, t=t: e.scalar_tensor_tensor(out=zxs[:, t * T:(t + 1) * T],
                                                                      in0=ub[:, 1 + t * T:1 + (t + 1) * T],
                                                                      scalar=cwsb[:, l, c, 1:2], in1=zxs[:, t * T:(t + 1) * T],
                                                                      op0=ALU.mult, op1=ALU.add),
                     reads=[rub, rzx, Rconst], writes=[rzx])
                P.op("dve", lambda e, c=c, t=t: e.scalar_tensor_tensor(out=zxs[:, t * T:(t + 1) * T],
                                                                      in0=ub[:, 2 + t * T:2 + (t + 1) * T],
                                                                      scalar=cwsb[:, l, c, 2:3], in1=zxs[:, t * T:(t + 1) * T],
                                                                      op0=ALU.mult, op1=ALU.add),
                     reads=[rub, rzx, Rconst], writes=[rzx])
            wt, wr = wpipe.begin(3 * c + 2)
            row, rr, rsem = rows.next()
            for t in range(NT):
                if t == 4:
                    wpipe.mid()
                ps, pr = next_ps()
                mm_group(ps, pr, wt, wr, KC, 0, aT, aR[t], t)
                P.op("dve", lambda e, ps=ps, t=t, row=row: e.tensor_tensor(out=row[:, t * T:(t + 1) * T], in0=ps[:],
                                                                           in1=zxs[:, t * T:(t + 1) * T], op=ALU.mult),
                     reads=[pr, rzx], writes=[rr])
            P.dma(ybT[c], row[:], rsem, reads=[rr], writes=[P.R("ybT", c)])
        for blk in range(12):
            wt, wr = wpipe.begin(12 + blk)
            for ci in range(2):
                if ci == 1:
                    wpipe.mid()
                oc = blk * 2 + ci
                row, rr, rsem = rows.next()
                for t in range(NT):
                    ps, pr = next_ps()
                    mm_group(ps, pr, wt, wr, KC, ci * 128, aT, aR[t], t)
                    P.op("act", lambda e, ps=ps, t=t, row=row: e.activation(out=row[:, t * T:(t + 1) * T], in_=ps[:],
                                                                            func=AF.Sigmoid), reads=[pr], writes=[rr])
                P.dma(gT[oc], row[:], rsem, reads=[rr], writes=[P.R("gT", oc)])
        P.barrier()
        A.reset(m)

    def phase_sgu(l, aT, aR):
        m = A.mark()
        wl = WLoader()
        W = w_in[l]
        uT = A.alloc([128, 4, S], BF16, "uT")
        ruT = [P.R("uT", b) for b in range(32)]
        wsl = Slots(2, [128, KC, 256], BF16, "wsu")
        s0 = dsem()
        lngs = A.alloc([128, 512], F32, "lng")
        lnbs = A.alloc([128, 512], F32, "lnb")
        sgbs = A.alloc([128, 512], F32, "sgb")
        wsf = A.alloc([128, 4, 128], F32, "wsf")
        wsb = A.alloc([128, 4, 128], BF16, "wsb")
        rc = P.R("sgconst")
        P.dma(lngs[:], lng[l].broadcast_to([128, 512]), s0, writes=[rc])
        P.dma(lnbs[:], lnb[l].broadcast_to([128, 512]), s0, writes=[rc])
        P.dma(sgbs[:], sgb[l].broadcast_to([128, 512]), s0, writes=[rc])
        P.dma(wsf[:], sgwT[l].rearrange("g s t -> s g t"), s0, writes=[rc])
        P.op("dve", lambda e: e.tensor_tensor(out=wsb[:], in0=wsf[:],
                                              in1=cmf[:, 3:4, :].broadcast_to([128, 4, 128]), op=ALU.mult),
             reads=[rc, Rconst], writes=[rc])
        wpipe = WPipe(wl, wsl, W, [(C_SU + blk * 256, 256) for blk in range(2)])
        for blk in range(2):
            wt, wr = wpipe.begin(blk)
            for ci in range(2):
                if ci == 1:
                    wpipe.mid()
                c = blk * 2 + ci
                for t in range(NT):
                    ps, pr = next_ps()
                    mm_group(ps, pr, wt, wr, KC, ci * 128, aT, aR[t], t)
                    P.op("act", lambda e, ps=ps, t=t, c=c: e.activation(out=uT[:, c, t * T:(t + 1) * T], in_=ps[:],
                                                                        func=AF.Gelu),
                         reads=[pr], writes=ruT[4 * t:4 * t + 4])
        wv = A.alloc([128, KC, 512], BF16, "wv")
        rwv = P.R("wv")
        wl.load(W, C_SV, 512, wv, rwv)
        gvs = Slots(2, [128, 4, 512], F32, "gv")
        sqs = Slots(1, [128, 4, 512], F32, "gsq")
        vbs = Slots(2, [128, 4, 512], BF16, "vb")
        sts = Slots(2, [128, 8, 4], F32, "st")
        tts = Slots(2, [128, 512], F32, "tt")
        def stage_a(t):
            gv, gr, _ = gvs.next()
            sq, sr, _ = sqs.next()
            vb, vbr, _ = vbs.next()
            st, str_, _ = sts.next()
            for bb in range(4):
                b = 4 * t + bb
                ps, pr = next_ps(0, 4)

                def fn(e, ps=ps, b=b):
                    ins = None
                    for kc in range(KC):
                        ins = e.matmul(ps[:], lhsT=aT[:, kc, b * 128:(b + 1) * 128], rhs=wv[:, kc, :],
                                       start=(kc == 0), stop=(kc == KC - 1))
                    return ins
                P.op("pe", fn, reads=[rwv, aR[t]], writes=[pr])
                P.op("act", lambda e, gv=gv, ps=ps, bb=bb: e.activation(out=gv[:, bb, :], in_=ps[:], func=AF.Gelu),
                     reads=[pr], writes=[gr])
            P.op("act", lambda e, gv=gv, sq=sq: e.activation(out=sq[:], in_=gv[:], func=AF.Square), reads=[gr], writes=[sr])
            P.op("dve", lambda e, gv=gv, st=st: e.tensor_reduce(out=st[:, 0, :], in_=gv[:], axis=AX.X, op=ALU.add),
                 reads=[gr], writes=[str_])
            P.op("dve", lambda e, sq=sq, st=st: e.tensor_reduce(out=st[:, 1, :], in_=sq[:], axis=AX.X, op=ALU.add),
                 reads=[sr, str_], writes=[str_])
            P.op("dve", lambda e, st=st: e.tensor_scalar(out=st[:, 2:4, :], in0=st[:, 0:2, :], scalar1=1.0 / 512, scalar2=None,
                                                         op0=ALU.mult), reads=[str_], writes=[str_])
            P.op("dve", lambda e, st=st: e.tensor_tensor(out=st[:, 4, :], in0=st[:, 2, :], in1=st[:, 2, :], op=ALU.mult),
                 reads=[str_], writes=[str_])
            P.op("dve", lambda e, st=st: e.scalar_tensor_tensor(out=st[:, 5, :], in0=st[:, 3, :], scalar=LN_EPS, in1=st[:, 4, :],
                                                                op0=ALU.add, op1=ALU.subtract), reads=[str_], writes=[str_])
            P.op("act", lambda e, st=st: e.activation(out=st[:, 6, :], in_=st[:, 5, :], func=AF.Sqrt), reads=[str_], writes=[str_])
            P.op("dve", lambda e, st=st: e.reciprocal(out=st[:, 7, :], in_=st[:, 6, :]), reads=[str_], writes=[str_])
            for bb in range(4):
                P.op("dve", lambda e, st=st, gv=gv, bb=bb: e.tensor_scalar(
                    out=gv[:, bb, :], in0=gv[:, bb, :], scalar1=st[:, 2, bb:bb + 1], scalar2=st[:, 7, bb:bb + 1],
                    op0=ALU.subtract, op1=ALU.mult), reads=[gr, str_], writes=[gr])
            P.op("pool", lambda e, gv=gv: e.tensor_tensor(out=gv[:, 0:2, :], in0=gv[:, 0:2, :],
                                                          in1=lngs[:].unsqueeze(1).broadcast_to([128, 2, 512]), op=ALU.mult),
                 reads=[gr, rc], writes=[gr])
            P.op("dve", lambda e, gv=gv: e.tensor_tensor(out=gv[:, 2:4, :], in0=gv[:, 2:4, :],
                                                         in1=lngs[:].unsqueeze(1).broadcast_to([128, 2, 512]), op=ALU.mult),
                 reads=[gr, rc], writes=[gr])
            P.op("pool", lambda e, gv=gv, vb=vb: e.tensor_tensor(out=vb[:, 0:2, :], in0=gv[:, 0:2, :],
                                                                 in1=lnbs[:].unsqueeze(1).broadcast_to([128, 2, 512]), op=ALU.add),
                 reads=[gr, rc], writes=[vbr])
            P.op("dve", lambda e, gv=gv, vb=vb: e.tensor_tensor(out=vb[:, 2:4, :], in0=gv[:, 2:4, :],
                                                                in1=lnbs[:].unsqueeze(1).broadcast_to([128, 2, 512]), op=ALU.add),
                 reads=[gr, rc], writes=[vbr])
            return vb, vbr

        def stage_b(t, vb, vbr):
            for bb in range(4):
                b = 4 * t + bb
                ps2, pr2 = next_ps(4, 8)

                def fn2(e, ps2=ps2, vb=vb, bb=bb):
                    ins = None
                    for g in range(4):
                        ins = e.matmul(ps2[:, g * 128:(g + 1) * 128], lhsT=vb[:, bb, g * 128:(g + 1) * 128], rhs=wsb[:, g, :],
                                       start=True, stop=True)
                    return ins
                P.op("pe", fn2, reads=[vbr, rc], writes=[pr2])
                tt, ttr, _ = tts.next()
                P.op("dve", lambda e, tt=tt, ps2=ps2: e.tensor_tensor(out=tt[:], in0=ps2[:], in1=sgbs[:], op=ALU.add),
                     reads=[pr2, rc], writes=[ttr])
                ub_ = uT[:, :, b * 128:(b + 1) * 128]
                P.op("dve", lambda e, tt=tt, ub_=ub_: e.tensor_tensor(out=ub_, in0=tt[:].rearrange("p (g t) -> p g t", g=4),
                                                                      in1=ub_, op=ALU.mult),
                     reads=[ttr, ruT[b]], writes=[ruT[b]])

        prev = stage_a(0)
        for t in range(NT):
            nxt_ = stage_a(t + 1) if t + 1 < NT else None
            stage_b(t, *prev)
            prev = nxt_
        s1 = dsem()
        for c in range(4):
            P.dma(ycT[c], uT[:, c, :], s1, reads=ruT, writes=[P.R("ycT", c)])
        P.barrier()
        A.reset(m)

    def phase_attn(l):
        m = A.mark()
        U = A.alloc([128, 6, S], BF16, "U")
        Dn = A.alloc([128, 2, S], F32, "Dn")
        rU = [P.R("U", t) for t in range(NT)]
        rD = [P.R("Dn", t) for t in range(NT)]
        qs = A.alloc([128, 2, S], BF16, "qs")
        ks = A.alloc([128, 2, S], BF16, "ks")
        vs = A.alloc([128, 32, 256], BF16, "vs")
        rq = [P.R("qs", 0), P.R("qs", 1)]
        rk = [P.R("ks", 0), P.R("ks", 1)]
        rv = P.R("vs")
        sq_ = [dsem(), dsem()]
        sk_ = [dsem(), dsem()]
        sv_ = dsem()
        pts = [Slots(2, [128, 512], BF16, "pT%d" % i) for i in range(2)]
        s0 = dsem()
        order = (2, 1, 0)
        for gi, g in enumerate(order):
            win, dil = GROUPS[g]
            nb = S // (dil * 128)
            for cc in range(2):
                P.dma(qs[:, cc, :], qkT[2 * g + cc], sq_[cc], reads=[P.R("qkT", 2 * g + cc)], writes=[rq[cc]])
                P.dma(ks[:, cc, :], qkT[6 + 2 * g + cc], sk_[cc], reads=[P.R("qkT", 6 + 2 * g + cc)], writes=[rk[cc]])
            P.dma(vs[:].rearrange("p b c -> p (b c)"), Vd[g], sv_, reads=[P.R("Vd", g)], writes=[rv])
            def scores(r, n):
                t0 = r + dil * 128 * n
                span = dil * 128
                qsl = slice(t0, t0 + dil * 127 + 1, dil)
                if (r * nb + n) % 4 == 1:
                    bg_pump(1)
                kbs = []
                if n > 0:
                    kbs.append((0, slice(t0 - span, t0 - span + dil * 127 + 1, dil), r * nb + n - 1))
                kbs.append((1, qsl, r * nb + n))
                lo = 0 if n > 0 else 256
                psE, prE = next_ps(0, 6)
                psO, prO = next_ps(0, 6)

                def fn(e, psE=psE, psO=psO, kbs=kbs, qsl=qsl):
                    ins = None
                    for (mi, ksl, vbi) in kbs:
                        for ps in (psE, psO):
                            e.matmul(ps[:, mi * 256:(mi + 1) * 256], lhsT=ident, rhs=maskb[:, mi, 0:256],
                                     start=True, stop=False, skip_group_check=True)
                    for (mi, ksl, vbi) in kbs:
                        for jj in range(2):
                            for par, ps in ((0, psE), (1, psO)):
                                pb = par * 64
                                c0 = mi * 256 + jj * 128
                                ins = e.matmul(ps[:, c0:c0 + 128], lhsT=ks[pb:pb + 64, jj, ksl],
                                               rhs=qs[pb:pb + 64, jj, qsl], start=False, stop=True,
                                               skip_group_check=True)
                    return ins
                P.op("pe", fn, reads=rq + rk + [Rconst], writes=[prE, prO])
                ptE, ptEr, _ = pts[0].next()
                ptO, ptOr, _ = pts[1].next()
                P.op("act", lambda e, pt=ptE, ps=psE, lo=lo: e.activation(out=pt[:, lo:512], in_=ps[:, lo:512],
                                                                         func=AF.Exp, scale=0.125),
                     reads=[prE], writes=[ptEr])
                P.op("act", lambda e, pt=ptO, ps=psO, lo=lo: e.activation(out=pt[:, lo:512], in_=ps[:, lo:512],
                                                                         func=AF.Exp, scale=0.125),
                     reads=[prO], writes=[ptOr])
                return (t0, qsl, kbs, ptE, ptEr, ptO, ptOr)

            def pv(st):
                t0, qsl, kbs, ptE, ptEr, ptO, ptOr = st
                pso, pro = next_ps(6, 8)

                def fn3(e, pso=pso, kbs=kbs, ptE=ptE, ptO=ptO):
                    ins = None
                    nk = len(kbs)
                    for j in range(4):
                        pb = (j % 2) * 64
                        pr_ = j // 2
                        pt = ptE if j % 2 == 0 else ptO
                        kw = {"tile_position": (0, pb)} if pb else {}
                        for i, (mi, ksl, vbi) in enumerate(kbs):
                            c0 = mi * 256 + pr_ * 128
                            ins = e.matmul(pso[pb:pb + 64, pr_ * 128:(pr_ + 1) * 128],
                                           lhsT=vs[:, vbi, j * 64:(j + 1) * 64], rhs=pt[:, c0:c0 + 128],
                                           start=(i == 0), stop=(i == nk - 1), skip_group_check=True, **kw)
                        for i, (mi, ksl, vbi) in enumerate(kbs):
                            c0 = mi * 256 + pr_ * 128
                            ins = e.matmul(pso[pb:pb + 64, 256 + pr_ * 128:256 + (pr_ + 1) * 128],
                                           lhsT=onesb[:, 0:64], rhs=pt[:, c0:c0 + 128],
                                           start=(i == 0), stop=(i == nk - 1), skip_group_check=True, **kw)
                    return ins
                P.op("pe", fn3, reads=[rv, Rconst, ptEr, ptOr], writes=[pro])
                tiles = sorted(set(range(t0 // T, (t0 + dil * 127) // T + 1)))
                P.op("act", lambda e, pso=pso, g=g, qsl=qsl: e.activation(
                    out=U[:, 2 * g:2 * g + 2, qsl], in_=pso[:, 0:256].rearrange("p (a q) -> p a q", a=2), func=AF.Copy),
                    reads=[pro], writes=[rU[t] for t in tiles])
                if gi == 0:
                    P.op("dve", lambda e, pso=pso, qsl=qsl: e.tensor_copy(
                        out=Dn[:, :, qsl], in_=pso[:, 256:512].rearrange("p (a q) -> p a q", a=2)),
                        reads=[pro], writes=[rD[t] for t in tiles])
                else:
                    P.op("dve", lambda e, pso=pso, qsl=qsl: e.tensor_tensor(
                        out=Dn[:, :, qsl], in0=pso[:, 256:512].rearrange("p (a q) -> p a q", a=2),
                        in1=Dn[:, :, qsl], op=ALU.add),
                        reads=[pro] + [rD[t] for t in tiles], writes=[rD[t] for t in tiles])

            def normalise(t):
                sl = slice(t * T, (t + 1) * T)
                P.op("act", lambda e, sl=sl: e.activation(out=Dn[:, :, sl], in_=Dn[:, :, sl], func=AF.Ln), reads=[rD[t]], writes=[rD[t]])
                P.op("act", lambda e, sl=sl: e.activation(out=Dn[:, :, sl], in_=Dn[:, :, sl], func=AF.Exp, scale=-1.0),
                     reads=[rD[t]], writes=[rD[t]])
                for g2 in range(3):
                    P.op("dve", lambda e, sl=sl, g2=g2: e.tensor_tensor(out=U[:, 2 * g2:2 * g2 + 2, sl], in0=U[:, 2 * g2:2 * g2 + 2, sl],
                                                                       in1=Dn[:, :, sl], op=ALU.mult),
                         reads=[rU[t], rD[t]], writes=[rU[t]])
                P.dma(yaT.rearrange("c p s -> p c s")[:, :, sl], U[:, :, sl], dsem(), reads=[rU[t]],
                      writes=[P.R("yaTt", t)])

            prev = None
            bidx = 0
            for r in range(dil):
                for n in range(nb):
                    cur_ = scores(r, n)
                    if prev is not None:
                        pv(prev)
                        if gi == len(order) - 1 and bidx % 4 == 0:
                            normalise(bidx // 4 - 1)
                    prev = cur_
                    bidx += 1
            pv(prev)
            if gi == len(order) - 1:
                normalise(NT - 1)
        P.barrier()
        A.reset(m)

    def alloc_merge_w(l):
        wa = A.alloc([128, 6, D], BF16, "wa")
        wb_ = A.alloc([128, 4, D], BF16, "wb")
        wc_ = A.alloc([128, 4, D], BF16, "wc")
        wo = A.alloc([128, 8, D], BF16, "wo")
        rw = P.R("wmerge", l)
        bg_queue(w_a[l], wa, rw)
        bg_queue(w_b[l], wb_, rw)
        bg_queue(w_c[l], wc_, rw)
        bg_queue(w_out[l], wo, rw)
        return wa, wb_, wc_, wo, rw

    def phase_merge(l, hsrc, wts):
        m = A.mark()
        wa, wb_, wc_, wo, rw = wts
        bg_flush()
        ysl = Slots(2, [128, 14, T], BF16, "ys")
        gsl = Slots(2, [128, 3, 4, T], BF16, "gs")
        csl = Slots(2, [128, KC, T], BF16, "ct")
        ntmp = norm_tmp()
        hsl = Slots(2, [128, KC, T], F32, "hs")
        mT = A.alloc([128, KC, T], BF16, "mT")
        rm = P.R("mT")
        tsl = [Slots(2, [128, T], F32, "mt%d" % i) for i in range(3)]
        def load(t):
            sl = slice(t * T, (t + 1) * T)
            yt, yr, ysem = ysl.next()
            hs, hr, hsem = hsl.next()
            P.dma(yt[:, 0:6, :], yaT.rearrange("c p s -> p c s")[:, :, sl], ysem,
                  reads=[P.R("yaT", c) for c in range(6)], writes=[yr])
            P.dma(yt[:, 6:10, :], ybT.rearrange("c p s -> p c s")[:, :, sl], ysem,
                  reads=[P.R("ybT", c) for c in range(4)], writes=[yr])
            P.dma(yt[:, 10:14, :], ycT.rearrange("c p s -> p c s")[:, :, sl], ysem,
                  reads=[P.R("ycT", c) for c in range(4)], writes=[yr])
            P.dma(hs[:], hview(hsrc, t), hsem, reads=[P.R("hT", t)], writes=[hr])
            return yt, yr, hs, hr, hsem

        def loadg(t, half):
            sl = slice(t * T, (t + 1) * T)
            gt, gr, gsem = gsl.next()
            for b3 in range(3):
                P.dma(gt[:, b3, :, :], gT.rearrange("c p s -> p c s")[:, b3 * 8 + half * 4:b3 * 8 + half * 4 + 4, sl], gsem,
                      reads=[P.R("gT", c) for c in range(b3 * 8 + half * 4, b3 * 8 + half * 4 + 4)],
                      writes=[P.R("gsq", id(gt), b3)])
            return gt, [P.R("gsq", id(gt), b3) for b3 in range(3)]

        nxt = load(0)
        gnx = loadg(0, 0)
        pend = None

        def finish_norm(pd):
            hs_, hr_, t_ = pd
            ct, ctr, ctsem = csl.next()
            norm_apply(hs_, hr_, l * 3 + 1, lambda c, ct=ct: ct[:, c, :], ctr, ntmp)
            P.dma(cTd.rearrange("c p s -> p c s")[:, :, t_ * T:(t_ + 1) * T], ct[:], ctsem, reads=[ctr],
                  writes=[P.R("cTd", t_)])

        for t in range(NT):
            yt, yr, hs, hr, hsem = nxt
            for oc in range(KC):
                if oc % 4 == 0:
                    gt, grl = gnx
                    if oc == 0:
                        gnx = loadg(t, 1)
                    elif t + 1 < NT:
                        gnx = loadg(t + 1, 0)
                prods = []
                for bi, (wt, k0, kn) in enumerate(((wa, 0, 6), (wb_, 6, 4), (wc_, 10, 4))):
                    ps, pr = next_ps()

                    def fn(e, ps=ps, wt=wt, k0=k0, kn=kn, yt=yt, oc=oc):
                        ins = None
                        for kc in range(kn):
                            ins = e.matmul(ps[:], lhsT=wt[:, kc, oc * 128:(oc + 1) * 128], rhs=yt[:, k0 + kc, :],
                                           start=(kc == 0), stop=(kc == kn - 1))
                        return ins
                    P.op("pe", fn, reads=[rw, yr], writes=[pr])
                    tt, ttr, _ = tsl[bi].next()
                    P.op("dve", lambda e, tt=tt, ps=ps, gt=gt, bi=bi, oc=oc: e.tensor_tensor(
                        out=tt[:], in0=ps[:], in1=gt[:, bi, oc % 4, :], op=ALU.mult), reads=[pr] + grl, writes=[ttr])
                    prods.append((tt, ttr))
                (ta, ra), (tb, rb), (tc, rc_) = prods
                P.op("dve", lambda e, ta=ta, tb=tb: e.tensor_tensor(out=ta[:], in0=ta[:], in1=tb[:], op=ALU.add),
                     reads=[ra, rb], writes=[ra])
                P.op("pool", lambda e, ta=ta, tc=tc, oc=oc: e.tensor_tensor(out=mT[:, oc, :], in0=ta[:], in1=tc[:], op=ALU.add),
                     reads=[ra, rc_], writes=[rm])
                if oc == 0 and pend is not None:
                    norm_rstd(ntmp)
                if oc == 1 and pend is not None:
                    finish_norm(pend)
                    pend = None
                if oc == 2 and t + 1 < NT:
                    nxt = load(t + 1)
            for oc in range(KC):
                ps, pr = next_ps()

                def fn(e, ps=ps, oc=oc):
                    ins = None
                    for kc in range(KC):
                        ins = e.matmul(ps[:], lhsT=wo[:, kc, oc * 128:(oc + 1) * 128], rhs=mT[:, kc, :],
                                       start=(kc == 0), stop=(kc == KC - 1))
                    return ins
                P.op("pe", fn, reads=[rw, rm], writes=[pr])
                P.op("dve", lambda e, ps=ps, hs=hs, oc=oc: e.tensor_tensor(out=hs[:, oc, :], in0=ps[:], in1=hs[:, oc, :],
                                                                           op=ALU.add), reads=[pr, hr], writes=[hr])
            P.dma(hview(hT, t), hs[:], hsem, reads=[hr], writes=[P.R("hT", t)])
            norm_square(hs, hr, ntmp)
            pend = (hs, hr, t)
        norm_rstd(ntmp)
        finish_norm(pend)
        P.barrier()
        A.reset(m)

    def phase_ffn_up(l, cT, cR, preload=None):
        m = A.mark()
        wl = WLoader()
        wsl = Slots(2, [128, KC, 256], BF16, "wup")
        rows = Slots(2, [128, S], BF16, "frow")
        rl = Slots(3, [128, T], F32, "relu")
        wpipe = WPipe(wl, wsl, w_up[l], [(blk * 256, 256) for blk in range(16)])
        first = wpipe.begin(0)
        if preload is not None:
            preload()
        for blk in range(16):
            wt, wr = first if blk == 0 else wpipe.begin(blk)
            for ci in range(2):
                if ci == 1:
                    wpipe.mid()
                fc = blk * 2 + ci
                bg_pump(1)
                row, rr, rsem = rows.next()
                for t in range(NT):
                    ps, pr = next_ps()
                    mm_group(ps, pr, wt, wr, KC, ci * 128, cT, cR[t], t)
                    rt, rtr, _ = rl.next()
                    P.op("act", lambda e, rt=rt, ps=ps: e.activation(out=rt[:], in_=ps[:], func=AF.Relu), reads=[pr], writes=[rtr])
                    P.op("dve", lambda e, rt=rt, row=row, t=t: e.tensor_tensor(out=row[:, t * T:(t + 1) * T], in0=rt[:],
                                                                               in1=rt[:], op=ALU.mult),
                         reads=[rtr], writes=[rr])
                P.dma(fT[fc], row[:], rsem, reads=[rr], writes=[P.R("fT", fc)])
        P.barrier()
        A.reset(m)

    def phase_ffn_down(l, wd, rw):
        m = A.mark()
        bg_flush_except = None
        fsl = Slots(2, [128, 32, T], BF16, "fs")
        hsl = Slots(2, [128, KC, T], F32, "hs")
        def load(t):
            sl = slice(t * T, (t + 1) * T)
            ft, fr, fsem = fsl.next()
            hs, hr, hsem = hsl.next()
            bg_pump(1)
            for q4 in range(4):
                P.dma(ft[:, 8 * q4:8 * q4 + 8, :], fT.rearrange("c p s -> p c s")[:, 8 * q4:8 * q4 + 8, sl], fsem,
                      reads=[P.R("fT", c) for c in range(8 * q4, 8 * q4 + 8)], writes=[P.R("fsq", id(ft), q4)])
            P.dma(hs[:], hview(hT, t), hsem, reads=[P.R("hT", t)], writes=[hr])
            return ft, [P.R("fsq", id(ft), q4) for q4 in range(4)], hs, hr, hsem

        nxt = load(0)
        for t in range(NT):
            ft, frl, hs, hr, hsem = nxt
            if t + 1 < NT:
                nxt = load(t + 1)
            for oc in range(KC):
                ps, pr = next_ps()

                def fn(e, ps=ps, oc=oc, ft=ft):
                    ins = None
                    for kc in range(32):
                        ins = e.matmul(ps[:], lhsT=wd[:, kc, oc * 128:(oc + 1) * 128], rhs=ft[:, kc, :],
                                       start=(kc == 0), stop=(kc == 31))
                    return ins
                P.op("pe", fn, reads=[rw] + frl, writes=[pr])
                P.op("dve", lambda e, ps=ps, hs=hs, oc=oc: e.tensor_tensor(out=hs[:, oc, :], in0=ps[:], in1=hs[:, oc, :],
                                                                           op=ALU.add), reads=[pr, hr], writes=[hr])
            P.dma(hview(hT, t), hs[:], hsem, reads=[hr], writes=[P.R("hT", t)])
        P.barrier()
        A.reset(m)

    def phase_ple(l, wg, wp, rw, dead_lo, fuse):
        m = A.mark()
        bg_flush()
        keep = A.off
        A.off = dead_lo
        hsl_ = Slots(3, [128, KC, T], F32, "hs")
        assert A.off <= dead_lo + 64 * 1024
        A.off = keep
        if fuse == "final":
            osl = Slots(2, [128, KC, T], F32, "os")
        elif fuse == "next":
            osl = Slots(2, [128, KC, T], BF16, "as")
        hsl = hsl_
        psl = Slots(2, [128, 2, T], F32, "pf")
        pbs = Slots(2, [128, 2, T], BF16, "pb")
        eTs = [A.alloc([128, KC, T], BF16, "eT") for _ in range(2)]
        reT = [P.R("eT", l, i) for i in range(2)]
        tmp3 = norm_tmp()
        tmpn = norm_tmp()
        sgs = Slots(2, [128, T], F32, "sg")
        tts = Slots(2, [128, T], F32, "pt")

        def load(t):
            sl = slice(t * T, (t + 1) * T)
            hs, hr, hsem = hsl.next()
            pf, pfr, pfsem = psl.next()
            P.dma(hs[:], hview(hT, t), hsem, reads=[P.R("hT", t)], writes=[hr])
            P.dma(pf[:], pT[l].rearrange("(c p) s -> p c s", p=128)[:, :, sl], pfsem, writes=[pfr])
            pb, pbr, _ = pbs.next()
            P.op("pool", lambda e, pb=pb, pf=pf: e.tensor_copy(out=pb[:], in_=pf[:]), reads=[pfr], writes=[pbr])
            return hs, hr, hsem, pb, pbr

        def finish_fused(pd):
            hs_, hr_, t_ = pd
            ot, otr, otsem = osl.next()
            if fuse == "next":
                norm_apply(hs_, hr_, (l + 1) * 3 + 0, lambda c, ot=ot: ot[:, c, :], otr, tmpn)
                P.dma(aTd.rearrange("c p s -> p c s")[:, :, t_ * T:(t_ + 1) * T], ot[:], otsem, reads=[otr],
                      writes=[P.R("aTd", t_)])
            else:
                norm_apply(hs_, hr_, NL * 3, lambda c, ot=ot: ot[:, c, :], otr, tmpn)
                P.dma(hview(outT, t_), ot[:], otsem, reads=[otr], writes=[P.R("outT", t_)])

        cur = load(0)
        norm_square(cur[0], cur[1], tmp3)
        norm_rstd(tmp3)
        norm_apply(cur[0], cur[1], l * 3 + 2, lambda c: eTs[0][:, c, :], reT[0], tmp3)
        pend = None
        nxt = None
        for t in range(NT):
            hs, hr, hsem, pb, pbr = cur
            eT, re_ = eTs[t % 2], reT[t % 2]
            if t + 1 < NT:
                nxt = load(t + 1)
            for oc in range(KC):
                ps, pr = next_ps()

                def fn(e, ps=ps, oc=oc, eT=eT):
                    ins = None
                    for kc in range(KC):
                        ins = e.matmul(ps[:], lhsT=wg[:, kc, oc * 128:(oc + 1) * 128], rhs=eT[:, kc, :],
                                       start=(kc == 0), stop=(kc == KC - 1))
                    return ins
                P.op("pe", fn, reads=[rw, re_], writes=[pr])
                ps2, pr2 = next_ps()

                def fn2(e, ps2=ps2, oc=oc, pb=pb):
                    ins = None
                    for kc in range(2):
                        ins = e.matmul(ps2[:], lhsT=wp[:, kc, oc * 128:(oc + 1) * 128], rhs=pb[:, kc, :],
                                       start=(kc == 0), stop=(kc == 1))
                    return ins
                P.op("pe", fn2, reads=[rw, pbr], writes=[pr2])
                sg, sgr, _ = sgs.next()
                tt, ttr, _ = tts.next()
                P.op("act", lambda e, sg=sg, ps=ps: e.activation(out=sg[:], in_=ps[:], func=AF.Sigmoid), reads=[pr], writes=[sgr])
                P.op("dve", lambda e, tt=tt, ps2=ps2, sg=sg: e.tensor_tensor(out=tt[:], in0=ps2[:], in1=sg[:], op=ALU.mult),
                     reads=[pr2, sgr], writes=[ttr])
                P.op("dve", lambda e, tt=tt, hs=hs, oc=oc: e.tensor_tensor(out=hs[:, oc, :], in0=tt[:], in1=hs[:, oc, :],
                                                                           op=ALU.add), reads=[ttr, hr], writes=[hr])
                if oc == 0 and pend is not None:
                    norm_rstd(tmpn)
                if oc == 1 and pend is not None:
                    finish_fused(pend)
                    pend = None
                if oc == 2 and t + 1 < NT:
                    norm_square(nxt[0], nxt[1], tmp3)
                if oc == 4 and t + 1 < NT:
                    norm_rstd(tmp3)
                if oc == 6 and t + 1 < NT:
                    norm_apply(nxt[0], nxt[1], l * 3 + 2, lambda c, t=t: eTs[(t + 1) % 2][:, c, :], reT[(t + 1) % 2], tmp3)
            if fuse != "final":
                P.dma(hview(hT, t), hs[:], hsem, reads=[hr], writes=[P.R("hT", t)])
            if fuse is not None:
                norm_square(hs, hr, tmpn)
                pend = (hs, hr, t)
            cur = nxt
        if pend is not None:
            norm_rstd(tmpn)
            finish_fused(pend)
        P.barrier()
        A.reset(m)

    setup()
    wrote_h = False
    fused_final = False
    for l in range(nl):
        m = A.mark()
        aT = A.alloc([128, KC, S], BF16, "aT")
        aR = [P.R("aT", l, t) for t in range(NT)]
        if l == 0:
            norm_phase(xT, 0, aT, aR)
            phase_qkv(l, aT, aR)
        else:
            phase_qkv(l, aT, aR, preload=lambda: load_act(aTd, aT, aR, "aTd"))
        phase_conv_gates(l, aT, aR)
        phase_sgu(l, aT, aR)
        A.reset(m)
        wts = alloc_merge_w(l)
        phase_attn(l)
        phase_merge(l, xT if l == 0 else hT, wts)
        wrote_h = True
        bg_flush()
        A.reset(m)
        wd = A.alloc([128, 32, D], BF16, "wd")
        rwd = P.R("wd", l)
        bg_queue(w_down[l], wd, rwd)
        m1 = A.mark()
        cT = A.alloc([128, KC, S], BF16, "cT")
        cR = [P.R("cT", l, t) for t in range(NT)]
        phase_ffn_up(l, cT, cR, preload=lambda: load_act(cTd, cT, cR, "cTd"))
        bg_flush()
        A.reset(m1)
        wg = A.alloc([128, 8, D], BF16, "wg")
        wp = A.alloc([128, 2, D], BF16, "wp")
        rwp = P.R("wple", l)
        bg_queue(w_pg[l], wg, rwp)
        bg_queue(w_pe[l], wp, rwp)
        phase_ffn_down(l, wd, rwd)
        last = (l == nl - 1)
        fuse = ("final" if (final_norm and nl == NL) else None) if last else "next"
        fused_final = fused_final or fuse == "final"
        phase_ple(l, wg, wp, rwp, m, fuse)
        bg_flush()
        A.reset(m)
    if not fused_final:
        final_phase(hT if wrote_h else xT, NL * 3)
    P.emit()
    return nc, P


def _consts():
    ident = np.eye(128, dtype=np.float32)
    rot = np.zeros((128, 128), np.float32)
    for base in (0, 64):
        for i in range(8):
            rot[base + i, base + 8 + i] = -1.0
            rot[base + 8 + i, base + i] = 1.0
    rotT = rot.T.copy()
    ones = np.ones((128, 128), np.float32)
    s = np.arange(128)[:, None]
    t = np.arange(128)[None, :]
    tril = (s <= t).astype(np.float32)
    cmat = np.stack([ident, rotT, ones, tril], axis=1).astype(np.float32)
    k = np.arange(128)[:, None]
    q = np.arange(128)[None, :]
    prev = np.where(k >= q, 0.0, MASKV).astype(np.float32)
    cur = np.where(k <= q, 0.0, MASKV).astype(np.float32)
    cmask = np.stack([np.tile(prev, (1, 4)), np.tile(cur, (1, 4))], axis=1).astype(np.float32)
    inv_freq = (np.float32(500000.0) ** (-(np.arange(0, 16, 2, dtype=np.float32) / np.float32(16)))).astype(np.float32)
    invf = np.zeros((128, 1), np.float32)
    for p in range(128):
        if p % 64 < 16:
            invf[p, 0] = inv_freq[(p % 64) % 8]
    return cmat, cmask, invf


_CACHE = {}


def _prep_shared(inp):
    f = lambda a: np.ascontiguousarray(np.asarray(a, dtype=np.float32))
    cmat, cmask, invf = _consts()
    gl = []
    for l in range(NL):
        for nm in ("norm_mix_g", "norm_mlp_g", "norm_ple_g"):
            gl.append(np.asarray(inp[nm][l], np.float32).reshape(KC, 128).T)
    gl.append(np.asarray(inp["norm_final_g"], np.float32).reshape(KC, 128).T)
    gains = np.ascontiguousarray(np.stack(gl, axis=1))
    cw = np.asarray(inp["conv_w"], np.float32)
    convw = np.ascontiguousarray(cw.reshape(NL, 3, 4, 128).transpose(3, 0, 2, 1))
    sgwT = np.ascontiguousarray(np.asarray(inp["sg_w"], np.float32).transpose(0, 1, 3, 2))
    shared = {
        "w_in": f(inp["w_in"]), "w_a": f(inp["w_branch_a"]), "w_b": f(inp["w_branch_b"]),
        "w_c": f(inp["w_branch_c"]), "w_out": f(inp["w_out"]), "w_up": f(inp["w_up"]),
        "w_down": f(inp["w_down"]), "w_pg": f(inp["w_ple_gate"]), "w_pe": f(inp["w_ple_proj"]),
        "sgwT": sgwT, "gains": gains, "convw": convw,
        "lng": f(inp["sg_ln_g"]).reshape(NL, 1, 512), "lnb": f(inp["sg_ln_b"]).reshape(NL, 1, 512),
        "sgb": f(inp["sg_b"]).reshape(NL, 1, 512),
        "cmat": cmat, "cmask": cmask, "invf": invf,
    }
    return shared


def run(inp, nl=NL, final_norm=True, cores=8, trace=False, stop=99):
    key = (nl, final_norm, stop)
    if key not in _CACHE:
        _CACHE[key] = build_program(nl, final_norm, stop)[0]
    nc = _CACHE[key]
    shared = _prep_shared(inp)
    x = np.asarray(inp["x"], np.float32)
    p = np.asarray(inp["p"], np.float32)
    posn = np.asarray(inp["positions"], np.int32)
    in_maps = []
    for b in range(cores):
        mp = dict(shared)
        mp["xT"] = np.ascontiguousarray(x[b].T)
        mp["pT"] = np.ascontiguousarray(p[:, b].transpose(0, 2, 1))
        mp["pos"] = np.ascontiguousarray(posn[b].reshape(1, S))
        in_maps.append(mp)
    res = run_bass_kernel_spmd(nc, in_maps, core_ids=list(range(cores)), **({"trace": True} if trace else {}))
    out = np.stack([np.ascontiguousarray(r["outT"].T) for r in res.results], axis=0)
    return out.astype(np.float32), res


def kernel(**inputs):
    out, _ = run(inputs)
    return out
```
